# Optimizing a Trainium2 kernel written in Bass

```python
import math
import jax, jax.numpy as jnp
from jax import lax
import numpy as np

D_MODEL = 1024
BATCH = 2
SEQ = 8192
DEPTH = 4

D_A = D_MODEL // 2
HEAD_A = 64
N_HEADS_A = D_A // HEAD_A
LORA_DECAY = 64
LORA_ICLR = 64
LORA_GATE = 128
N_DIR = 2
C_RWKV = 3 * D_A + N_DIR * LORA_DECAY + N_DIR * LORA_ICLR + LORA_GATE
RWKV_SPLITS = [D_A, 2 * D_A, 3 * D_A, 3 * D_A + N_DIR * LORA_DECAY,
               3 * D_A + N_DIR * LORA_DECAY + N_DIR * LORA_ICLR]
DECAY_SCALE = math.exp(-0.5)
GN_EPS = 64e-5
HEAD_B = 64
D_B = D_MODEL // 2
N_HEADS_B = D_B // (2 * HEAD_B)
C_ATTN = 3 * D_B
ALIBI_MAX_EXP = 8.0
Q_BLOCK = 128
N_BRANCH = 2
C_IN = C_RWKV + C_ATTN + N_BRANCH * D_MODEL
D_FF = ((8 * D_MODEL // 3 + 127) // 128) * 128
NORM_EPS = 1e-6

kernel_name = 'hybrid_rwkv7_diffattn_macaron_encoder'


def rms_norm(x, g, eps=NORM_EPS):
    xf = x.astype(jnp.float32)
    y = xf * lax.rsqrt(jnp.mean(xf * xf, axis=-1, keepdims=True) + eps)
    return (y * g.astype(jnp.float32)).astype(x.dtype)


def swiglu(h, w_in, w_out):
    gate, up = jnp.split(h @ w_in, 2, axis=-1)
    return (jax.nn.silu(gate) * up) @ w_out


def centred_shift(p):
    prev = jnp.pad(p[:, :-1], ((0, 0), (1, 0), (0, 0)))
    nxt = jnp.pad(p[:, 1:], ((0, 0), (0, 1), (0, 0)))
    return 0.5 * (prev + nxt)


def to_heads(t, n_heads):
    return t.reshape(t.shape[:-1] + (n_heads, t.shape[-1] // n_heads))


def rwkv7_scan(r, w, k, v, kk, a, reverse):
    b, s, h, n = r.shape
    xs = tuple(jnp.moveaxis(t.astype(jnp.float32), 1, 0) for t in (r, w, k, v, kk, a))

    def step(state, inp):
        r_t, w_t, k_t, v_t, kk_t, a_t = inp
        sa = jnp.einsum('bhvk,bhk->bhv', state, -kk_t)
        state = (state * w_t[:, :, None, :]
                 + sa[..., None] * (kk_t * a_t)[:, :, None, :]
                 + v_t[..., None] * k_t[:, :, None, :])
        y_t = jnp.einsum('bhvk,bhk->bhv', state, r_t)
        return state, y_t

    s0 = jnp.zeros((b, h, n, n), jnp.float32)
    _, ys = lax.scan(step, s0, xs, reverse=reverse)
    return jnp.moveaxis(ys, 0, 1)


def rwkv7_mixer(p, mu, w0, w2, a0, a2, g2, k_k, k_a, r_k, ln_g, ln_b):
    p = p + mu * (centred_shift(p) - p)
    r, k, v, dw, da, dg = jnp.split(p, RWKV_SPLITS, axis=-1)
    b, s, _ = r.shape
    dw = dw.reshape(b, s, N_DIR, LORA_DECAY)
    da = da.reshape(b, s, N_DIR, LORA_ICLR)
    decay = jnp.exp(-DECAY_SCALE * jax.nn.sigmoid(
        w0 + jnp.einsum('bsgr,grc->bsgc', jnp.tanh(dw), w2)))
    iclr = jax.nn.sigmoid(a0 + jnp.einsum('bsgr,grc->bsgc', da, a2))
    gate = jax.nn.sigmoid(dg) @ g2
    kk = to_heads(k * k_k, N_HEADS_A).astype(jnp.float32)
    kk = kk / jnp.maximum(jnp.sqrt(jnp.sum(kk * kk, axis=-1, keepdims=True)), 1e-12)
    k_dir = k[:, :, None, :] * (1.0 + (iclr - 1.0) * k_a)
    r_h = to_heads(r, N_HEADS_A)
    v_h = to_heads(v, N_HEADS_A)
    k_h = to_heads(k_dir, N_HEADS_A)
    w_h = to_heads(decay, N_HEADS_A)
    a_h = to_heads(iclr, N_HEADS_A)
    y = (rwkv7_scan(r_h, w_h[:, :, 0], k_h[:, :, 0], v_h, kk, a_h[:, :, 0], False)
         + rwkv7_scan(r_h, w_h[:, :, 1], k_h[:, :, 1], v_h, kk, a_h[:, :, 1], True))
    mean = jnp.mean(y, axis=-1, keepdims=True)
    var = jnp.mean(jnp.square(y - mean), axis=-1, keepdims=True)
    y = ((y - mean) * lax.rsqrt(var + GN_EPS)).reshape(b, s, D_A) * ln_g + ln_b
    bonus = jnp.einsum('bshn,bsghn,hn->bsh', r_h.astype(jnp.float32),
                       k_h.astype(jnp.float32), r_k.astype(jnp.float32))[..., None] * v_h
    return (y + bonus.reshape(b, s, D_A)) * gate


def diff_attention(p, q_gain, k_gain, lam_vecs, sub_g, lam_init):
    b, s, _ = p.shape
    q, k, v = jnp.split(p, 3, axis=-1)
    q = rms_norm(q.reshape(b, s, N_HEADS_B, 2, HEAD_B), q_gain)
    k = rms_norm(k.reshape(b, s, N_HEADS_B, 2, HEAD_B), k_gain)
    v = v.reshape(b, s, N_HEADS_B, 2 * HEAD_B)
    lv = lam_vecs.astype(jnp.float32)
    lam = jnp.exp(jnp.sum(lv[0] * lv[1])) - jnp.exp(jnp.sum(lv[2] * lv[3])) + lam_init
    n_blk = s // Q_BLOCK
    q_blk = q.reshape(b, n_blk, Q_BLOCK, N_HEADS_B, 2, HEAD_B).transpose(1, 0, 3, 4, 2, 5)
    k_t = k.transpose(0, 2, 3, 1, 4)
    v_t = v.transpose(0, 2, 1, 3)
    slopes = 2.0 ** (-ALIBI_MAX_EXP * jnp.arange(1, N_HEADS_B + 1, dtype=jnp.float32) / N_HEADS_B)
    k_pos = jnp.arange(s, dtype=jnp.float32)
    q_pos = k_pos.reshape(n_blk, Q_BLOCK)
    scale = HEAD_B ** -0.5

    def attend_block(args):
        qb, qp = args
        logits = jnp.einsum('bhmqd,bhmkd->bhmqk', qb, k_t,
                            preferred_element_type=jnp.float32) * scale
        bias = -slopes[:, None, None] * jnp.abs(qp[:, None] - k_pos[None, :])
        prob = jax.nn.softmax(logits + bias[None, :, None], axis=-1)
        attn = prob[:, :, 0] - lam * prob[:, :, 1]
        return jnp.einsum('bhqk,bhkc->bhqc', attn.astype(v_t.dtype), v_t)

    o = lax.map(attend_block, (q_blk, q_pos))
    o = o.transpose(1, 0, 3, 2, 4).reshape(b, s, N_HEADS_B, 2 * HEAD_B)
    o = rms_norm(o, sub_g) * (1.0 - lam_init)
    return o.reshape(b, s, D_B)


def setup_inputs(seed: int = 0) -> dict:
    key = jax.random.key(seed)
    ks = iter(jax.random.split(key, 32))
    L = DEPTH

    def nrm(shape, scale):
        return jax.random.normal(next(ks), shape, jnp.float32) * scale

    def gain(shape):
        return 1.0 + nrm(shape, 0.02)

    return {
        'x': nrm((BATCH, SEQ, D_MODEL), 1.0),
        'norm_ffn1': gain((L, D_MODEL)),
        'ffn1_in': nrm((L, D_MODEL, 2 * D_FF), D_MODEL ** -0.5),
        'ffn1_out': nrm((L, D_FF, D_MODEL), D_FF ** -0.5),
        'norm_mix': gain((L, D_MODEL)),
        'w_in': nrm((L, D_MODEL, C_IN), D_MODEL ** -0.5),
        'rwkv_mu': jax.random.uniform(next(ks), (L, C_RWKV), jnp.float32),
        'decay_w0': nrm((L, N_DIR, D_A), 1.0),
        'decay_w2': nrm((L, N_DIR, LORA_DECAY, D_A), 0.5 * LORA_DECAY ** -0.5),
        'iclr_a0': nrm((L, N_DIR, D_A), 0.5),
        'iclr_a2': nrm((L, N_DIR, LORA_ICLR, D_A), 0.5 * LORA_ICLR ** -0.5),
        'gate_g2': nrm((L, LORA_GATE, D_A), LORA_GATE ** -0.5),
        'k_k': 0.85 + nrm((L, D_A), 0.05),
        'k_a': 1.0 + nrm((L, D_A), 0.05),
        'r_k': nrm((L, N_HEADS_A, HEAD_A), 0.1),
        'ln_x_g': gain((L, D_A)),
        'ln_x_b': nrm((L, D_A), 0.02),
        'q_gain': gain((L, HEAD_B)),
        'k_gain': gain((L, HEAD_B)),
        'diff_lambda': nrm((L, 4, HEAD_B), 0.1),
        'subln_g': gain((L, 2 * HEAD_B)),
        'w_branch': nrm((L, N_BRANCH, D_A, D_MODEL), D_A ** -0.5),
        'w_out': nrm((L, D_MODEL, D_MODEL), D_MODEL ** -0.5),
        'norm_ffn2': gain((L, D_MODEL)),
        'ffn2_in': nrm((L, D_MODEL, 2 * D_FF), D_MODEL ** -0.5),
        'ffn2_out': nrm((L, D_FF, D_MODEL), D_FF ** -0.5),
    }


def reference(x, norm_ffn1, ffn1_in, ffn1_out, norm_mix, w_in, rwkv_mu, decay_w0, decay_w2,
              iclr_a0, iclr_a2, gate_g2, k_k, k_a, r_k, ln_x_g, ln_x_b, q_gain, k_gain,
              diff_lambda, subln_g, w_branch, w_out, norm_ffn2, ffn2_in, ffn2_out):
    b, s, d = x.shape
    for l in range(DEPTH):
        lam_init = 0.8 - 0.6 * math.exp(-0.3 * l)
        x = x + 0.5 * swiglu(rms_norm(x, norm_ffn1[l]), ffn1_in[l], ffn1_out[l])
        h = rms_norm(x, norm_mix[l])
        proj = h @ w_in[l]
        p_rwkv, p_attn, p_gate = jnp.split(proj, [C_RWKV, C_RWKV + C_ATTN], axis=-1)
        o_a = rwkv7_mixer(p_rwkv, rwkv_mu[l], decay_w0[l], decay_w2[l], iclr_a0[l],
                          iclr_a2[l], gate_g2[l], k_k[l], k_a[l], r_k[l],
                          ln_x_g[l], ln_x_b[l]).astype(x.dtype)
        o_b = diff_attention(p_attn, q_gain[l], k_gain[l], diff_lambda[l], subln_g[l],
                             lam_init).astype(x.dtype)
        branches = jnp.stack([o_a, o_b], axis=2)
        y = jnp.einsum('bsgc,gcd->bsgd', branches, w_branch[l])
        gates = jax.nn.sigmoid(p_gate.reshape(b, s, N_BRANCH, d))
        merged = jnp.sum(gates * y, axis=2)
        x = x + merged @ w_out[l]
        x = x + 0.5 * swiglu(rms_norm(x, norm_ffn2[l]), ffn2_in[l], ffn2_out[l])
    return x
```

```python
import math
from contextlib import ExitStack
import numpy as np
import ml_dtypes
import concourse.bass as bass
import concourse.mybir as mybir
from concourse.bass_utils import run_bass_kernel_spmd

F32 = mybir.dt.float32
BF16 = mybir.dt.bfloat16
AF = mybir.ActivationFunctionType
ALU = mybir.AluOpType
AX = mybir.AxisListType

D = 1024
DFF = 2816
NJ = DFF // 128
DEPTH = 4
SEQ = 8192
BATCH = 2
C_RWKV = 1920
C_ATTN = 1536
C_MIX = C_RWKV + C_ATTN
C_IN = 5504
NCORES = 8
NORM_EPS = 1e-6
GN_EPS = 64e-5
DECAY_SCALE = math.exp(-0.5)


class Buf:
    __slots__ = ("w", "r", "name")

    def __init__(self, name=""):
        self.w = None
        self.r = []
        self.name = name


ENGS = ("pe", "act", "dve", "pool", "sp")
N_DMA_SEMS = 20


class Prog:
    def __init__(self, nc, es):
        self.nc = nc
        self.es = es
        self.streams = {e: [] for e in ENGS}
        self.count = {e: 0 for e in ENGS}
        self.known = {e: {} for e in ENGS}
        self.sems = {}
        for e in ENGS:
            self.sems[e] = es.enter_context(nc.semaphore("s_" + e))
        self.dma_sems = []
        self.dma_tot = []
        for i in range(N_DMA_SEMS):
            nm = "d%d" % i
            self.sems[nm] = es.enter_context(nc.semaphore("s_" + nm))
            self.dma_sems.append(nm)
            self.dma_tot.append(0)
        self.dma_rr = 0
        self.out_events = []
        self.banks = []
        for i in range(8):
            t = es.enter_context(nc.psum_tensor("bank%d" % i, [128, 512], F32))
            self.banks.append((t, Buf("bank%d" % i)))

    def sb(self, name, shape, dtype, es=None):
        t = (es or self.es).enter_context(self.nc.sbuf_tensor(name, list(shape), dtype))
        return t

    def ps(self, name, shape, dtype=F32, es=None):
        t = (es or self.es).enter_context(self.nc.psum_tensor(name, list(shape), dtype))
        return t

    def _deps(self, eng, reads, writes):
        deps = {}

        def add(ev):
            if ev is None:
                return
            s, v = ev
            if eng == "pe" and s == "pe":
                return
            if deps.get(s, 0) < v:
                deps[s] = v

        for b in reads:
            add(b.w)
        for b in writes:
            add(b.w)
            for ev in b.r:
                add(ev)
        waits = []
        kn = self.known[eng]
        for s, v in deps.items():
            if kn.get(s, 0) < v:
                kn[s] = v
                waits.append((s, v))
        return waits

    def _mark(self, ev, reads, writes):
        for b in writes:
            b.w = ev
            b.r = []
        for b in reads:
            if len(b.r) > 12:
                best = {}
                for s, v in b.r:
                    if best.get(s, 0) < v:
                        best[s] = v
                b.r = list(best.items())
            b.r.append(ev)

    def op(self, eng, fn, reads=(), writes=()):
        waits = self._deps(eng, reads, writes)
        self.count[eng] += 1
        ev = (eng, self.count[eng])
        self.streams[eng].append((waits, fn, (eng, 1)))
        self._mark(ev, reads, writes)
        return ev

    def dma(self, q, out_ap, in_ap, reads=(), writes=(), is_output=False, **kw):
        i = self.dma_rr
        self.dma_rr = (self.dma_rr + 1) % N_DMA_SEMS
        nm = self.dma_sems[i]
        waits = self._deps(q, reads, writes)
        prev = self.dma_tot[i]
        if prev > 0 and self.known[q].get(nm, 0) < prev:
            self.known[q][nm] = prev
            waits.append((nm, prev))
        self.dma_tot[i] = prev + 16
        ev = (nm, prev + 16)

        def fn(e, out_ap=out_ap, in_ap=in_ap, kw=kw):
            return e.dma_start(out=out_ap, in_=in_ap, **kw)

        self.streams[q].append((waits, fn, (nm, 16)))
        self._mark(ev, reads, writes)
        if is_output:
            self.out_events.append(ev)
        return ev

    def barrier(self):
        tot = {e: self.count[e] for e in ENGS}
        for i, nm in enumerate(self.dma_sems):
            tot[nm] = self.dma_tot[i]
        for e in ENGS:
            waits = []
            for sname, v in tot.items():
                if sname == e and e == "pe":
                    continue
                if v > 0 and self.known[e].get(sname, 0) < v:
                    self.known[e][sname] = v
                    waits.append((sname, v))
            if waits:
                self.streams[e].append((waits, None, None))

    def finish(self):
        best = {}
        for s, v in self.out_events:
            if best.get(s, 0) < v:
                best[s] = v
        waits = list(best.items())
        self.streams["sp"].append((waits, None, None))

    def emit(self):
        nc = self.nc
        sems = self.sems
        streams = self.streams

        def run(eng_handle, lst):
            for waits, fn, inc in lst:
                for s, v in waits:
                    eng_handle.wait_ge(sems[s], v)
                if fn is not None:
                    ins = fn(eng_handle)
                    ins.then_inc(sems[inc[0]], inc[1])

        with nc.Block() as block:
            @block.tensor
            def _(e):
                run(e, streams["pe"])

            @block.scalar
            def _(e):
                run(e, streams["act"])

            @block.vector
            def _(e):
                run(e, streams["dve"])

            @block.gpsimd
            def _(e):
                run(e, streams["pool"])

            @block.sync
            def _(e):
                run(e, streams["sp"])


def mm(P, out, lhsT, rhs, start, stop, reads, writes):
    return P.op("pe", lambda e: e.matmul(out, lhsT, rhs, start=start, stop=stop), reads, writes)


def transp(P, out, in_, ident, reads, writes):
    return P.op("pe", lambda e: e.transpose(out, in_, ident), reads, writes)


class BankView:
    def __init__(self, t):
        self.t = t

    def __getitem__(self, key):
        v = self.t[:].bitcast(BF16).rearrange("p (c t) -> p c t", t=128)
        return v[key]


class Rot:
    def __init__(self, items):
        self.items = items
        self.i = 0

    def next(self):
        it = self.items[self.i]
        self.i = (self.i + 1) % len(self.items)
        return it


def make_rot(P, kind, name, shape, dtype, n, es=None):
    items = []
    for i in range(n):
        if kind == "sb":
            t = P.sb("%s%d" % (name, i), shape, dtype, es)
        else:
            t = P.ps("%s%d" % (name, i), shape, dtype, es)
        items.append((t, Buf("%s%d" % (name, i))))
    return Rot(items)


class TokCtx:
    pass


def setup_tok(P, nc, NT, es):
    T = TokCtx()
    T.NT = NT
    T.NTT = NT // 128
    T.x = P.sb("x_res", [128, T.NTT, D], F32, es)
    T.xb = [Buf("x%d" % i) for i in range(T.NTT)]
    T.ident_bf = P.sb("sb_ident_bf", [128, 128], BF16, es)
    T.ident_b = Buf("ident_bf")
    T.ident_f = P.sb("sb_ident_f", [128, 128], F32, es)
    T.identf_b = Buf("ident_f")
    T.hT = P.sb("hT", [128, 8, 512], BF16, es)
    T.hT_b = Buf("hT")
    T.gcol = P.sb("gcol", [128, 8], F32, es)
    T.gcol_b = Buf("gcol")
    T.xn = make_rot(P, "sb", "xn", [128, D], BF16, 2, es)
    T.sq = P.sb("sq_junk", [128, D], BF16, es)
    T.sq_b = Buf("sq")
    T.stat = make_rot(P, "sb", "stat", [128, 4], F32, 4, es)
    T.w1 = make_rot(P, "sb", "w1s", [128, 8, 256], BF16, 3, es)
    T.ps_tp = Rot([(BankView(P.banks[0][0]), P.banks[0][1])])
    T.ps_a = Rot(P.banks[1:5])
    T.ps_b = Rot(P.banks[5:8])
    return T


def scope_ffn(P, T, es, tag):
    T.u = P.sb("u_hid" + tag, [128, NJ, 512], BF16, es)
    T.u_b = [Buf("u%d" % j) for j in range(NJ)]
    T.w2 = P.sb("w2_res" + tag, [128, NJ, D], BF16, es)
    T.w2_b = [Buf("w2_%d" % j) for j in range(NJ)]
    T.sg = make_rot(P, "sb", "sg" + tag, [128, 512], F32, 2, es)


def scope_proj(P, T, es, tag):
    T.stage = make_rot(P, "sb", "stage" + tag, [128, 512], F32, 3, es)


def load_consts(P, T, ident_bf_d, ident_f_d):
    P.dma("sp", T.ident_bf[:], ident_bf_d, (), (T.ident_b,))
    P.dma("sp", T.ident_f[:], ident_f_d, (), (T.identf_b,))


def load_gain(P, T, g_row):
    P.dma("sp", T.gcol[:], g_row.rearrange("(c p) -> p c", p=128), (), (T.gcol_b,),
          allow_slow_non_contiguous=True)


def prenorm_group(P, T, grp):
    for ti in range(4):
        tt = grp * 4 + ti
        xb = T.xb[tt]
        xt = T.x[:, tt, :]
        st, stb = T.stat.next()
        P.op("act", lambda e, xt=xt, st=st: e.activation(T.sq[:], xt, AF.Square, accum_out=st[:, 0:1]),
             (xb,), (T.sq_b, stb))
        P.op("dve", lambda e, st=st: e.tensor_scalar(st[:, 1:2], st[:, 0:1], 1.0 / D, NORM_EPS, ALU.mult, ALU.add),
             (stb,), (stb,))
        P.op("act", lambda e, st=st: e.activation(st[:, 2:3], st[:, 1:2], AF.Sqrt), (stb,), (stb,))
        P.op("dve", lambda e, st=st: e.reciprocal(st[:, 3:4], st[:, 2:3]), (stb,), (stb,))
        xn, xnb = T.xn.next()
        P.op("act", lambda e, xn=xn, xt=xt, st=st: e.activation(xn[:], xt, AF.Copy, scale=st[:, 3:4]),
             (xb, stb), (xnb,))
        tp, tpb = T.ps_tp.next()
        for c in range(8):
            transp(P, tp[:, c, :], xn[:, c * 128:(c + 1) * 128], T.ident_bf[:], (xnb, T.ident_b), (tpb,))
        gb = T.gcol[:].unsqueeze(2).to_broadcast([128, 8, 128])
        P.op("dve", lambda e, tp=tp, ti=ti, gb=gb: e.tensor_tensor(
            T.hT[:, :, ti * 128:(ti + 1) * 128], tp[:], gb, ALU.mult),
            (tpb, T.gcol_b), (T.hT_b,))


def load_w2(P, T, w2_d):
    for j in range(NJ):
        P.dma("pool", T.w2[:, j, :], w2_d[j * 128:(j + 1) * 128, :], (), (T.w2_b[j],))


def ffn_group(P, T, grp, w1_d):
    for j in range(NJ):
        w1, w1b = T.w1.next()
        P.dma("pool", w1[:, :, 0:128],
              w1_d[:, j * 128:(j + 1) * 128].rearrange("(c p) n -> p c n", p=128), (), (w1b,))
        P.dma("pool", w1[:, :, 128:256],
              w1_d[:, DFF + j * 128:DFF + (j + 1) * 128].rearrange("(c p) n -> p c n", p=128), (), (w1b,))
        pg, pgb = T.ps_a.next()
        pu, pub = T.ps_a.next()
        for c in range(8):
            mm(P, pg[:], w1[:, c, 0:128], T.hT[:, c, :], c == 0, c == 7, (w1b, T.hT_b), (pgb,))
        for c in range(8):
            mm(P, pu[:], w1[:, c, 128:256], T.hT[:, c, :], c == 0, c == 7, (w1b, T.hT_b), (pub,))
        sg, sgb = T.sg.next()
        P.op("act", lambda e, sg=sg, pg=pg: e.activation(sg[:], pg[:], AF.Silu), (pgb,), (sgb,))
        P.op("dve", lambda e, sg=sg, pu=pu, j=j: e.tensor_tensor(T.u[:, j, :], sg[:], pu[:], ALU.mult),
             (sgb, pub), (T.u_b[j],))
    for ti in range(4):
        tt = grp * 4 + ti
        for dh in range(2):
            po, pob = T.ps_b.next()
            for j in range(NJ):
                mm(P, po[:], T.u[:, j, ti * 128:(ti + 1) * 128], T.w2[:, j, dh * 512:(dh + 1) * 512],
                   j == 0, j == NJ - 1, (T.u_b[j], T.w2_b[j]), (pob,))
            xs = T.x[:, tt, dh * 512:(dh + 1) * 512]
            P.op("dve", lambda e, xs=xs, po=po: e.scalar_tensor_tensor(xs, po[:], 0.5, xs, ALU.mult, ALU.add),
                 (pob, T.xb[tt]), (T.xb[tt],))


def proj_group(P, T, grp, w_d, ncols, out_d, tok0, sigmoid_from=None):
    nch = ncols // 128
    for j in range(nch):
        w1, w1b = T.w1.next()
        P.dma("pool", w1[:, :, 0:128],
              w_d[:, j * 128:(j + 1) * 128].rearrange("(c p) n -> p c n", p=128), (), (w1b,))
        pg, pgb = T.ps_a.next()
        for c in range(8):
            mm(P, pg[:], w1[:, c, 0:128], T.hT[:, c, :], c == 0, c == 7, (w1b, T.hT_b), (pgb,))
        sg, sgb = T.stage.next()
        if sigmoid_from is not None and j * 128 >= sigmoid_from:
            P.op("act", lambda e, sg=sg, pg=pg: e.activation(sg[:], pg[:], AF.Sigmoid), (pgb,), (sgb,))
        else:
            P.op("act", lambda e, sg=sg, pg=pg: e.activation(sg[:], pg[:], AF.Copy), (pgb,), (sgb,))
        P.dma("sp", out_d[j * 128:(j + 1) * 128, tok0:tok0 + 512], sg[:], (sgb,), (), is_output=True)


def phase_ffn(P, T, g_d, w1_d, w2_d, tag):
    with ExitStack() as es2:
        scope_ffn(P, T, es2, tag)
        load_gain(P, T, g_d)
        load_w2(P, T, w2_d)
        for grp in range(T.NT // 512):
            prenorm_group(P, T, grp)
            ffn_group(P, T, grp, w1_d)
        P.barrier()


def phase_proj(P, T, gm_d, win_d, pt_d, tag):
    with ExitStack() as es2:
        scope_proj(P, T, es2, tag)
        load_gain(P, T, gm_d)
        for grp in range(T.NT // 512):
            prenorm_group(P, T, grp)
            proj_group(P, T, grp, win_d, C_IN, pt_d, grp * 512, sigmoid_from=C_MIX)
        P.barrier()


def phase_merge(P, T, sg_d, oa_d, ob_d, wb_d, wout_d, tag):
    with ExitStack() as es2:
        wb = P.sb("wb" + tag, [128, 2, 4, D], BF16, es2)
        wb_b = Buf("wb")
        wo = P.sb("wo" + tag, [128, 8, D], BF16, es2)
        wo_b = Buf("wo")
        oT = make_rot(P, "sb", "oT" + tag, [128, 2, 4, 512], BF16, 2, es2)
        sgl = make_rot(P, "sb", "sgl" + tag, [128, 2, 512], F32, 3, es2)
        t1 = make_rot(P, "sb", "mt1" + tag, [128, 512], F32, 2, es2)
        for g in range(2):
            for cc in range(4):
                P.dma("pool", wb[:, g, cc, :], wb_d[g, cc * 128:(cc + 1) * 128, :], (), (wb_b,))
        for dc in range(8):
            P.dma("pool", wo[:, dc, :], wout_d[dc * 128:(dc + 1) * 128, :], (), (wo_b,))
        for grp in range(T.NT // 512):
            tok0 = grp * 512
            o, ob_ = oT.next()
            for g, src in ((0, oa_d), (1, ob_d)):
                P.dma("pool", o[:, g, :, :],
                      src[:, tok0:tok0 + 512].rearrange("(c p) t -> p c t", p=128), (), (ob_,))
            for dc in range(8):
                sl, slb = sgl.next()
                for g in range(2):
                    P.dma("sp", sl[:, g, :], sg_d[g * D + dc * 128:g * D + (dc + 1) * 128, tok0:tok0 + 512],
                          (), (slb,))
                pa, pab = T.ps_a.next()
                pb, pbb = T.ps_a.next()
                for cc in range(4):
                    mm(P, pa[:], wb[:, 0, cc, dc * 128:(dc + 1) * 128], o[:, 0, cc, :], cc == 0, cc == 3,
                       (wb_b, ob_), (pab,))
                for cc in range(4):
                    mm(P, pb[:], wb[:, 1, cc, dc * 128:(dc + 1) * 128], o[:, 1, cc, :], cc == 0, cc == 3,
                       (wb_b, ob_), (pbb,))
                ta, tab = t1.next()
                P.op("dve", lambda e, ta=ta, pa=pa, sl=sl: e.tensor_tensor(ta[:], pa[:], sl[:, 0, :], ALU.mult),
                     (pab, slb), (tab,))
                tb, tbb = t1.next()
                P.op("dve", lambda e, tb=tb, pb=pb, sl=sl: e.tensor_tensor(tb[:], pb[:], sl[:, 1, :], ALU.mult),
                     (pbb, slb), (tbb,))
                P.op("dve", lambda e, ta=ta, tb=tb, dc=dc: e.tensor_tensor(T.hT[:, dc, :], ta[:], tb[:], ALU.add),
                     (tab, tbb), (T.hT_b,))
            for ti in range(4):
                tt = grp * 4 + ti
                for dh in range(2):
                    po, pob = T.ps_b.next()
                    for dc in range(8):
                        mm(P, po[:], T.hT[:, dc, ti * 128:(ti + 1) * 128], wo[:, dc, dh * 512:(dh + 1) * 512],
                           dc == 0, dc == 7, (T.hT_b, wo_b), (pob,))
                    xs = T.x[:, tt, dh * 512:(dh + 1) * 512]
                    P.op("dve", lambda e, xs=xs, po=po: e.tensor_tensor(xs, xs, po[:], ALU.add),
                         (pob, T.xb[tt]), (T.xb[tt],))
        P.barrier()


def tok_io(nc, NT, names):
    d = {}
    shapes = {
        "g1": [D], "w1": [D, 2 * DFF], "w2": [DFF, D], "gm": [D], "win": [D, C_IN],
        "g2n": [D], "w1b": [D, 2 * DFF], "w2b": [DFF, D],
        "wb": [2, 512, D], "wout": [D, D],
    }
    for n in names:
        d[n] = nc.dram_tensor(n, shapes[n], F32, kind="ExternalInput").ap()
    return d


def build_k1(NT):
    nc = bass.Bass("TRN2", target_bir_lowering=False)
    x_d = nc.dram_tensor("x", [NT, D], F32, kind="ExternalInput").ap()
    w = tok_io(nc, NT, ["g1", "w1", "w2", "gm", "win"])
    idb_d = nc.dram_tensor("ident_bf", [128, 128], BF16, kind="ExternalInput").ap()
    idf_d = nc.dram_tensor("ident_f", [128, 128], F32, kind="ExternalInput").ap()
    x1_d = nc.dram_tensor("x1", [NT, D], F32, kind="ExternalOutput").ap()
    pt_d = nc.dram_tensor("pt", [C_IN, NT], F32, kind="ExternalOutput").ap()
    with ExitStack() as es:
        P = Prog(nc, es)
        T = setup_tok(P, nc, NT, es)
        load_consts(P, T, idb_d, idf_d)
        for tt in range(T.NTT):
            P.dma("sp", T.x[:, tt, :], x_d[tt * 128:(tt + 1) * 128, :], (), (T.xb[tt],))
        phase_ffn(P, T, w["g1"], w["w1"], w["w2"], "a")
        for tt in range(T.NTT):
            P.dma("sp", x1_d[tt * 128:(tt + 1) * 128, :], T.x[:, tt, :], (T.xb[tt],), (), is_output=True)
        phase_proj(P, T, w["gm"], w["win"], pt_d, "a")
        P.finish()
        P.emit()
    return nc


def build_k3(NT, with_next):
    nc = bass.Bass("TRN2", target_bir_lowering=False)
    x_d = nc.dram_tensor("x", [NT, D], F32, kind="ExternalInput").ap()
    sg_d = nc.dram_tensor("sg", [2 * D, NT], F32, kind="ExternalInput").ap()
    oa_d = nc.dram_tensor("oa", [512, NT], F32, kind="ExternalInput").ap()
    ob_d = nc.dram_tensor("ob", [512, NT], F32, kind="ExternalInput").ap()
    names = ["wb", "wout", "g2n", "w1b", "w2b"]
    if with_next:
        names += ["g1", "w1", "w2", "gm", "win"]
    w = tok_io(nc, NT, names)
    idb_d = nc.dram_tensor("ident_bf", [128, 128], BF16, kind="ExternalInput").ap()
    idf_d = nc.dram_tensor("ident_f", [128, 128], F32, kind="ExternalInput").ap()
    x1_d = nc.dram_tensor("x1", [NT, D], F32, kind="ExternalOutput").ap()
    if with_next:
        pt_d = nc.dram_tensor("pt", [C_IN, NT], F32, kind="ExternalOutput").ap()
    with ExitStack() as es:
        P = Prog(nc, es)
        T = setup_tok(P, nc, NT, es)
        load_consts(P, T, idb_d, idf_d)
        for tt in range(T.NTT):
            P.dma("sp", T.x[:, tt, :], x_d[tt * 128:(tt + 1) * 128, :], (), (T.xb[tt],))
        phase_merge(P, T, sg_d, oa_d, ob_d, w["wb"], w["wout"], "m")
        phase_ffn(P, T, w["g2n"], w["w1b"], w["w2b"], "b")
        if with_next:
            phase_ffn(P, T, w["g1"], w["w1"], w["w2"], "a")
        for tt in range(T.NTT):
            P.dma("sp", x1_d[tt * 128:(tt + 1) * 128, :], T.x[:, tt, :], (T.xb[tt],), (), is_output=True)
        if with_next:
            phase_proj(P, T, w["gm"], w["win"], pt_d, "a")
        P.finish()
        P.emit()
    return nc


ATT_SHIFT = 4.0


def attn_consts(head, S):
    slope = 2.0 ** (-8.0 * (head + 1) / 4.0)
    ii = np.arange(S) % 512
    qaug = np.stack([(ii // 32) * 32, ii % 32]).astype(np.float32)
    ka = np.full((2, S), -slope, np.float32)
    kb = np.full((2, S), slope, np.float32)
    jj = np.arange(128, dtype=np.float32)[:, None]
    d = np.arange(64, dtype=np.float32)[None, :]
    biasA = -slope * 128.0 * d + slope * jj - ATT_SHIFT
    biasB = -slope * 128.0 * d - slope * jj - ATT_SHIFT
    iq = np.arange(512, dtype=np.float32)[None, None, :]
    off = np.arange(4, dtype=np.float32)[None, :, None]
    biasD = -slope * np.abs(iq - (128.0 * off + jj[:, :, None])) - ATT_SHIFT
    bf = ml_dtypes.bfloat16
    return {
        "qaug": qaug.astype(bf), "kaug_a": ka.astype(bf), "kaug_b": kb.astype(bf),
        "biasA": biasA.astype(np.float32), "biasB": biasB.astype(np.float32),
        "biasD": biasD.astype(np.float32),
        "ones64": np.full((64, 64), 1.0 / 64, np.float32),
    }


def attn_dram_inputs(nc, S):
    d = {}
    d["qT"] = nc.dram_tensor("qT", [2, 64, S], F32, kind="ExternalInput").ap()
    d["kT"] = nc.dram_tensor("kT", [2, 64, S], F32, kind="ExternalInput").ap()
    d["vT"] = nc.dram_tensor("vT", [128, S], F32, kind="ExternalInput").ap()
    d["qaug"] = nc.dram_tensor("qaug", [2, S], BF16, kind="ExternalInput").ap()
    d["kaug_a"] = nc.dram_tensor("kaug_a", [2, S], BF16, kind="ExternalInput").ap()
    d["kaug_b"] = nc.dram_tensor("kaug_b", [2, S], BF16, kind="ExternalInput").ap()
    d["biasA"] = nc.dram_tensor("biasA", [128, 64], F32, kind="ExternalInput").ap()
    d["biasB"] = nc.dram_tensor("biasB", [128, 64], F32, kind="ExternalInput").ap()
    d["biasD"] = nc.dram_tensor("biasD", [128, 4, 512], F32, kind="ExternalInput").ap()
    d["ones64"] = nc.dram_tensor("ones64", [64, 64], F32, kind="ExternalInput").ap()
    d["q_gain"] = nc.dram_tensor("q_gain", [64], F32, kind="ExternalInput").ap()
    d["k_gain"] = nc.dram_tensor("k_gain", [64], F32, kind="ExternalInput").ap()
    d["lamv"] = nc.dram_tensor("lamv", [4, 64], F32, kind="ExternalInput").ap()
    d["subg"] = nc.dram_tensor("subg", [128], F32, kind="ExternalInput").ap()
    d["laminit"] = nc.dram_tensor("laminit", [128, 2], F32, kind="ExternalInput").ap()
    return d


def phase_attn(P, ident_f, identf_b, A, ob_out_d, S, tag):
    NJT = S // 128
    NI = S // 512
    banks = P.banks
    sc_rot = Rot(banks[0:3])
    acc = banks[3:7]
    misc = banks[7]
    with ExitStack() as es2:
        Qa = P.sb("Qa" + tag, [66, S], BF16, es2)
        Ka = P.sb("Ka" + tag, [66, S], BF16, es2)
        Kb = P.sb("Kb" + tag, [66, S], BF16, es2)
        Qa_b, Ka_b, Kb_b = Buf("Qa"), Buf("Ka"), Buf("Kb")
        V = P.sb("V" + tag, [128, NJT, 129], BF16, es2)
        V_b = Buf("V")
        o0 = P.sb("o0" + tag, [128, NJT, 129], F32, es2)
        o0_b = Buf("o0")
        bA = P.sb("bA" + tag, [128, 64], F32, es2)
        bB = P.sb("bB" + tag, [128, 64], F32, es2)
        bD = P.sb("bD" + tag, [128, 4, 512], F32, es2)
        ones64 = P.sb("ones64" + tag, [64, 64], F32, es2)
        cst_b = Buf("cst")
        gq = P.sb("gq" + tag, [64, 2], F32, es2)
        gk = P.sb("gk" + tag, [64, 1], F32, es2)
        lamv = P.sb("lamv" + tag, [128, 4, 64], F32, es2)
        lamw = P.sb("lamw" + tag, [128, 8], F32, es2)
        subg = P.sb("subgc" + tag, [128, 2], F32, es2)
        li = P.sb("laminit_sb" + tag, [128, 2], F32, es2)
        par_b = Buf("par")
        ld = make_rot(P, "sb", "ald" + tag, [128, 512], F32, 3, es2)
        sq = make_rot(P, "sb", "asq" + tag, [64, 512], F32, 2, es2)
        rs = make_rot(P, "sb", "ars" + tag, [64, 512], F32, 2, es2)
        eT = make_rot(P, "sb", "eT" + tag, [128, 512], BF16, 3, es2)
        dtmp = make_rot(P, "sb", "dtmp" + tag, [128, 512], F32, 2, es2)
        osm = make_rot(P, "sb", "osm" + tag, [128, 136], F32, 3, es2)
        ost = make_rot(P, "sb", "ost" + tag, [128, 8], F32, 3, es2)
        ostage = make_rot(P, "sb", "ostage" + tag, [128, 512], F32, 2, es2)

        P.dma("sp", bA[:], A["biasA"], (), (cst_b,))
        P.dma("sp", bB[:], A["biasB"], (), (cst_b,))
        P.dma("sp", bD[:], A["biasD"], (), (cst_b,))
        P.dma("sp", ones64[:], A["ones64"], (), (cst_b,))
        P.dma("sp", gq[:, 0:1], A["q_gain"].rearrange("(p o) -> p o", o=1), (), (par_b,))
        P.dma("sp", gk[:, 0:1], A["k_gain"].rearrange("(p o) -> p o", o=1), (), (par_b,))
        P.dma("sp", subg[:, 0:1], A["subg"].rearrange("(p o) -> p o", o=1), (), (par_b,))
        P.dma("sp", li[:], A["laminit"], (), (par_b,))
        P.dma("sp", lamv[:].rearrange("p a b -> p (a b)"),
              A["lamv"].rearrange("a b -> (a b)").partition_broadcast(128), (), (par_b,))
        P.op("dve", lambda e: e.tensor_scalar(gq[:, 1:2], gq[:, 0:1], 0.125, None, ALU.mult), (par_b,), (par_b,))
        P.op("dve", lambda e: e.tensor_tensor(lamv[:, 0, :], lamv[:, 0, :], lamv[:, 1, :], ALU.mult), (par_b,), (par_b,))
        P.op("dve", lambda e: e.tensor_tensor(lamv[:, 2, :], lamv[:, 2, :], lamv[:, 3, :], ALU.mult), (par_b,), (par_b,))
        P.op("dve", lambda e: e.reduce_sum(lamw[:, 0:1], lamv[:, 0, :], axis=AX.X), (par_b,), (par_b,))
        P.op("dve", lambda e: e.reduce_sum(lamw[:, 1:2], lamv[:, 2, :], axis=AX.X), (par_b,), (par_b,))
        P.op("act", lambda e: e.activation(lamw[:, 2:4], lamw[:, 0:2], AF.Exp), (par_b,), (par_b,))
        P.op("dve", lambda e: e.tensor_tensor(lamw[:, 4:5], lamw[:, 3:4], lamw[:, 2:3], ALU.subtract), (par_b,), (par_b,))
        P.op("dve", lambda e: e.tensor_scalar(lamw[:, 4:5], lamw[:, 4:5], li[:, 0:1], None, ALU.subtract), (par_b,), (par_b,))
        P.op("dve", lambda e: e.tensor_scalar(subg[:, 1:2], subg[:, 0:1], li[:, 1:2], None, ALU.mult), (par_b,), (par_b,))

        P.op("pool", lambda e: e.memset(V[:, :, 128:129], 1.0), (), (V_b,))
        for ch in range(NI):
            l, lb = ld.next()
            P.dma("sp", l[:], A["vT"][:, ch * 512:(ch + 1) * 512], (), (lb,))
            mt, mb = misc
            for q4 in range(4):
                transp(P, mt[:, q4 * 128:(q4 + 1) * 128], l[:, q4 * 128:(q4 + 1) * 128], ident_f[:],
                       (lb, identf_b), (mb,))
            P.op("act", lambda e, ch=ch, mt=mt: e.activation(
                V[:, ch * 4:(ch + 1) * 4, 0:128], mt[:].rearrange("p (a b) -> p a b", b=128), AF.Copy),
                (mb,), (V_b,))

        for m in range(2):
            P.dma("sp", Qa[64:66, :], A["qaug"], (), (Qa_b,))
            P.dma("sp", Ka[64:66, :], A["kaug_a"], (), (Ka_b,))
            P.dma("sp", Kb[64:66, :], A["kaug_b"], (), (Kb_b,))
            for which in range(2):
                src = A["qT"] if which == 0 else A["kT"]
                for ch in range(NI):
                    cs = slice(ch * 512, (ch + 1) * 512)
                    l, lb = ld.next()
                    P.dma("sp", l[0:64, :], src[m, :, cs], (), (lb,))
                    s_, sb_ = sq.next()
                    P.op("act", lambda e, s_=s_, l=l: e.activation(s_[:], l[0:64, :], AF.Square), (lb,), (sb_,))
                    sc, scb = sc_rot.next()
                    mm(P, sc[0:64, :], ones64[:], s_[:], True, True, (sb_, cst_b), (scb,))
                    r_, rb_ = rs.next()
                    P.op("dve", lambda e, r_=r_, sc=sc: e.tensor_scalar(r_[:], sc[0:64, :], NORM_EPS, None, ALU.add),
                         (scb,), (rb_,))
                    P.op("act", lambda e, r_=r_: e.activation(r_[:], r_[:], AF.Sqrt), (rb_,), (rb_,))
                    P.op("dve", lambda e, r_=r_: e.reciprocal(r_[:], r_[:]), (rb_,), (rb_,))
                    if which == 0:
                        P.op("dve", lambda e, l=l, r_=r_, cs=cs: e.scalar_tensor_tensor(
                            Qa[0:64, cs], l[0:64, :], gq[:, 1:2], r_[:], ALU.mult, ALU.mult),
                            (lb, rb_, par_b), (Qa_b,))
                    else:
                        P.op("dve", lambda e, l=l, r_=r_, cs=cs: e.scalar_tensor_tensor(
                            Ka[0:64, cs], l[0:64, :], gk[:, 0:1], r_[:], ALU.mult, ALU.mult),
                            (lb, rb_, par_b), (Ka_b,))
                        P.op("act", lambda e, cs=cs: e.activation(Kb[0:64, cs], Ka[0:64, cs], AF.Copy),
                             (Ka_b,), (Kb_b,))
            for I in range(NI):
                qs = slice(I * 512, (I + 1) * 512)
                for J in range(NJT):
                    ks = slice(J * 128, (J + 1) * 128)
                    sc, scb = sc_rot.next()
                    et, etb = eT.next()
                    dlt = 4 * I - J
                    if dlt >= 1:
                        mm(P, sc[:], Ka[:, ks], Qa[:, qs], True, True, (Ka_b, Qa_b), (scb,))
                        P.op("act", lambda e, et=et, sc=sc, dlt=dlt: e.activation(
                            et[:], sc[:], AF.Exp, bias=bA[:, dlt:dlt + 1]), (scb, cst_b), (etb,))
                    elif dlt <= -4:
                        mm(P, sc[:], Kb[:, ks], Qa[:, qs], True, True, (Kb_b, Qa_b), (scb,))
                        P.op("act", lambda e, et=et, sc=sc, dlt=dlt: e.activation(
                            et[:], sc[:], AF.Exp, bias=bB[:, -dlt:-dlt + 1]), (scb, cst_b), (etb,))
                    else:
                        off = -dlt
                        mm(P, sc[:], Ka[0:64, ks], Qa[0:64, qs], True, True, (Ka_b, Qa_b), (scb,))
                        dt_, dtb = dtmp.next()
                        P.op("dve", lambda e, dt_=dt_, sc=sc, off=off: e.tensor_tensor(
                            dt_[:], sc[:], bD[:, off, :], ALU.add), (scb, cst_b), (dtb,))
                        P.op("act", lambda e, et=et, dt_=dt_: e.activation(et[:], dt_[:], AF.Exp), (dtb,), (etb,))
                    for qi in range(4):
                        at, ab = acc[qi]
                        mm(P, at[:, 0:129], et[:, qi * 128:(qi + 1) * 128], V[:, J, :], J == 0, J == NJT - 1,
                           (etb, V_b), (ab,))
                for qi in range(4):
                    at, ab = acc[qi]
                    qt = I * 4 + qi
                    if m == 0:
                        P.op("act", lambda e, at=at, qt=qt: e.activation(o0[:, qt, :], at[:, 0:129], AF.Copy),
                             (ab,), (o0_b,))
                        continue
                    st, stb = ost.next()
                    om, omb = osm.next()
                    P.op("dve", lambda e, st=st, qt=qt: e.reciprocal(st[:, 0:1], o0[:, qt, 128:129]), (o0_b,), (stb,))
                    P.op("dve", lambda e, st=st, at=at: e.reciprocal(st[:, 1:2], at[:, 128:129]), (ab,), (stb,))
                    P.op("dve", lambda e, st=st: e.tensor_tensor(st[:, 1:2], st[:, 1:2], lamw[:, 4:5], ALU.mult),
                         (stb, par_b), (stb,))
                    P.op("dve", lambda e, om=om, st=st, qt=qt: e.tensor_scalar(
                        om[:, 0:128], o0[:, qt, 0:128], st[:, 0:1], None, ALU.mult), (o0_b, stb), (omb,))
                    P.op("dve", lambda e, om=om, st=st, at=at: e.scalar_tensor_tensor(
                        om[:, 0:128], at[:, 0:128], st[:, 1:2], om[:, 0:128], ALU.mult, ALU.add),
                        (ab, stb, omb), (omb,))
                    s_, sb_ = sq.next()
                    P.op("act", lambda e, om=om, st=st, s_=s_: e.activation(
                        s_[:, 0:128].bitcast(F32) if False else dtmp.items[0][0][:, 0:128], om[:, 0:128], AF.Square,
                        accum_out=st[:, 2:3]), (omb,), (stb, dtmp.items[0][1]))
                    P.op("dve", lambda e, st=st: e.tensor_scalar(st[:, 3:4], st[:, 2:3], 1.0 / 128, NORM_EPS,
                                                                ALU.mult, ALU.add), (stb,), (stb,))
                    P.op("act", lambda e, st=st: e.activation(st[:, 4:5], st[:, 3:4], AF.Sqrt), (stb,), (stb,))
                    P.op("dve", lambda e, st=st: e.reciprocal(st[:, 5:6], st[:, 4:5]), (stb,), (stb,))
                    P.op("dve", lambda e, om=om, st=st: e.tensor_scalar(
                        om[:, 0:128], om[:, 0:128], st[:, 5:6], None, ALU.mult), (omb, stb), (omb,))
                    mt, mb = misc
                    transp(P, mt[:, qi * 128:(qi + 1) * 128], om[:, 0:128], ident_f[:], (omb, identf_b), (mb,))
                if m == 1:
                    mt, mb = misc
                    og, ogb = ostage.next()
                    P.op("dve", lambda e, og=og, mt=mt: e.tensor_scalar(og[:], mt[:], subg[:, 1:2], None, ALU.mult),
                         (mb, par_b), (ogb,))
                    P.dma("sp", ob_out_d[:, qs], og[:], (ogb,), (), is_output=True)
        P.barrier()


def build_k2a(S):
    nc = bass.Bass("TRN2", target_bir_lowering=False)
    A = attn_dram_inputs(nc, S)
    idf_d = nc.dram_tensor("ident_f", [128, 128], F32, kind="ExternalInput").ap()
    ob_d = nc.dram_tensor("obT", [128, S], F32, kind="ExternalOutput").ap()
    with ExitStack() as es:
        P = Prog(nc, es)
        ident_f = P.sb("sb_ident_f", [128, 128], F32, es)
        identf_b = Buf("identf")
        P.dma("sp", ident_f[:], idf_d, (), (identf_b,))
        phase_attn(P, ident_f, identf_b, A, ob_d, S, "t")
        P.finish()
        P.emit()
    return nc


RW_L = 512


def rwkv_consts():
    p = np.arange(128)[:, None]
    f = np.arange(128)[None, :]
    su = (f > p).astype(np.float32)
    iu = (f >= p).astype(np.float32)
    sl = (f < p).astype(np.float32)
    il = (f <= p).astype(np.float32)
    MK = np.stack([np.concatenate([-su, iu], 1), np.concatenate([-sl, il], 1)])
    BMK = np.stack([np.concatenate([su, iu], 1), np.concatenate([sl, il], 1)])
    NK = np.stack([-sl, -su])
    blk = np.zeros((128, 128), np.float32)
    blk[:64, :64] = 1.0
    blk[64:, 64:] = 1.0
    rm = np.ones((128, RW_L), np.float32)
    rm[:, ::128] = 0.0
    return {"MK": MK, "BMK": BMK, "NK": NK, "onesblk": blk, "resetm": rm}


def rwkv_dram_inputs(nc, S):
    d = {}
    def inp(n, shape):
        d[n] = nc.dram_tensor(n, shape, F32, kind="ExternalInput").ap()
    inp("rkv", [3, 128, S])
    inp("lor", [3, 128, S])
    inp("mu6", [6, 128])
    inp("w0", [2, 128]); inp("w2", [128, 128]); inp("a0", [2, 128]); inp("a2", [128, 128])
    inp("g2", [128, 128])
    inp("vec5", [5, 128])
    inp("MK", [2, 128, 256]); inp("BMK", [2, 128, 256]); inp("NK", [2, 128, 128])
    inp("onesblk", [128, 128]); inp("resetm", [128, RW_L])
    return d


def phase_rwkv(P, ident_f, identf_b, R, oa_out_d, S, tag):
    L = RW_L
    NCH = L // 128
    NSEG = S // L
    pb = Rot(P.banks[0:7])
    ybank = P.banks[7]
    with ExitStack() as es2:
        def sbt(name, shape):
            return P.sb(name + tag, shape, F32, es2)
        MK = sbt("MK", [128, 2, 256]); BMK = sbt("BMK", [128, 2, 256]); NK = sbt("NK", [128, 2, 128])
        onesblk = sbt("onesblk", [128, 128]); resetm = sbt("resetm", [128, L])
        cst_b = Buf("rcst")
        mu = sbt("mu", [128, 6]); hmu = sbt("hmu", [128, 6]); omm = sbt("omm", [128, 6])
        w0c = sbt("w0c", [128, 2]); a0c = sbt("a0c", [128, 2])
        w2s = sbt("w2s", [128, 128]); a2s = sbt("a2s", [128, 128]); g2s = sbt("g2s", [128, 128])
        vec = sbt("vec", [128, 8])
        par_b = Buf("rpar")
        raw = [(sbt("raw%d" % i, [128, L + 2]), Buf("raw%d" % i)) for i in range(6)]
        sh = [(sbt("sh%d" % i, [128, L]), Buf("sh%d" % i)) for i in range(6)]
        names = ["tmpA", "logw", "a_", "a_o", "kap", "kd", "bb", "G", "E1", "E3", "bt", "kt", "bh", "kh", "bon", "gate"]
        tl = {n: (sbt(n, [128, L]), Buf(n)) for n in names}
        KR = sbt("KR", [128, NCH, 256]); KR_b = Buf("KR")
        tot = sbt("tot", [128, NCH]); etot = sbt("etot", [128, NCH]); tot_b = Buf("tot")
        yacc = sbt("yacc", [128, S // 128, 128]); yacc_b = [Buf("yacc%d" % i) for i in range(S // 128)]
        wk256 = make_rot(P, "sb", "wk256" + tag, [128, 256], F32, 6, es2)
        wk128 = make_rot(P, "sb", "wk128" + tag, [128, 128], F32, 12, es2)
        wk64 = make_rot(P, "sb", "wk64" + tag, [128, 64], F32, 6, es2)
        tokr = make_rot(P, "sb", "tokr" + tag, [128, 512], F32, 2, es2)
        Srot = [make_rot(P, "sb", "S%d" % hh + tag, [128, 64], F32, 3, es2) for hh in range(2)]
        gst = make_rot(P, "sb", "gst" + tag, [128, 16], F32, 3, es2)
        ostage = make_rot(P, "sb", "rostage" + tag, [128, 512], F32, 2, es2)

        def dve(fn, reads, writes):
            return P.op("dve", fn, reads, writes)

        def act(fn, reads, writes):
            return P.op("act", fn, reads, writes)

        P.dma("sp", MK[:], R["MK"].rearrange("d p f -> p d f"), (), (cst_b,))
        P.dma("sp", BMK[:], R["BMK"].rearrange("d p f -> p d f"), (), (cst_b,))
        P.dma("sp", NK[:], R["NK"].rearrange("d p f -> p d f"), (), (cst_b,))
        P.dma("sp", onesblk[:], R["onesblk"], (), (cst_b,))
        P.dma("sp", resetm[:], R["resetm"], (), (cst_b,))
        P.dma("sp", mu[:], R["mu6"].rearrange("i p -> p i"), (), (par_b,), allow_slow_non_contiguous=True)
        P.dma("sp", w0c[:], R["w0"].rearrange("i p -> p i"), (), (par_b,), allow_slow_non_contiguous=True)
        P.dma("sp", a0c[:], R["a0"].rearrange("i p -> p i"), (), (par_b,), allow_slow_non_contiguous=True)
        P.dma("sp", vec[:, 0:5], R["vec5"].rearrange("i p -> p i"), (), (par_b,), allow_slow_non_contiguous=True)
        P.dma("sp", w2s[:], R["w2"], (), (par_b,))
        P.dma("sp", a2s[:], R["a2"], (), (par_b,))
        P.dma("sp", g2s[:], R["g2"], (), (par_b,))
        dve(lambda e: e.tensor_scalar(hmu[:], mu[:], 0.5, None, ALU.mult), (par_b,), (par_b,))
        dve(lambda e: e.tensor_scalar(omm[:], mu[:], -1.0, 1.0, ALU.mult, ALU.add), (par_b,), (par_b,))
        dve(lambda e: e.tensor_scalar(vec[:, 5:6], vec[:, 1:2], -1.0, 1.0, ALU.mult, ALU.add), (par_b,), (par_b,))
        dve(lambda e: e.tensor_scalar(vec[:, 6:7], vec[:, 1:2], -2.0, 2.0, ALU.mult, ALU.add), (par_b,), (par_b,))

        def lora_sig(dst, src_i, wsb, biascol, d):
            ds_ = slice(d * 64, (d + 1) * 64)
            st, stb = sh[src_i]
            rhs_t, rhs_b = st, stb
            if src_i == 3:
                tt_, ttb = tl["tmpA"]
                act(lambda e: e.activation(tt_[ds_, :], st[ds_, :], AF.Tanh), (stb,), (ttb,))
                rhs_t, rhs_b = tt_, ttb
            bk, bkb = pb.next()
            mm(P, bk[:, 0:L], wsb[ds_, :], rhs_t[ds_, :], True, True, (par_b, rhs_b), (bkb,))
            act(lambda e: e.activation(dst[0][:], bk[:, 0:L], AF.Sigmoid, bias=biascol[:, d:d + 1]),
                (bkb, par_b), (dst[1],))

        def prep(seg, d, final):
            t0 = seg * L
            use = [0, 1, 2, 3, 4] + ([5] if final else [])
            for i in use:
                src = R["rkv"][i] if i < 3 else R["lor"][i - 3]
                rt, rb = raw[i]
                lo = max(t0 - 1, 0)
                hi = min(t0 + L + 1, S)
                P.dma("sp", rt[:, lo - (t0 - 1):hi - (t0 - 1)], src[:, lo:hi], (), (rb,))
                if t0 == 0:
                    P.op("pool", lambda e, rt=rt: e.memset(rt[:, 0:1], 0.0), (), (rb,))
                if t0 + L == S:
                    P.op("pool", lambda e, rt=rt: e.memset(rt[:, L + 1:L + 2], 0.0), (), (rb,))
                tA, tAb = tl["tmpA"]
                st, stb = sh[i]
                dve(lambda e, rt=rt: e.tensor_tensor(tA[:], rt[:, 0:L], rt[:, 2:L + 2], ALU.add), (rb,), (tAb,))
                dve(lambda e, i=i: e.tensor_scalar(tA[:], tA[:], hmu[:, i:i + 1], None, ALU.mult), (tAb, par_b), (tAb,))
                dve(lambda e, rt=rt, st=st, i=i: e.scalar_tensor_tensor(
                    st[:], rt[:, 1:L + 1], omm[:, i:i + 1], tA[:], ALU.mult, ALU.add), (rb, tAb, par_b), (stb,))
            kp, kpb = tl["kap"]
            tA, tAb = tl["tmpA"]
            ksh, kshb = sh[1]
            dve(lambda e: e.tensor_scalar(kp[:], ksh[:], vec[:, 0:1], None, ALU.mult), (kshb, par_b), (kpb,))
            act(lambda e: e.activation(tA[:], kp[:], AF.Square), (kpb,), (tAb,))
            bk, bkb = pb.next()
            mm(P, bk[:, 0:L], onesblk[:], tA[:], True, True, (cst_b, tAb), (bkb,))
            act(lambda e, bk=bk: e.activation(tA[:], bk[:, 0:L], AF.Sqrt), (bkb,), (tAb,))
            dve(lambda e: e.tensor_scalar(tA[:], tA[:], 1e-12, None, ALU.max), (tAb,), (tAb,))
            dve(lambda e: e.reciprocal(tA[:], tA[:]), (tAb,), (tAb,))
            dve(lambda e: e.tensor_tensor(kp[:], kp[:], tA[:], ALU.mult), (kpb, tAb), (kpb,))
            lw, lwb = tl["logw"]
            lora_sig(tl["logw"], 3, w2s, w0c, d)
            dve(lambda e: e.tensor_scalar(lw[:], lw[:], -DECAY_SCALE, None, ALU.mult), (lwb,), (lwb,))
            lora_sig(tl["a_"], 4, a2s, a0c, d)
            av, avb = tl["a_"]
            kdv, kdb = tl["kd"]
            dve(lambda e: e.tensor_scalar(tA[:], av[:], vec[:, 1:2], vec[:, 5:6], ALU.mult, ALU.add),
                (avb, par_b), (tAb,))
            dve(lambda e: e.tensor_tensor(kdv[:], ksh[:], tA[:], ALU.mult), (kshb, tAb), (kdb,))
            bbv, bbb = tl["bb"]
            dve(lambda e: e.tensor_tensor(bbv[:], kp[:], av[:], ALU.mult), (kpb, avb), (bbb,))
            G, Gb = tl["G"]
            dve(lambda e: e.tensor_tensor_scan(G[:], resetm[:], lw[:], 0.0, ALU.mult, ALU.add), (cst_b, lwb), (Gb,))
            G3 = G[:].rearrange("p (c t) -> p c t", t=128)
            dve(lambda e: e.tensor_copy(tot[:], G3[:, :, 127]), (Gb,), (tot_b,))
            act(lambda e: e.activation(etot[:], tot[:], AF.Exp), (tot_b,), (tot_b,))
            if d == 1:
                tb3 = tot[:].unsqueeze(2).to_broadcast([128, NCH, 128])
                dve(lambda e: e.tensor_tensor(G3, G3, tb3, ALU.subtract), (Gb, tot_b), (Gb,))
                dve(lambda e: e.scalar_tensor_tensor(G[:], G[:], -1.0, lw[:], ALU.mult, ALU.add), (Gb, lwb), (Gb,))
            E1, E1b = tl["E1"]
            E3, E3b = tl["E3"]
            rsh, rshb = sh[0]
            KR3k = KR[:, :, 0:128]
            KR3r = KR[:, :, 128:256]
            act(lambda e: e.activation(E1[:], G[:], AF.Exp), (Gb,), (E1b,))
            dve(lambda e: e.tensor_tensor(KR3r, rsh[:].rearrange("p (c t) -> p c t", t=128),
                                          E1[:].rearrange("p (c t) -> p c t", t=128), ALU.mult),
                (rshb, E1b), (KR_b,))
            dve(lambda e: e.tensor_tensor(tA[:], G[:], lw[:], ALU.subtract), (Gb, lwb), (tAb,))
            act(lambda e: e.activation(E1[:], tA[:], AF.Exp), (tAb,), (E1b,))
            dve(lambda e: e.tensor_tensor(KR3k, kp[:].rearrange("p (c t) -> p c t", t=128),
                                          E1[:].rearrange("p (c t) -> p c t", t=128), ALU.mult),
                (kpb, E1b), (KR_b,))
            act(lambda e: e.activation(E3[:], G[:], AF.Exp, scale=-1.0), (Gb,), (E3b,))
            for nm, srcv in (("bt", tl["bb"]), ("kt", tl["kd"])):
                o_, ob_ = tl[nm]
                dve(lambda e, o_=o_, srcv=srcv: e.tensor_tensor(o_[:], srcv[0][:], E3[:], ALU.mult),
                    (srcv[1], E3b), (ob_,))
            eb3 = etot[:].unsqueeze(2).to_broadcast([128, NCH, 128])
            E33 = E3[:].rearrange("p (c t) -> p c t", t=128)
            dve(lambda e: e.tensor_tensor(E33, E33, eb3, ALU.mult), (E3b, tot_b), (E3b,))
            for nm, srcv in (("bh", tl["bb"]), ("kh", tl["kd"])):
                o_, ob_ = tl[nm]
                dve(lambda e, o_=o_, srcv=srcv: e.tensor_tensor(o_[:], srcv[0][:], E3[:], ALU.mult),
                    (srcv[1], E3b), (ob_,))
            if final:
                lora_sig(tl["a_o"], 4, a2s, a0c, 0)
                ao, aob = tl["a_o"]
                dve(lambda e: e.tensor_tensor(tA[:], av[:], ao[:], ALU.add), (avb, aob), (tAb,))
                dve(lambda e: e.tensor_scalar(tA[:], tA[:], vec[:, 1:2], vec[:, 6:7], ALU.mult, ALU.add),
                    (tAb, par_b), (tAb,))
                dve(lambda e: e.tensor_tensor(tA[:], tA[:], ksh[:], ALU.mult), (tAb, kshb), (tAb,))
                dve(lambda e: e.tensor_tensor(tA[:], tA[:], rsh[:], ALU.mult), (tAb, rshb), (tAb,))
                dve(lambda e: e.tensor_scalar(tA[:], tA[:], vec[:, 2:3], None, ALU.mult), (tAb, par_b), (tAb,))
                bk, bkb = pb.next()
                mm(P, bk[:, 0:L], onesblk[:], tA[:], True, True, (cst_b, tAb), (bkb,))
                bon, bonb = tl["bon"]
                vsh, vshb = sh[2]
                dve(lambda e, bk=bk: e.tensor_tensor(bon[:], bk[:, 0:L], vsh[:], ALU.mult), (bkb, vshb), (bonb,))
                gsh, gshb = sh[5]
                act(lambda e: e.activation(tA[:], gsh[:], AF.Sigmoid), (gshb,), (tAb,))
                bk, bkb = pb.next()
                mm(P, bk[:, 0:L], g2s[:], tA[:], True, True, (par_b, tAb), (bkb,))
                gt, gtb = tl["gate"]
                act(lambda e, bk=bk: e.activation(gt[:], bk[:, 0:L], AF.Copy), (bkb,), (gtb,))

        def chunk(seg, c, d, final, Scur, og):
            gc = seg * NCH + c
            cs = slice(c * 128, (c + 1) * 128)
            bt, btb = tl["bt"]; kt, ktb = tl["kt"]; bh, bhb = tl["bh"]; kh, khb = tl["kh"]
            vsh, vshb = sh[2]
            bk, bkb = pb.next()
            transp(P, bk[:, 0:128], KR[:, c, 0:128], ident_f[:], (KR_b, identf_b), (bkb,))
            transp(P, bk[:, 128:256], bh[:, cs], ident_f[:], (bhb, identf_b), (bkb,))
            transp(P, bk[:, 256:384], kh[:, cs], ident_f[:], (khb, identf_b), (bkb,))
            transp(P, bk[:, 384:512], vsh[:, cs], ident_f[:], (vshb, identf_b), (bkb,))
            tok, tokb = tokr.next()
            act(lambda e: e.activation(tok[:], bk[:], AF.Copy), (bkb,), (tokb,))
            st = []
            for hh in range(2):
                hs = slice(hh * 64, (hh + 1) * 64)
                hc = lambda base, hh=hh: slice(base + hh * 64, base + hh * 64 + 64)
                b1, b1b = pb.next()
                mm(P, b1[:, 0:256], bt[hs, cs], KR[hs, c, :], True, True, (btb, KR_b), (b1b,))
                am1, am1b = wk256.next()
                dve(lambda e, am1=am1, b1=b1: e.tensor_tensor(am1[:], b1[:, 0:256], MK[:, d, :], ALU.mult),
                    (b1b, cst_b), (am1b,))
                b2, b2b = pb.next()
                mm(P, b2[:, 0:256], kt[hs, cs], KR[hs, c, :], True, True, (ktb, KR_b), (b2b,))
                bm2, bm2b = wk256.next()
                dve(lambda e, bm2=bm2, b2=b2: e.tensor_tensor(bm2[:], b2[:, 0:256], BMK[:, d, :], ALU.mult),
                    (b2b, cst_b), (bm2b,))
                b3, b3b = pb.next()
                mm(P, b3[:, 0:128], KR[hs, c, 0:128], bt[hs, cs], True, True, (KR_b, btb), (b3b,))
                nq, nqb = wk128.next()
                dve(lambda e, nq=nq, b3=b3: e.tensor_tensor(nq[:], b3[:, 0:128], NK[:, d, :], ALU.mult),
                    (b3b, cst_b), (nqb,))
                b4, b4b = pb.next()
                mm(P, b4[:, 0:64], bm2[:, 0:128], tok[:, hc(384)], True, True, (bm2b, tokb), (b4b,))
                rh, rhb = wk128.next()
                act(lambda e, rh=rh, tok=tok, hc=hc: e.activation(rh[:, 0:64], tok[:, hc(0)], AF.Copy), (tokb,), (rhb,))
                act(lambda e, rh=rh, b4=b4: e.activation(rh[:, 64:128], b4[:, 0:64], AF.Copy), (b4b,), (rhb,))
                st.append(dict(hs=hs, hc=hc, am1=(am1, am1b), bm2=(bm2, bm2b), Q=(nq, nqb),
                               QT=(am1[:, 0:128], am1b), rh=(rh, rhb)))
            for j in range(7):
                for hh in range(2):
                    X = st[hh]
                    if j > 0:
                        Qv, Qb = X["Q"]
                        QTv, QTb = X["QT"]
                        Qap = Qv[:] if not isinstance(Qv, bass.AP) else Qv
                        c1, c1b = pb.next()
                        mm(P, c1[:, 0:128], Qap, QTv, True, True, (Qb, QTb), (c1b,))
                        nQT, nQTb = wk128.next()
                        act(lambda e, nQT=nQT, c1=c1: e.activation(nQT[:], c1[:, 0:128], AF.Copy), (c1b,), (nQTb,))
                        if j < 6:
                            c2, c2b = pb.next()
                            mm(P, c2[:, 0:128], QTv, Qap, True, True, (Qb, QTb), (c2b,))
                            nQ, nQb = wk128.next()
                            act(lambda e, nQ=nQ, c2=c2: e.activation(nQ[:], c2[:, 0:128], AF.Copy), (c2b,), (nQb,))
                            X["Q"] = (nQ, nQb)
                        X["QT"] = (nQT[:], nQTb)
                    QTv, QTb = X["QT"]
                    rh, rhb = X["rh"]
                    c3, c3b = pb.next()
                    mm(P, c3[:, 0:128], QTv, rh[:], True, True, (QTb, rhb), (c3b,))
                    nr, nrb = wk128.next()
                    dve(lambda e, nr=nr, rh=rh, c3=c3: e.tensor_tensor(nr[:], rh[:], c3[:, 0:128], ALU.add),
                        (rhb, c3b), (nrb,))
                    X["rh"] = (nr, nrb)
            ybk, ybkb = ybank
            newS = []
            for hh in range(2):
                X = st[hh]
                hs, hc = X["hs"], X["hc"]
                rh, rhb = X["rh"]
                am1, am1b = X["am1"]
                bm2, bm2b = X["bm2"]
                u0, u0b = wk64.next()
                act(lambda e, u0=u0, rh=rh: e.activation(u0[:], rh[:, 64:128], AF.Copy, scale=-1.0), (rhb,), (u0b,))
                b8, b8b = pb.next()
                mm(P, b8[hs, 0:64], rh[:, 0:64], tok[:, hc(128)], True, True, (rhb, tokb), (b8b,))
                dgt, dgtb = wk128.next()
                dve(lambda e, dgt=dgt, hs=hs, hh=hh: e.tensor_scalar(
                    dgt[hs, 0:64], ident_f[hs, hh * 64:(hh + 1) * 64], etot[hs, c:c + 1], None, ALU.mult),
                    (identf_b, tot_b), (dgtb,))
                TT, TTb = wk128.next()
                dve(lambda e, TT=TT, b8=b8, dgt=dgt, hs=hs: e.scalar_tensor_tensor(
                    TT[hs, 0:64], b8[hs, 0:64], -1.0, dgt[hs, 0:64], ALU.mult, ALU.add), (b8b, dgtb), (TTb,))
                b9, b9b = pb.next()
                mm(P, b9[hs, 0:64], tok[:, hc(128)], u0[:], True, False, (tokb, u0b), (b9b,))
                mm(P, b9[hs, 0:64], tok[:, hc(256)], tok[:, hc(384)], False, True, (tokb,), (b9b,))
                Dsb, Dsbb = wk64.next()
                act(lambda e, Dsb=Dsb, b9=b9, hs=hs: e.activation(Dsb[hs, :], b9[hs, 0:64], AF.Copy), (b9b,), (Dsbb,))
                b10, b10b = pb.next()
                mm(P, b10[hs, 0:128], rh[:, 0:64], am1[:, 128:256], True, True, (rhb, am1b), (b10b,))
                RpT, RpTb = wk128.next()
                dve(lambda e, RpT=RpT, b10=b10, hs=hs: e.tensor_tensor(
                    RpT[hs, :], KR[hs, c, 128:256], b10[hs, 0:128], ALU.subtract), (KR_b, b10b), (RpTb,))
                S0, S0b = Scur[hh]
                b11, b11b = pb.next()
                mm(P, b11[hs, 0:64], TT[hs, 0:64], S0[hs, :], True, True, (TTb, S0b), (b11b,))
                S1, S1b = Srot[hh].next()
                dve(lambda e, S1=S1, b11=b11, Dsb=Dsb, hs=hs: e.tensor_tensor(
                    S1[hs, :], b11[hs, 0:64], Dsb[hs, :], ALU.add), (b11b, Dsbb), (S1b,))
                newS.append((S1, S1b))
                yo = ybk[:, hh * 64:(hh + 1) * 64]
                mm(P, yo, RpT[hs, :], S0[hs, :], True, False, (RpTb, S0b), (ybkb,))
                mm(P, yo, am1[:, 128:256], u0[:], False, False, (am1b, u0b), (ybkb,))
                mm(P, yo, bm2[:, 128:256], tok[:, hc(384)], False, True, (bm2b, tokb), (ybkb,))
            if not final:
                act(lambda e: e.activation(yacc[:, gc, :], ybk[:, 0:128], AF.Copy), (ybkb,), (yacc_b[gc],))
            else:
                yt, ytb = wk128.next()
                dve(lambda e, yt=yt: e.tensor_tensor(yt[:], yacc[:, gc, :], ybk[:, 0:128], ALU.add),
                    (yacc_b[gc], ybkb), (ytb,))
                g_, gb_ = gst.next()
                for hh in range(2):
                    hcol = slice(hh * 64, (hh + 1) * 64)
                    o6 = hh * 8
                    dve(lambda e, g_=g_, yt=yt, hcol=hcol, o6=o6: e.bn_stats(g_[:, o6:o6 + 6], yt[:, hcol]), (ytb,), (gb_,))
                    dve(lambda e, g_=g_, o6=o6: e.bn_aggr(g_[:, o6 + 6:o6 + 8], g_[:, o6:o6 + 6]), (gb_,), (gb_,))
                    dve(lambda e, g_=g_, o6=o6: e.tensor_scalar(g_[:, o6 + 7:o6 + 8], g_[:, o6 + 7:o6 + 8], GN_EPS, None, ALU.add),
                        (gb_,), (gb_,))
                    act(lambda e, g_=g_, o6=o6: e.activation(g_[:, o6 + 7:o6 + 8], g_[:, o6 + 7:o6 + 8], AF.Sqrt), (gb_,), (gb_,))
                    dve(lambda e, g_=g_, o6=o6: e.reciprocal(g_[:, o6 + 7:o6 + 8], g_[:, o6 + 7:o6 + 8]), (gb_,), (gb_,))
                    dve(lambda e, g_=g_, yt=yt, hcol=hcol, o6=o6: e.tensor_scalar(
                        yt[:, hcol], yt[:, hcol], g_[:, o6 + 6:o6 + 7], g_[:, o6 + 7:o6 + 8], ALU.subtract, ALU.mult),
                        (ytb, gb_), (ytb,))
                tb_, tbb = pb.next()
                transp(P, tb_[:, 0:128], yt[:], ident_f[:], (ytb, identf_b), (tbb,))
                o1, o1b = wk128.next()
                dve(lambda e, o1=o1, tb_=tb_: e.tensor_scalar(o1[:], tb_[:, 0:128], vec[:, 3:4], vec[:, 4:5], ALU.mult, ALU.add),
                    (tbb, par_b), (o1b,))
                bon, bonb = tl["bon"]
                gt, gtb = tl["gate"]
                import os
                DBG = int(os.environ.get("RW_DBG", "0"))
                if DBG == 1:
                    dve(lambda e: e.tensor_copy(og[0][:, cs], gt[:, cs]), (gtb,), (og[1],))
                elif DBG == 2:
                    dve(lambda e: e.tensor_copy(og[0][:, cs], bon[:, cs]), (bonb,), (og[1],))
                elif DBG in (6, 7, 8, 9):
                    srcd = {6: sh[1], 7: tl["a_"], 8: tl["a_o"], 9: sh[2]}[DBG]
                    dve(lambda e, srcd=srcd: e.tensor_copy(og[0][:, cs], srcd[0][:, cs]), (srcd[1],), (og[1],))
                elif DBG == 20:
                    srcd = [tl["G"], tl["E3"], tl["bt"], tl["logw"]][c]
                    dve(lambda e, srcd=srcd: e.tensor_copy(og[0][:, cs], srcd[0][:, cs]), (srcd[1],), (og[1],))
                elif DBG == 21:
                    srcd = [tl["kap"], tl["bb"], tl["kd"], tl["E1"]][c]
                    dve(lambda e, srcd=srcd: e.tensor_copy(og[0][:, cs], srcd[0][:, cs]), (srcd[1],), (og[1],))
                elif DBG == 3:
                    dve(lambda e, o1=o1: e.tensor_copy(og[0][:, cs], o1[:]), (o1b,), (og[1],))
                elif DBG == 4:
                    dve(lambda e: e.tensor_copy(og[0][:, cs], yacc[:, gc, :]), (yacc_b[gc],), (og[1],))
                elif DBG == 5:
                    dve(lambda e: e.tensor_copy(og[0][:, cs], ybk[:, 0:128]), (ybkb,), (og[1],))
                else:
                    dve(lambda e, o1=o1: e.tensor_tensor(o1[:], o1[:], bon[:, cs], ALU.add), (o1b, bonb), (o1b,))
                    dve(lambda e, o1=o1: e.tensor_tensor(og[0][:, cs], o1[:], gt[:, cs], ALU.mult), (o1b, gtb), (og[1],))
            return newS

        for d in range(2):
            final = d == 1
            Scur = []
            for hh in range(2):
                S0, S0b = Srot[hh].next()
                P.op("pool", lambda e, S0=S0: e.memset(S0[:], 0.0), (), (S0b,))
                Scur.append((S0, S0b))
            segs = range(NSEG) if d == 0 else range(NSEG - 1, -1, -1)
            for seg in segs:
                prep(seg, d, final)
                og = ostage.next() if final else None
                chs = range(NCH) if d == 0 else range(NCH - 1, -1, -1)
                for c in chs:
                    Scur = chunk(seg, c, d, final, Scur, og)
                if final:
                    P.dma("sp", oa_out_d[:, seg * L:(seg + 1) * L], og[0][:], (og[1],), (), is_output=True)
        P.barrier()


def build_k2r(S):
    nc = bass.Bass("TRN2", target_bir_lowering=False)
    R = rwkv_dram_inputs(nc, S)
    idf_d = nc.dram_tensor("ident_f", [128, 128], F32, kind="ExternalInput").ap()
    oa_d = nc.dram_tensor("oaT", [128, S], F32, kind="ExternalOutput").ap()
    with ExitStack() as es:
        P = Prog(nc, es)
        ident_f = P.sb("sb_ident_f", [128, 128], F32, es)
        identf_b = Buf("identf")
        P.dma("sp", ident_f[:], idf_d, (), (identf_b,))
        phase_rwkv(P, ident_f, identf_b, R, oa_d, S, "r")
        P.finish()
        P.emit()
    return nc


def consts_common():
    return {
        "ident_bf": np.eye(128, dtype=np.float32).astype(ml_dtypes.bfloat16),
        "ident_f": np.eye(128, dtype=np.float32),
    }


def build_k2(S):
    nc = bass.Bass("TRN2", target_bir_lowering=False)
    R = rwkv_dram_inputs(nc, S)
    A = attn_dram_inputs(nc, S)
    idf_d = nc.dram_tensor("ident_f", [128, 128], F32, kind="ExternalInput").ap()
    oa_d = nc.dram_tensor("oaT", [128, S], F32, kind="ExternalOutput").ap()
    ob_d = nc.dram_tensor("obT", [128, S], F32, kind="ExternalOutput").ap()
    with ExitStack() as es:
        P = Prog(nc, es)
        ident_f = P.sb("sb_ident_f", [128, 128], F32, es)
        identf_b = Buf("identf")
        P.dma("sp", ident_f[:], idf_d, (), (identf_b,))
        phase_rwkv(P, ident_f, identf_b, R, oa_d, S, "r")
        phase_attn(P, ident_f, identf_b, A, ob_d, S, "t")
        P.finish()
        P.emit()
    return nc


def kernel(**inputs):
    f = lambda a: np.ascontiguousarray(np.asarray(a, dtype=np.float32))
    inp = {k: np.asarray(v) for k, v in inputs.items()}
    x = f(inp["x"])
    NT = SEQ // 4
    cores = list(range(NCORES))
    cc = consts_common()
    rc = rwkv_consts()
    ac = [attn_consts(g, SEQ) for g in range(4)]
    ident_f = np.eye(128, dtype=np.float32)

    def k1_weights(l):
        return dict(g1=f(inp["norm_ffn1"][l]), w1=f(inp["ffn1_in"][l]), w2=f(inp["ffn1_out"][l]),
                    gm=f(inp["norm_mix"][l]), win=f(inp["w_in"][l]))

    xs = [f(x[c // 4, (c % 4) * NT:(c % 4 + 1) * NT]) for c in cores]
    nc1 = build_k1(NT)
    w = k1_weights(0)
    res = run_bass_kernel_spmd(nc1, [dict(x=xs[c], **w, **cc) for c in cores], core_ids=cores)
    x1 = [res.results[c]["x1"] for c in cores]
    pt = [res.results[c]["pt"] for c in cores]
    nc2 = build_k2(SEQ)
    nc3n = build_k3(NT, True)
    nc3l = build_k3(NT, False)
    for l in range(DEPTH):
        lam_init = 0.8 - 0.6 * math.exp(-0.3 * l)
        li = np.tile(np.array([[lam_init, 1.0 - lam_init]], np.float32), (128, 1))
        maps = []
        mu = inp["rwkv_mu"][l]
        for c in cores:
            b, g = c // 4, c % 4
            cols = slice(g * 128, (g + 1) * 128)
            PTb = np.concatenate([pt[b * 4 + j][0:C_MIX] for j in range(4)], axis=1)
            rkv = np.stack([PTb[0:512][cols], PTb[512:1024][cols], PTb[1024:1536][cols]])
            lor = PTb[1536:1920].reshape(3, 128, SEQ)
            mu6 = np.stack([mu[0:512][cols], mu[512:1024][cols], mu[1024:1536][cols],
                            mu[1536:1664], mu[1664:1792], mu[1792:1920]])
            pa = PTb[C_RWKV:]
            m = dict(
                rkv=f(rkv), lor=f(lor), mu6=f(mu6),
                w0=f(inp["decay_w0"][l][:, cols]), w2=f(inp["decay_w2"][l][:, :, cols].reshape(128, 128)),
                a0=f(inp["iclr_a0"][l][:, cols]), a2=f(inp["iclr_a2"][l][:, :, cols].reshape(128, 128)),
                g2=f(inp["gate_g2"][l][:, cols]),
                vec5=f(np.stack([inp["k_k"][l][cols], inp["k_a"][l][cols], inp["r_k"][l].reshape(512)[cols],
                                 inp["ln_x_g"][l][cols], inp["ln_x_b"][l][cols]])),
                qT=f(pa[0:512][cols].reshape(2, 64, SEQ)), kT=f(pa[512:1024][cols].reshape(2, 64, SEQ)),
                vT=f(pa[1024:1536][cols]),
                q_gain=f(inp["q_gain"][l]), k_gain=f(inp["k_gain"][l]), lamv=f(inp["diff_lambda"][l]),
                subg=f(inp["subln_g"][l]), laminit=li, ident_f=ident_f, **rc, **ac[g])
            maps.append(m)
        res = run_bass_kernel_spmd(nc2, maps, core_ids=cores)
        oaT = [res.results[c]["oaT"] for c in cores]
        obT = [res.results[c]["obT"] for c in cores]
        maps = []
        last = l == DEPTH - 1
        for c in cores:
            b, j = c // 4, c % 4
            ts = slice(j * NT, (j + 1) * NT)
            oa = np.concatenate([oaT[b * 4 + g][:, ts] for g in range(4)], axis=0)
            ob = np.concatenate([obT[b * 4 + g][:, ts] for g in range(4)], axis=0)
            m = dict(x=f(x1[c]), sg=f(pt[c][C_MIX:]), oa=f(oa), ob=f(ob),
                     wb=f(inp["w_branch"][l]), wout=f(inp["w_out"][l]), g2n=f(inp["norm_ffn2"][l]),
                     w1b=f(inp["ffn2_in"][l]), w2b=f(inp["ffn2_out"][l]), **cc)
            if not last:
                m.update(k1_weights(l + 1))
            maps.append(m)
        res = run_bass_kernel_spmd(nc3l if last else nc3n, maps, core_ids=cores)
        x1 = [res.results[c]["x1"] for c in cores]
        if not last:
            pt = [res.results[c]["pt"] for c in cores]
    out = np.zeros((BATCH, SEQ, D), np.float32)
    for c in cores:
        out[c // 4, (c % 4) * NT:(c % 4 + 1) * NT] = x1[c]
    return out
```

```python
import math
from contextlib import ExitStack
import numpy as np
import ml_dtypes
import concourse.bass as bass
import concourse.mybir as mybir
from concourse.bass_utils import run_bass_kernel_spmd

F32 = mybir.dt.float32
BF16 = mybir.dt.bfloat16
AF = mybir.ActivationFunctionType
ALU = mybir.AluOpType
AX = mybir.AxisListType

D = 1024
DFF = 2816
NJ = DFF // 128
DEPTH = 4
SEQ = 8192
BATCH = 2
C_RWKV = 1920
C_ATTN = 1536
C_MIX = C_RWKV + C_ATTN
C_IN = 5504
NCORES = 8
NORM_EPS = 1e-6
GN_EPS = 64e-5
DECAY_SCALE = math.exp(-0.5)


class Buf:
    __slots__ = ("w", "r", "name")

    def __init__(self, name=""):
        self.w = None
        self.r = []
        self.name = name


ENGS = ("pe", "act", "dve", "pool", "sp")
N_DMA_SEMS = 20


class Prog:
    def __init__(self, nc, es):
        self.nc = nc
        self.es = es
        self.streams = {e: [] for e in ENGS}
        self.count = {e: 0 for e in ENGS}
        self.known = {e: {} for e in ENGS}
        self.sems = {}
        for e in ENGS:
            self.sems[e] = es.enter_context(nc.semaphore("s_" + e))
        self.dma_sems = []
        self.dma_tot = []
        for i in range(N_DMA_SEMS):
            nm = "d%d" % i
            self.sems[nm] = es.enter_context(nc.semaphore("s_" + nm))
            self.dma_sems.append(nm)
            self.dma_tot.append(0)
        self.dma_rr = 0
        self.out_events = []
        self.banks = []
        for i in range(8):
            t = es.enter_context(nc.psum_tensor("bank%d" % i, [128, 512], F32))
            self.banks.append((t, Buf("bank%d" % i)))

    def sb(self, name, shape, dtype, es=None):
        t = (es or self.es).enter_context(self.nc.sbuf_tensor(name, list(shape), dtype))
        return t

    def ps(self, name, shape, dtype=F32, es=None):
        t = (es or self.es).enter_context(self.nc.psum_tensor(name, list(shape), dtype))
        return t

    def _deps(self, eng, reads, writes):
        deps = {}

        def add(ev):
            if ev is None:
                return
            s, v = ev
            if eng == "pe" and s == "pe":
                return
            if deps.get(s, 0) < v:
                deps[s] = v

        for b in reads:
            add(b.w)
        for b in writes:
            add(b.w)
            for ev in b.r:
                add(ev)
        waits = []
        kn = self.known[eng]
        for s, v in deps.items():
            if kn.get(s, 0) < v:
                kn[s] = v
                waits.append((s, v))
        return waits

    def _mark(self, ev, reads, writes):
        for b in writes:
            b.w = ev
            b.r = []
        for b in reads:
            if len(b.r) > 12:
                best = {}
                for s, v in b.r:
                    if best.get(s, 0) < v:
                        best[s] = v
                b.r = list(best.items())
            b.r.append(ev)

    def op(self, eng, fn, reads=(), writes=()):
        waits = self._deps(eng, reads, writes)
        self.count[eng] += 1
        ev = (eng, self.count[eng])
        self.streams[eng].append((waits, fn, (eng, 1)))
        self._mark(ev, reads, writes)
        return ev

    def dma(self, q, out_ap, in_ap, reads=(), writes=(), is_output=False, **kw):
        i = self.dma_rr
        self.dma_rr = (self.dma_rr + 1) % N_DMA_SEMS
        nm = self.dma_sems[i]
        waits = self._deps(q, reads, writes)
        prev = self.dma_tot[i]
        if prev > 0 and self.known[q].get(nm, 0) < prev:
            self.known[q][nm] = prev
            waits.append((nm, prev))
        self.dma_tot[i] = prev + 16
        ev = (nm, prev + 16)

        def fn(e, out_ap=out_ap, in_ap=in_ap, kw=kw):
            return e.dma_start(out=out_ap, in_=in_ap, **kw)

        self.streams[q].append((waits, fn, (nm, 16)))
        self._mark(ev, reads, writes)
        if is_output:
            self.out_events.append(ev)
        return ev

    def collective(self, kind, in_ap, out_ap, groups, reads=(), writes=()):
        i = self.dma_rr
        self.dma_rr = (self.dma_rr + 1) % N_DMA_SEMS
        nm = self.dma_sems[i]
        waits = self._deps("pool", reads, writes)
        prev = self.dma_tot[i]
        if prev > 0 and self.known["pool"].get(nm, 0) < prev:
            self.known["pool"][nm] = prev
            waits.append((nm, prev))
        self.dma_tot[i] = prev + 16
        ev = (nm, prev + 16)

        def fn(e):
            return e.collective_compute(kind, ALU.bypass, replica_groups=groups, ins=[in_ap], outs=[out_ap])

        self.streams["pool"].append((waits, fn, (nm, 16)))
        self._mark(ev, reads, writes)
        return ev

    def barrier(self):
        tot = {e: self.count[e] for e in ENGS}
        for i, nm in enumerate(self.dma_sems):
            tot[nm] = self.dma_tot[i]
        for e in ENGS:
            waits = []
            for sname, v in tot.items():
                if sname == e and e == "pe":
                    continue
                if v > 0 and self.known[e].get(sname, 0) < v:
                    self.known[e][sname] = v
                    waits.append((sname, v))
            if waits:
                self.streams[e].append((waits, None, None))

    def finish(self):
        best = {}
        for s, v in self.out_events:
            if best.get(s, 0) < v:
                best[s] = v
        waits = list(best.items())
        self.streams["sp"].append((waits, None, None))

    def emit(self):
        nc = self.nc
        sems = self.sems
        streams = self.streams

        def run(eng_handle, lst):
            for waits, fn, inc in lst:
                for s, v in waits:
                    eng_handle.wait_ge(sems[s], v)
                if fn is not None:
                    ins = fn(eng_handle)
                    ins.then_inc(sems[inc[0]], inc[1])

        with nc.Block() as block:
            @block.tensor
            def _(e):
                run(e, streams["pe"])

            @block.scalar
            def _(e):
                run(e, streams["act"])

            @block.vector
            def _(e):
                run(e, streams["dve"])

            @block.gpsimd
            def _(e):
                run(e, streams["pool"])

            @block.sync
            def _(e):
                run(e, streams["sp"])


F32R = mybir.dt.float32r


def mm(P, out, lhsT, rhs, start, stop, reads, writes, f32r=False):
    if f32r:
        lhsT = lhsT.bitcast(F32R)
        rhs = rhs.bitcast(F32R)
    return P.op("pe", lambda e: e.matmul(out, lhsT, rhs, start=start, stop=stop), reads, writes)


def transp(P, out, in_, ident, reads, writes):
    return P.op("pe", lambda e: e.transpose(out, in_, ident), reads, writes)


class BankView:
    def __init__(self, t):
        self.t = t

    def __getitem__(self, key):
        v = self.t[:].bitcast(BF16).rearrange("p (c t) -> p c t", t=128)
        return v[key]


class RhView:
    def __init__(self, t):
        self.t = t

    def __getitem__(self, key):
        return self.t[:, 128:256][key]


USE_F32R = True


class Rot:
    def __init__(self, items):
        self.items = items
        self.i = 0

    def next(self):
        it = self.items[self.i]
        self.i = (self.i + 1) % len(self.items)
        return it


def make_rot(P, kind, name, shape, dtype, n, es=None):
    items = []
    for i in range(n):
        if kind == "sb":
            t = P.sb("%s%d" % (name, i), shape, dtype, es)
        else:
            t = P.ps("%s%d" % (name, i), shape, dtype, es)
        items.append((t, Buf("%s%d" % (name, i))))
    return Rot(items)


class TokCtx:
    pass


def setup_tok(P, nc, NT, es):
    T = TokCtx()
    T.NT = NT
    T.NTT = NT // 128
    T.x = P.sb("x_res", [128, T.NTT, D], F32, es)
    T.xb = [Buf("x%d" % i) for i in range(T.NTT)]
    T.ident_bf = P.sb("sb_ident_bf", [128, 128], BF16, es)
    T.ident_b = Buf("ident_bf")
    T.ident_f = P.sb("sb_ident_f", [128, 128], F32, es)
    T.identf_b = Buf("ident_f")
    T.hT = P.sb("hT", [128, 8, 512], BF16, es)
    T.hT_b = Buf("hT")
    T.gcol = P.sb("gcol", [128, 8], F32, es)
    T.gcol_b = Buf("gcol")
    T.xn = make_rot(P, "sb", "xn", [128, D], BF16, 2, es)
    T.sq = P.sb("sq_junk", [128, D], BF16, es)
    T.sq_b = Buf("sq")
    T.stat = make_rot(P, "sb", "stat", [128, 4], F32, 4, es)
    T.w1 = make_rot(P, "sb", "w1s", [128, 8, 256], BF16, 3, es)
    T.ps_tp = Rot([(BankView(P.banks[0][0]), P.banks[0][1])])
    T.ps_a = Rot(P.banks[1:5])
    T.ps_b = Rot(P.banks[5:8])
    return T


def scope_ffn(P, T, es, tag):
    T.u = P.sb("u_hid" + tag, [128, NJ, 512], BF16, es)
    T.u_b = [Buf("u%d" % j) for j in range(NJ)]
    T.w2 = P.sb("w2_res" + tag, [128, NJ, D], BF16, es)
    T.w2_b = [Buf("w2_%d" % j) for j in range(NJ)]
    T.sg = make_rot(P, "sb", "sg" + tag, [128, 512], F32, 2, es)


def scope_proj(P, T, es, tag):
    T.stage = make_rot(P, "sb", "stage" + tag, [128, 512], F32, 3, es)


def load_consts(P, T, ident_bf_d, ident_f_d):
    P.dma("sp", T.ident_bf[:], ident_bf_d, (), (T.ident_b,))
    P.dma("sp", T.ident_f[:], ident_f_d, (), (T.identf_b,))


def load_gain(P, T, g_row):
    P.dma("sp", T.gcol[:], g_row.rearrange("(c p) -> p c", p=128), (), (T.gcol_b,),
          allow_slow_non_contiguous=True)


def prenorm_group(P, T, grp):
    for ti in range(4):
        tt = grp * 4 + ti
        xb = T.xb[tt]
        xt = T.x[:, tt, :]
        st, stb = T.stat.next()
        P.op("act", lambda e, xt=xt, st=st: e.activation(T.sq[:], xt, AF.Square, accum_out=st[:, 0:1]),
             (xb,), (T.sq_b, stb))
        P.op("dve", lambda e, st=st: e.tensor_scalar(st[:, 1:2], st[:, 0:1], 1.0 / D, NORM_EPS, ALU.mult, ALU.add),
             (stb,), (stb,))
        P.op("act", lambda e, st=st: e.activation(st[:, 2:3], st[:, 1:2], AF.Sqrt), (stb,), (stb,))
        P.op("dve", lambda e, st=st: e.reciprocal(st[:, 3:4], st[:, 2:3]), (stb,), (stb,))
        xn, xnb = T.xn.next()
        P.op("act", lambda e, xn=xn, xt=xt, st=st: e.activation(xn[:], xt, AF.Copy, scale=st[:, 3:4]),
             (xb, stb), (xnb,))
        tp, tpb = T.ps_tp.next()
        for c in range(8):
            transp(P, tp[:, c, :], xn[:, c * 128:(c + 1) * 128], T.ident_bf[:], (xnb, T.ident_b), (tpb,))
        gb = T.gcol[:].unsqueeze(2).to_broadcast([128, 8, 128])
        P.op("dve", lambda e, tp=tp, ti=ti, gb=gb: e.tensor_tensor(
            T.hT[:, :, ti * 128:(ti + 1) * 128], tp[:], gb, ALU.mult),
            (tpb, T.gcol_b), (T.hT_b,))


def load_w2(P, T, w2_d):
    for j in range(NJ):
        P.dma("pool", T.w2[:, j, :], w2_d[j * 128:(j + 1) * 128, :], (), (T.w2_b[j],))


def ffn_group(P, T, grp, w1_d):
    for j in range(NJ):
        w1, w1b = T.w1.next()
        P.dma("pool", w1[:, :, 0:128],
              w1_d[:, j * 128:(j + 1) * 128].rearrange("(c p) n -> p c n", p=128), (), (w1b,))
        P.dma("pool", w1[:, :, 128:256],
              w1_d[:, DFF + j * 128:DFF + (j + 1) * 128].rearrange("(c p) n -> p c n", p=128), (), (w1b,))
        pg, pgb = T.ps_a.next()
        pu, pub = T.ps_a.next()
        for c in range(8):
            mm(P, pg[:], w1[:, c, 0:128], T.hT[:, c, :], c == 0, c == 7, (w1b, T.hT_b), (pgb,))
        for c in range(8):
            mm(P, pu[:], w1[:, c, 128:256], T.hT[:, c, :], c == 0, c == 7, (w1b, T.hT_b), (pub,))
        sg, sgb = T.sg.next()
        P.op("act", lambda e, sg=sg, pg=pg: e.activation(sg[:], pg[:], AF.Silu), (pgb,), (sgb,))
        P.op("dve", lambda e, sg=sg, pu=pu, j=j: e.tensor_tensor(T.u[:, j, :], sg[:], pu[:], ALU.mult),
             (sgb, pub), (T.u_b[j],))
    for ti in range(4):
        tt = grp * 4 + ti
        for dh in range(2):
            po, pob = T.ps_b.next()
            for j in range(NJ):
                mm(P, po[:], T.u[:, j, ti * 128:(ti + 1) * 128], T.w2[:, j, dh * 512:(dh + 1) * 512],
                   j == 0, j == NJ - 1, (T.u_b[j], T.w2_b[j]), (pob,))
            xs = T.x[:, tt, dh * 512:(dh + 1) * 512]
            P.op("dve", lambda e, xs=xs, po=po: e.scalar_tensor_tensor(xs, po[:], 0.5, xs, ALU.mult, ALU.add),
                 (pob, T.xb[tt]), (T.xb[tt],))


def proj_group(P, T, grp, w_d, ncols, out_d, tok0, sigmoid_from=None):
    nch = ncols // 128
    for j in range(nch):
        w1, w1b = T.w1.next()
        P.dma("pool", w1[:, :, 0:128],
              w_d[:, j * 128:(j + 1) * 128].rearrange("(c p) n -> p c n", p=128), (), (w1b,))
        pg, pgb = T.ps_a.next()
        for c in range(8):
            mm(P, pg[:], w1[:, c, 0:128], T.hT[:, c, :], c == 0, c == 7, (w1b, T.hT_b), (pgb,))
        sg, sgb = T.stage.next()
        if sigmoid_from is not None and j * 128 >= sigmoid_from:
            P.op("act", lambda e, sg=sg, pg=pg: e.activation(sg[:], pg[:], AF.Sigmoid), (pgb,), (sgb,))
        else:
            P.op("act", lambda e, sg=sg, pg=pg: e.activation(sg[:], pg[:], AF.Copy), (pgb,), (sgb,))
        P.dma("sp", out_d[j * 128:(j + 1) * 128, tok0:tok0 + 512], sg[:], (sgb,), (), is_output=True)


def phase_ffn(P, T, g_d, w1_d, w2_d, tag):
    with ExitStack() as es2:
        scope_ffn(P, T, es2, tag)
        load_gain(P, T, g_d)
        load_w2(P, T, w2_d)
        for grp in range(T.NT // 512):
            prenorm_group(P, T, grp)
            ffn_group(P, T, grp, w1_d)
        P.barrier()


def phase_proj(P, T, gm_d, win_d, pt_d, tag):
    with ExitStack() as es2:
        scope_proj(P, T, es2, tag)
        load_gain(P, T, gm_d)
        for grp in range(T.NT // 512):
            prenorm_group(P, T, grp)
            proj_group(P, T, grp, win_d, C_IN, pt_d, grp * 512, sigmoid_from=C_MIX)
        P.barrier()


def phase_merge(P, T, sg_d, oa_d, ob_d, wb_d, wout_d, tag):
    with ExitStack() as es2:
        wb = P.sb("wb" + tag, [128, 2, 4, D], BF16, es2)
        wb_b = Buf("wb")
        wo = P.sb("wo" + tag, [128, 8, D], BF16, es2)
        wo_b = Buf("wo")
        oT = make_rot(P, "sb", "oT" + tag, [128, 2, 4, 512], BF16, 2, es2)
        sgl = make_rot(P, "sb", "sgl" + tag, [128, 2, 512], F32, 3, es2)
        t1 = make_rot(P, "sb", "mt1" + tag, [128, 512], F32, 2, es2)
        for g in range(2):
            for cc in range(4):
                P.dma("pool", wb[:, g, cc, :], wb_d[g, cc * 128:(cc + 1) * 128, :], (), (wb_b,))
        for dc in range(8):
            P.dma("pool", wo[:, dc, :], wout_d[dc * 128:(dc + 1) * 128, :], (), (wo_b,))
        for grp in range(T.NT // 512):
            tok0 = grp * 512
            o, ob_ = oT.next()
            for g, src in ((0, oa_d), (1, ob_d)):
                P.dma("pool", o[:, g, :, :],
                      src[:, tok0:tok0 + 512].rearrange("(c p) t -> p c t", p=128), (), (ob_,))
            for dc in range(8):
                sl, slb = sgl.next()
                for g in range(2):
                    P.dma("sp", sl[:, g, :], sg_d[g * D + dc * 128:g * D + (dc + 1) * 128, tok0:tok0 + 512],
                          (), (slb,))
                pa, pab = T.ps_a.next()
                pb, pbb = T.ps_a.next()
                for cc in range(4):
                    mm(P, pa[:], wb[:, 0, cc, dc * 128:(dc + 1) * 128], o[:, 0, cc, :], cc == 0, cc == 3,
                       (wb_b, ob_), (pab,))
                for cc in range(4):
                    mm(P, pb[:], wb[:, 1, cc, dc * 128:(dc + 1) * 128], o[:, 1, cc, :], cc == 0, cc == 3,
                       (wb_b, ob_), (pbb,))
                ta, tab = t1.next()
                P.op("dve", lambda e, ta=ta, pa=pa, sl=sl: e.tensor_tensor(ta[:], pa[:], sl[:, 0, :], ALU.mult),
                     (pab, slb), (tab,))
                tb, tbb = t1.next()
                P.op("dve", lambda e, tb=tb, pb=pb, sl=sl: e.tensor_tensor(tb[:], pb[:], sl[:, 1, :], ALU.mult),
                     (pbb, slb), (tbb,))
                P.op("dve", lambda e, ta=ta, tb=tb, dc=dc: e.tensor_tensor(T.hT[:, dc, :], ta[:], tb[:], ALU.add),
                     (tab, tbb), (T.hT_b,))
            for ti in range(4):
                tt = grp * 4 + ti
                for dh in range(2):
                    po, pob = T.ps_b.next()
                    for dc in range(8):
                        mm(P, po[:], T.hT[:, dc, ti * 128:(ti + 1) * 128], wo[:, dc, dh * 512:(dh + 1) * 512],
                           dc == 0, dc == 7, (T.hT_b, wo_b), (pob,))
                    xs = T.x[:, tt, dh * 512:(dh + 1) * 512]
                    P.op("dve", lambda e, xs=xs, po=po: e.tensor_tensor(xs, xs, po[:], ALU.add),
                         (pob, T.xb[tt]), (T.xb[tt],))
        P.barrier()


def tok_io(nc, NT, names):
    d = {}
    shapes = {
        "g1": [D], "w1": [D, 2 * DFF], "w2": [DFF, D], "gm": [D], "win": [D, C_IN],
        "g2n": [D], "w1b": [D, 2 * DFF], "w2b": [DFF, D],
        "wb": [2, 512, D], "wout": [D, D],
    }
    for n in names:
        d[n] = nc.dram_tensor(n, shapes[n], F32, kind="ExternalInput").ap()
    return d


def build_k1(NT):
    nc = bass.Bass("TRN2", target_bir_lowering=False)
    x_d = nc.dram_tensor("x", [NT, D], F32, kind="ExternalInput").ap()
    w = tok_io(nc, NT, ["g1", "w1", "w2", "gm", "win"])
    idb_d = nc.dram_tensor("ident_bf", [128, 128], BF16, kind="ExternalInput").ap()
    idf_d = nc.dram_tensor("ident_f", [128, 128], F32, kind="ExternalInput").ap()
    x1_d = nc.dram_tensor("x1", [NT, D], F32, kind="ExternalOutput").ap()
    pt_d = nc.dram_tensor("pt", [C_IN, NT], F32, kind="ExternalOutput").ap()
    with ExitStack() as es:
        P = Prog(nc, es)
        T = setup_tok(P, nc, NT, es)
        load_consts(P, T, idb_d, idf_d)
        for tt in range(T.NTT):
            P.dma("sp", T.x[:, tt, :], x_d[tt * 128:(tt + 1) * 128, :], (), (T.xb[tt],))
        phase_ffn(P, T, w["g1"], w["w1"], w["w2"], "a")
        for tt in range(T.NTT):
            P.dma("sp", x1_d[tt * 128:(tt + 1) * 128, :], T.x[:, tt, :], (T.xb[tt],), (), is_output=True)
        phase_proj(P, T, w["gm"], w["win"], pt_d, "a")
        P.finish()
        P.emit()
    return nc


def build_k3(NT, with_next):
    nc = bass.Bass("TRN2", target_bir_lowering=False)
    x_d = nc.dram_tensor("x", [NT, D], F32, kind="ExternalInput").ap()
    sg_d = nc.dram_tensor("sg", [2 * D, NT], F32, kind="ExternalInput").ap()
    oa_d = nc.dram_tensor("oa", [512, NT], F32, kind="ExternalInput").ap()
    ob_d = nc.dram_tensor("ob", [512, NT], F32, kind="ExternalInput").ap()
    names = ["wb", "wout", "g2n", "w1b", "w2b"]
    if with_next:
        names += ["g1", "w1", "w2", "gm", "win"]
    w = tok_io(nc, NT, names)
    idb_d = nc.dram_tensor("ident_bf", [128, 128], BF16, kind="ExternalInput").ap()
    idf_d = nc.dram_tensor("ident_f", [128, 128], F32, kind="ExternalInput").ap()
    x1_d = nc.dram_tensor("x1", [NT, D], F32, kind="ExternalOutput").ap()
    if with_next:
        pt_d = nc.dram_tensor("pt", [C_IN, NT], F32, kind="ExternalOutput").ap()
    with ExitStack() as es:
        P = Prog(nc, es)
        T = setup_tok(P, nc, NT, es)
        load_consts(P, T, idb_d, idf_d)
        for tt in range(T.NTT):
            P.dma("sp", T.x[:, tt, :], x_d[tt * 128:(tt + 1) * 128, :], (), (T.xb[tt],))
        phase_merge(P, T, sg_d, oa_d, ob_d, w["wb"], w["wout"], "m")
        phase_ffn(P, T, w["g2n"], w["w1b"], w["w2b"], "b")
        if with_next:
            phase_ffn(P, T, w["g1"], w["w1"], w["w2"], "a")
        for tt in range(T.NTT):
            P.dma("sp", x1_d[tt * 128:(tt + 1) * 128, :], T.x[:, tt, :], (T.xb[tt],), (), is_output=True)
        if with_next:
            phase_proj(P, T, w["gm"], w["win"], pt_d, "a")
        P.finish()
        P.emit()
    return nc


ATT_SHIFT = 4.0


def attn_consts(head, S):
    slope = 2.0 ** (-8.0 * (head + 1) / 4.0)
    ii = np.arange(S) % 512
    qaug = np.stack([(ii // 32) * 32, ii % 32]).astype(np.float32)
    ka = np.full((2, S), -slope, np.float32)
    kb = np.full((2, S), slope, np.float32)
    jj = np.arange(128, dtype=np.float32)[:, None]
    d = np.arange(64, dtype=np.float32)[None, :]
    biasA = -slope * 128.0 * d + slope * jj - ATT_SHIFT
    biasB = -slope * 128.0 * d - slope * jj - ATT_SHIFT
    iq = np.arange(512, dtype=np.float32)[None, None, :]
    off = np.arange(4, dtype=np.float32)[None, :, None]
    biasD = -slope * np.abs(iq - (128.0 * off + jj[:, :, None])) - ATT_SHIFT
    bf = ml_dtypes.bfloat16
    return {
        "qaug": qaug.astype(bf), "kaug_a": ka.astype(bf), "kaug_b": kb.astype(bf),
        "biasA": biasA.astype(np.float32), "biasB": biasB.astype(np.float32),
        "biasD": biasD.astype(np.float32),
        "ones64": np.full((64, 64), 1.0 / 64, np.float32),
    }


def attn_dram_inputs(nc, S):
    d = {}
    d["qT"] = nc.dram_tensor("qT", [2, 64, S], F32, kind="ExternalInput").ap()
    d["kT"] = nc.dram_tensor("kT", [2, 64, S], F32, kind="ExternalInput").ap()
    d["vT"] = nc.dram_tensor("vT", [128, S], F32, kind="ExternalInput").ap()
    d["qaug"] = nc.dram_tensor("qaug", [2, S], BF16, kind="ExternalInput").ap()
    d["kaug_a"] = nc.dram_tensor("kaug_a", [2, S], BF16, kind="ExternalInput").ap()
    d["kaug_b"] = nc.dram_tensor("kaug_b", [2, S], BF16, kind="ExternalInput").ap()
    d["biasA"] = nc.dram_tensor("biasA", [128, 64], F32, kind="ExternalInput").ap()
    d["biasB"] = nc.dram_tensor("biasB", [128, 64], F32, kind="ExternalInput").ap()
    d["biasD"] = nc.dram_tensor("biasD", [128, 4, 512], F32, kind="ExternalInput").ap()
    d["ones64"] = nc.dram_tensor("ones64", [64, 64], F32, kind="ExternalInput").ap()
    d["q_gain"] = nc.dram_tensor("q_gain", [64], F32, kind="ExternalInput").ap()
    d["k_gain"] = nc.dram_tensor("k_gain", [64], F32, kind="ExternalInput").ap()
    d["lamv"] = nc.dram_tensor("lamv", [4, 64], F32, kind="ExternalInput").ap()
    d["subg"] = nc.dram_tensor("subg", [128], F32, kind="ExternalInput").ap()
    d["laminit"] = nc.dram_tensor("laminit", [128, 2], F32, kind="ExternalInput").ap()
    return d


def phase_attn(P, ident_f, identf_b, A, ob_out_d, S, tag):
    NJT = S // 128
    NI = S // 512
    banks = P.banks
    sc_rot = Rot(banks[0:3])
    acc = banks[3:7]
    misc = banks[7]
    with ExitStack() as es2:
        Qa = P.sb("Qa" + tag, [66, S], BF16, es2)
        Ka = P.sb("Ka" + tag, [66, S], BF16, es2)
        Kb = P.sb("Kb" + tag, [66, S], BF16, es2)
        Qa_b, Ka_b, Kb_b = Buf("Qa"), Buf("Ka"), Buf("Kb")
        V = P.sb("V" + tag, [128, NJT, 129], BF16, es2)
        V_b = Buf("V")
        o0 = P.sb("o0" + tag, [128, NJT, 129], F32, es2)
        o0_b = Buf("o0")
        bA = P.sb("bA" + tag, [128, 64], F32, es2)
        bB = P.sb("bB" + tag, [128, 64], F32, es2)
        bD = P.sb("bD" + tag, [128, 4, 512], F32, es2)
        ones64 = P.sb("ones64" + tag, [64, 64], F32, es2)
        cst_b = Buf("cst")
        gq = P.sb("gq" + tag, [64, 2], F32, es2)
        gk = P.sb("gk" + tag, [64, 1], F32, es2)
        lamv = P.sb("lamv" + tag, [128, 4, 64], F32, es2)
        lamw = P.sb("lamw" + tag, [128, 8], F32, es2)
        subg = P.sb("subgc" + tag, [128, 2], F32, es2)
        li = P.sb("laminit_sb" + tag, [128, 2], F32, es2)
        par_b = Buf("par")
        ld = make_rot(P, "sb", "ald" + tag, [128, 512], F32, 3, es2)
        sq = make_rot(P, "sb", "asq" + tag, [64, 512], F32, 2, es2)
        rs = make_rot(P, "sb", "ars" + tag, [64, 512], F32, 2, es2)
        eT = make_rot(P, "sb", "eT" + tag, [128, 512], BF16, 4, es2)
        dtmp = make_rot(P, "sb", "dtmp" + tag, [128, 512], F32, 2, es2)
        osm = make_rot(P, "sb", "osm" + tag, [128, 136], F32, 3, es2)
        ost = make_rot(P, "sb", "ost" + tag, [128, 8], F32, 3, es2)
        ostage = make_rot(P, "sb", "ostage" + tag, [128, 512], F32, 2, es2)

        P.dma("sp", bA[:], A["biasA"], (), (cst_b,))
        P.dma("sp", bB[:], A["biasB"], (), (cst_b,))
        P.dma("sp", bD[:], A["biasD"], (), (cst_b,))
        P.dma("sp", ones64[:], A["ones64"], (), (cst_b,))
        P.dma("sp", gq[:, 0:1], A["q_gain"].rearrange("(p o) -> p o", o=1), (), (par_b,))
        P.dma("sp", gk[:, 0:1], A["k_gain"].rearrange("(p o) -> p o", o=1), (), (par_b,))
        P.dma("sp", subg[:, 0:1], A["subg"].rearrange("(p o) -> p o", o=1), (), (par_b,))
        P.dma("sp", li[:], A["laminit"], (), (par_b,))
        P.dma("sp", lamv[:].rearrange("p a b -> p (a b)"),
              A["lamv"].rearrange("a b -> (a b)").partition_broadcast(128), (), (par_b,))
        P.op("dve", lambda e: e.tensor_scalar(gq[:, 1:2], gq[:, 0:1], 0.125, None, ALU.mult), (par_b,), (par_b,))
        P.op("dve", lambda e: e.tensor_tensor(lamv[:, 0, :], lamv[:, 0, :], lamv[:, 1, :], ALU.mult), (par_b,), (par_b,))
        P.op("dve", lambda e: e.tensor_tensor(lamv[:, 2, :], lamv[:, 2, :], lamv[:, 3, :], ALU.mult), (par_b,), (par_b,))
        P.op("dve", lambda e: e.reduce_sum(lamw[:, 0:1], lamv[:, 0, :], axis=AX.X), (par_b,), (par_b,))
        P.op("dve", lambda e: e.reduce_sum(lamw[:, 1:2], lamv[:, 2, :], axis=AX.X), (par_b,), (par_b,))
        P.op("act", lambda e: e.activation(lamw[:, 2:4], lamw[:, 0:2], AF.Exp), (par_b,), (par_b,))
        P.op("dve", lambda e: e.tensor_tensor(lamw[:, 4:5], lamw[:, 3:4], lamw[:, 2:3], ALU.subtract), (par_b,), (par_b,))
        P.op("dve", lambda e: e.tensor_scalar(lamw[:, 4:5], lamw[:, 4:5], li[:, 0:1], None, ALU.subtract), (par_b,), (par_b,))
        P.op("dve", lambda e: e.tensor_scalar(subg[:, 1:2], subg[:, 0:1], li[:, 1:2], None, ALU.mult), (par_b,), (par_b,))

        P.op("pool", lambda e: e.memset(V[:, :, 128:129], 1.0), (), (V_b,))
        for ch in range(NI):
            l, lb = ld.next()
            P.dma("sp", l[:], A["vT"][:, ch * 512:(ch + 1) * 512], (), (lb,))
            mt, mb = misc
            for q4 in range(4):
                transp(P, mt[:, q4 * 128:(q4 + 1) * 128], l[:, q4 * 128:(q4 + 1) * 128], ident_f[:],
                       (lb, identf_b), (mb,))
            P.op("act", lambda e, ch=ch, mt=mt: e.activation(
                V[:, ch * 4:(ch + 1) * 4, 0:128], mt[:].rearrange("p (a b) -> p a b", b=128), AF.Copy),
                (mb,), (V_b,))

        for m in range(2):
            P.dma("sp", Qa[64:66, :], A["qaug"], (), (Qa_b,))
            P.dma("sp", Ka[64:66, :], A["kaug_a"], (), (Ka_b,))
            P.dma("sp", Kb[64:66, :], A["kaug_b"], (), (Kb_b,))
            for which in range(2):
                src = A["qT"] if which == 0 else A["kT"]
                for ch in range(NI):
                    cs = slice(ch * 512, (ch + 1) * 512)
                    l, lb = ld.next()
                    P.dma("sp", l[0:64, :], src[m, :, cs], (), (lb,))
                    s_, sb_ = sq.next()
                    P.op("act", lambda e, s_=s_, l=l: e.activation(s_[:], l[0:64, :], AF.Square), (lb,), (sb_,))
                    sc, scb = sc_rot.next()
                    mm(P, sc[0:64, :], ones64[:], s_[:], True, True, (sb_, cst_b), (scb,))
                    r_, rb_ = rs.next()
                    P.op("dve", lambda e, r_=r_, sc=sc: e.tensor_scalar(r_[:], sc[0:64, :], NORM_EPS, None, ALU.add),
                         (scb,), (rb_,))
                    P.op("act", lambda e, r_=r_: e.activation(r_[:], r_[:], AF.Sqrt), (rb_,), (rb_,))
                    P.op("dve", lambda e, r_=r_: e.reciprocal(r_[:], r_[:]), (rb_,), (rb_,))
                    if which == 0:
                        P.op("dve", lambda e, l=l, r_=r_, cs=cs: e.scalar_tensor_tensor(
                            Qa[0:64, cs], l[0:64, :], gq[:, 1:2], r_[:], ALU.mult, ALU.mult),
                            (lb, rb_, par_b), (Qa_b,))
                    else:
                        P.op("dve", lambda e, l=l, r_=r_, cs=cs: e.scalar_tensor_tensor(
                            Ka[0:64, cs], l[0:64, :], gk[:, 0:1], r_[:], ALU.mult, ALU.mult),
                            (lb, rb_, par_b), (Ka_b,))
                        P.op("act", lambda e, cs=cs: e.activation(Kb[0:64, cs], Ka[0:64, cs], AF.Copy),
                             (Ka_b,), (Kb_b,))
            def score(I, J):
                qs = slice(I * 512, (I + 1) * 512)
                ks = slice(J * 128, (J + 1) * 128)
                sc, scb = sc_rot.next()
                et, etb = eT.next()
                dlt = 4 * I - J
                if dlt >= 1:
                    mm(P, sc[:], Ka[:, ks], Qa[:, qs], True, True, (Ka_b, Qa_b), (scb,))
                    P.op("act", lambda e, et=et, sc=sc, dlt=dlt: e.activation(
                        et[:], sc[:], AF.Exp, bias=bA[:, dlt:dlt + 1]), (scb, cst_b), (etb,))
                elif dlt <= -4:
                    mm(P, sc[:], Kb[:, ks], Qa[:, qs], True, True, (Kb_b, Qa_b), (scb,))
                    P.op("act", lambda e, et=et, sc=sc, dlt=dlt: e.activation(
                        et[:], sc[:], AF.Exp, bias=bB[:, -dlt:-dlt + 1]), (scb, cst_b), (etb,))
                else:
                    off = -dlt
                    mm(P, sc[:], Ka[0:64, ks], Qa[0:64, qs], True, True, (Ka_b, Qa_b), (scb,))
                    dt_, dtb = dtmp.next()
                    P.op("dve", lambda e, dt_=dt_, sc=sc, off=off: e.tensor_tensor(
                        dt_[:], sc[:], bD[:, off, :], ALU.add), (scb, cst_b), (dtb,))
                    P.op("act", lambda e, et=et, dt_=dt_: e.activation(et[:], dt_[:], AF.Exp), (dtb,), (etb,))
                return et, etb

            seq = [(I, J) for I in range(NI) for J in range(NJT)]
            LOOK = 2
            pend = {}
            for idx in range(min(LOOK, len(seq))):
                pend[idx] = score(*seq[idx])
            for idx, (I, J) in enumerate(seq):
                qs = slice(I * 512, (I + 1) * 512)
                if idx + LOOK < len(seq):
                    pend[idx + LOOK] = score(*seq[idx + LOOK])
                et, etb = pend.pop(idx)
                for qi in range(4):
                    at, ab = acc[qi]
                    mm(P, at[:, 0:129], et[:, qi * 128:(qi + 1) * 128], V[:, J, :], J == 0, J == NJT - 1,
                       (etb, V_b), (ab,))
                if J != NJT - 1:
                    continue
                for qi in range(4):
                    at, ab = acc[qi]
                    qt = I * 4 + qi
                    if m == 0:
                        P.op("dve", lambda e, at=at, qt=qt: e.tensor_copy(o0[:, qt, :], at[:, 0:129]),
                             (ab,), (o0_b,))
                        continue
                    st, stb = ost.next()
                    om, omb = osm.next()
                    P.op("dve", lambda e, st=st, qt=qt: e.reciprocal(st[:, 0:1], o0[:, qt, 128:129]), (o0_b,), (stb,))
                    P.op("dve", lambda e, st=st, at=at: e.reciprocal(st[:, 1:2], at[:, 128:129]), (ab,), (stb,))
                    P.op("dve", lambda e, st=st: e.tensor_tensor(st[:, 1:2], st[:, 1:2], lamw[:, 4:5], ALU.mult),
                         (stb, par_b), (stb,))
                    P.op("dve", lambda e, om=om, st=st, qt=qt: e.tensor_scalar(
                        om[:, 0:128], o0[:, qt, 0:128], st[:, 0:1], None, ALU.mult), (o0_b, stb), (omb,))
                    P.op("dve", lambda e, om=om, st=st, at=at: e.scalar_tensor_tensor(
                        om[:, 0:128], at[:, 0:128], st[:, 1:2], om[:, 0:128], ALU.mult, ALU.add),
                        (ab, stb, omb), (omb,))
                    s_, sb_ = sq.next()
                    P.op("act", lambda e, om=om, st=st, s_=s_: e.activation(
                        s_[:, 0:128].bitcast(F32) if False else dtmp.items[0][0][:, 0:128], om[:, 0:128], AF.Square,
                        accum_out=st[:, 2:3]), (omb,), (stb, dtmp.items[0][1]))
                    P.op("dve", lambda e, st=st: e.tensor_scalar(st[:, 3:4], st[:, 2:3], 1.0 / 128, NORM_EPS,
                                                                ALU.mult, ALU.add), (stb,), (stb,))
                    P.op("act", lambda e, st=st: e.activation(st[:, 4:5], st[:, 3:4], AF.Sqrt), (stb,), (stb,))
                    P.op("dve", lambda e, st=st: e.reciprocal(st[:, 5:6], st[:, 4:5]), (stb,), (stb,))
                    P.op("dve", lambda e, om=om, st=st: e.tensor_scalar(
                        om[:, 0:128], om[:, 0:128], st[:, 5:6], None, ALU.mult), (omb, stb), (omb,))
                    mt, mb = misc
                    transp(P, mt[:, qi * 128:(qi + 1) * 128], om[:, 0:128], ident_f[:], (omb, identf_b), (mb,))
                if m == 1:
                    mt, mb = misc
                    og, ogb = ostage.next()
                    P.op("dve", lambda e, og=og, mt=mt: e.tensor_scalar(og[:], mt[:], subg[:, 1:2], None, ALU.mult),
                         (mb, par_b), (ogb,))
                    P.dma("sp", ob_out_d[:, qs], og[:], (ogb,), (), is_output=True)
        P.barrier()


def build_k2a(S):
    nc = bass.Bass("TRN2", target_bir_lowering=False)
    A = attn_dram_inputs(nc, S)
    idf_d = nc.dram_tensor("ident_f", [128, 128], F32, kind="ExternalInput").ap()
    ob_d = nc.dram_tensor("obT", [128, S], F32, kind="ExternalOutput").ap()
    with ExitStack() as es:
        P = Prog(nc, es)
        ident_f = P.sb("sb_ident_f", [128, 128], F32, es)
        identf_b = Buf("identf")
        P.dma("sp", ident_f[:], idf_d, (), (identf_b,))
        phase_attn(P, ident_f, identf_b, A, ob_d, S, "t")
        P.finish()
        P.emit()
    return nc


RW_L = 512


def rwkv_consts():
    p = np.arange(128)[:, None]
    f = np.arange(128)[None, :]
    su = (f > p).astype(np.float32)
    iu = (f >= p).astype(np.float32)
    sl = (f < p).astype(np.float32)
    il = (f <= p).astype(np.float32)
    MK = np.stack([np.concatenate([-su, iu], 1), np.concatenate([-sl, il], 1)])
    BMK = np.stack([np.concatenate([su, iu], 1), np.concatenate([sl, il], 1)])
    NK = np.stack([-sl, -su])
    blk = np.zeros((128, 128), np.float32)
    blk[:64, :64] = 1.0
    blk[64:, 64:] = 1.0
    rm = np.ones((128, RW_L), np.float32)
    rm[:, ::128] = 0.0
    return {"MK": MK, "BMK": BMK, "NK": NK, "onesblk": blk, "resetm": rm}


def rwkv_dram_inputs(nc, S):
    d = {}
    def inp(n, shape):
        d[n] = nc.dram_tensor(n, shape, F32, kind="ExternalInput").ap()
    inp("rkv", [3, 128, S])
    inp("lor", [3, 128, S])
    inp("mu6", [6, 128])
    inp("w0", [2, 128]); inp("w2", [128, 128]); inp("a0", [2, 128]); inp("a2", [128, 128])
    inp("g2", [128, 128])
    inp("vec5", [5, 128])
    inp("MK", [2, 128, 256]); inp("BMK", [2, 128, 256]); inp("NK", [2, 128, 128])
    inp("onesblk", [128, 128]); inp("resetm", [128, RW_L])
    return d


def phase_rwkv(P, ident_f, identf_b, R, oa_out_d, S, tag):
    L = RW_L
    NCH = L // 128
    NSEG = S // L
    pb = Rot(P.banks[0:7])
    ybank = P.banks[7]
    with ExitStack() as es2:
        def sbt(name, shape):
            return P.sb(name + tag, shape, F32, es2)
        MK = sbt("MK", [128, 2, 256]); BMK = sbt("BMK", [128, 2, 256]); NK = sbt("NK", [128, 2, 128])
        onesblk = sbt("onesblk", [128, 128]); resetm = sbt("resetm", [128, L])
        cst_b = Buf("rcst")
        mu = sbt("mu", [128, 6]); hmu = sbt("hmu", [128, 6]); omm = sbt("omm", [128, 6])
        w0c = sbt("w0c", [128, 2]); a0c = sbt("a0c", [128, 2])
        w2s = sbt("w2s", [128, 128]); a2s = sbt("a2s", [128, 128]); g2s = sbt("g2s", [128, 128])
        vec = sbt("vec", [128, 8])
        par_b = Buf("rpar")
        raw = [(sbt("raw%d" % i, [128, L + 2]), Buf("raw%d" % i)) for i in range(6)]
        sh = [(sbt("sh%d" % i, [128, L]), Buf("sh%d" % i)) for i in range(6)]
        names = ["tmpA", "logw", "a_", "a_o", "kap", "kd", "bb", "G", "E1", "E3", "bt", "kt", "bh", "kh", "bon", "gate"]
        tl = {n: (sbt(n, [128, L]), Buf(n)) for n in names}
        KR = sbt("KR", [128, NCH, 256]); KR_b = Buf("KR")
        tot = sbt("tot", [128, NCH]); etot = sbt("etot", [128, NCH]); tot_b = Buf("tot")
        yacc = sbt("yacc", [128, S // 128, 128]); yacc_b = [Buf("yacc%d" % i) for i in range(S // 128)]
        wk128 = make_rot(P, "sb", "wk128" + tag, [128, 128], F32, 4, es2)
        Srot = [make_rot(P, "sb", "S%d" % hh + tag, [128, 64], F32, 3, es2) for hh in range(2)]
        gst = make_rot(P, "sb", "gst" + tag, [128, 16], F32, 3, es2)
        ostage = make_rot(P, "sb", "rostage" + tag, [128, 512], F32, 2, es2)

        def dve(fn, reads, writes):
            return P.op("dve", fn, reads, writes)

        def act(fn, reads, writes):
            return P.op("act", fn, reads, writes)

        P.dma("sp", MK[:], R["MK"].rearrange("d p f -> p d f"), (), (cst_b,))
        P.dma("sp", BMK[:], R["BMK"].rearrange("d p f -> p d f"), (), (cst_b,))
        P.dma("sp", NK[:], R["NK"].rearrange("d p f -> p d f"), (), (cst_b,))
        P.dma("sp", onesblk[:], R["onesblk"], (), (cst_b,))
        P.dma("sp", resetm[:], R["resetm"], (), (cst_b,))
        P.dma("sp", mu[:], R["mu6"].rearrange("i p -> p i"), (), (par_b,), allow_slow_non_contiguous=True)
        P.dma("sp", w0c[:], R["w0"].rearrange("i p -> p i"), (), (par_b,), allow_slow_non_contiguous=True)
        P.dma("sp", a0c[:], R["a0"].rearrange("i p -> p i"), (), (par_b,), allow_slow_non_contiguous=True)
        P.dma("sp", vec[:, 0:5], R["vec5"].rearrange("i p -> p i"), (), (par_b,), allow_slow_non_contiguous=True)
        P.dma("sp", w2s[:], R["w2"], (), (par_b,))
        P.dma("sp", a2s[:], R["a2"], (), (par_b,))
        P.dma("sp", g2s[:], R["g2"], (), (par_b,))
        dve(lambda e: e.tensor_scalar(hmu[:], mu[:], 0.5, None, ALU.mult), (par_b,), (par_b,))
        dve(lambda e: e.tensor_scalar(omm[:], mu[:], -1.0, 1.0, ALU.mult, ALU.add), (par_b,), (par_b,))
        dve(lambda e: e.tensor_scalar(vec[:, 5:6], vec[:, 1:2], -1.0, 1.0, ALU.mult, ALU.add), (par_b,), (par_b,))
        dve(lambda e: e.tensor_scalar(vec[:, 6:7], vec[:, 1:2], -2.0, 2.0, ALU.mult, ALU.add), (par_b,), (par_b,))

        def lora_sig(dst, src_i, wsb, biascol, d):
            ds_ = slice(d * 64, (d + 1) * 64)
            st, stb = sh[src_i]
            rhs_t, rhs_b = st, stb
            if src_i == 3:
                tt_, ttb = tl["tmpA"]
                act(lambda e: e.activation(tt_[ds_, :], st[ds_, :], AF.Tanh), (stb,), (ttb,))
                rhs_t, rhs_b = tt_, ttb
            bk, bkb = pb.next()
            mm(P, bk[:, 0:L], wsb[ds_, :], rhs_t[ds_, :], True, True, (par_b, rhs_b), (bkb,))
            act(lambda e: e.activation(dst[0][:], bk[:, 0:L], AF.Sigmoid, bias=biascol[:, d:d + 1]),
                (bkb, par_b), (dst[1],))

        def prep(seg, d, final):
            t0 = seg * L
            use = [0, 1, 2, 3, 4] + ([5] if final else [])
            for i in use:
                src = R["rkv"][i] if i < 3 else R["lor"][i - 3]
                rt, rb = raw[i]
                lo = max(t0 - 1, 0)
                hi = min(t0 + L + 1, S)
                P.dma("sp", rt[:, lo - (t0 - 1):hi - (t0 - 1)], src[:, lo:hi], (), (rb,))
                if t0 == 0:
                    P.op("pool", lambda e, rt=rt: e.memset(rt[:, 0:1], 0.0), (), (rb,))
                if t0 + L == S:
                    P.op("pool", lambda e, rt=rt: e.memset(rt[:, L + 1:L + 2], 0.0), (), (rb,))
                tA, tAb = tl["tmpA"]
                st, stb = sh[i]
                dve(lambda e, rt=rt: e.tensor_tensor(tA[:], rt[:, 0:L], rt[:, 2:L + 2], ALU.add), (rb,), (tAb,))
                dve(lambda e, i=i: e.tensor_scalar(tA[:], tA[:], hmu[:, i:i + 1], None, ALU.mult), (tAb, par_b), (tAb,))
                dve(lambda e, rt=rt, st=st, i=i: e.scalar_tensor_tensor(
                    st[:], rt[:, 1:L + 1], omm[:, i:i + 1], tA[:], ALU.mult, ALU.add), (rb, tAb, par_b), (stb,))
            kp, kpb = tl["kap"]
            tA, tAb = tl["tmpA"]
            ksh, kshb = sh[1]
            dve(lambda e: e.tensor_scalar(kp[:], ksh[:], vec[:, 0:1], None, ALU.mult), (kshb, par_b), (kpb,))
            act(lambda e: e.activation(tA[:], kp[:], AF.Square), (kpb,), (tAb,))
            bk, bkb = pb.next()
            mm(P, bk[:, 0:L], onesblk[:], tA[:], True, True, (cst_b, tAb), (bkb,))
            act(lambda e, bk=bk: e.activation(tA[:], bk[:, 0:L], AF.Sqrt), (bkb,), (tAb,))
            dve(lambda e: e.tensor_scalar(tA[:], tA[:], 1e-12, None, ALU.max), (tAb,), (tAb,))
            dve(lambda e: e.reciprocal(tA[:], tA[:]), (tAb,), (tAb,))
            dve(lambda e: e.tensor_tensor(kp[:], kp[:], tA[:], ALU.mult), (kpb, tAb), (kpb,))
            lw, lwb = tl["logw"]
            lora_sig(tl["logw"], 3, w2s, w0c, d)
            dve(lambda e: e.tensor_scalar(lw[:], lw[:], -DECAY_SCALE, None, ALU.mult), (lwb,), (lwb,))
            lora_sig(tl["a_"], 4, a2s, a0c, d)
            av, avb = tl["a_"]
            kdv, kdb = tl["kd"]
            dve(lambda e: e.tensor_scalar(tA[:], av[:], vec[:, 1:2], vec[:, 5:6], ALU.mult, ALU.add),
                (avb, par_b), (tAb,))
            dve(lambda e: e.tensor_tensor(kdv[:], ksh[:], tA[:], ALU.mult), (kshb, tAb), (kdb,))
            bbv, bbb = tl["bb"]
            dve(lambda e: e.tensor_tensor(bbv[:], kp[:], av[:], ALU.mult), (kpb, avb), (bbb,))
            G, Gb = tl["G"]
            dve(lambda e: e.tensor_tensor_scan(G[:], resetm[:], lw[:], 0.0, ALU.mult, ALU.add), (cst_b, lwb), (Gb,))
            G3 = G[:].rearrange("p (c t) -> p c t", t=128)
            dve(lambda e: e.tensor_copy(tot[:], G3[:, :, 127]), (Gb,), (tot_b,))
            act(lambda e: e.activation(etot[:], tot[:], AF.Exp), (tot_b,), (tot_b,))
            if d == 1:
                tb3 = tot[:].unsqueeze(2).to_broadcast([128, NCH, 128])
                dve(lambda e: e.tensor_tensor(G3, G3, tb3, ALU.subtract), (Gb, tot_b), (Gb,))
                dve(lambda e: e.scalar_tensor_tensor(G[:], G[:], -1.0, lw[:], ALU.mult, ALU.add), (Gb, lwb), (Gb,))
            E1, E1b = tl["E1"]
            E3, E3b = tl["E3"]
            rsh, rshb = sh[0]
            KR3k = KR[:, :, 0:128]
            KR3r = KR[:, :, 128:256]
            act(lambda e: e.activation(E1[:], G[:], AF.Exp), (Gb,), (E1b,))
            dve(lambda e: e.tensor_tensor(KR3r, rsh[:].rearrange("p (c t) -> p c t", t=128),
                                          E1[:].rearrange("p (c t) -> p c t", t=128), ALU.mult),
                (rshb, E1b), (KR_b,))
            dve(lambda e: e.tensor_tensor(tA[:], G[:], lw[:], ALU.subtract), (Gb, lwb), (tAb,))
            act(lambda e: e.activation(E1[:], tA[:], AF.Exp), (tAb,), (E1b,))
            dve(lambda e: e.tensor_tensor(KR3k, kp[:].rearrange("p (c t) -> p c t", t=128),
                                          E1[:].rearrange("p (c t) -> p c t", t=128), ALU.mult),
                (kpb, E1b), (KR_b,))
            act(lambda e: e.activation(E3[:], G[:], AF.Exp, scale=-1.0), (Gb,), (E3b,))
            for nm, srcv in (("bt", tl["bb"]), ("kt", tl["kd"])):
                o_, ob_ = tl[nm]
                dve(lambda e, o_=o_, srcv=srcv: e.tensor_tensor(o_[:], srcv[0][:], E3[:], ALU.mult),
                    (srcv[1], E3b), (ob_,))
            eb3 = etot[:].unsqueeze(2).to_broadcast([128, NCH, 128])
            E33 = E3[:].rearrange("p (c t) -> p c t", t=128)
            dve(lambda e: e.tensor_tensor(E33, E33, eb3, ALU.mult), (E3b, tot_b), (E3b,))
            for nm, srcv in (("bh", tl["bb"]), ("kh", tl["kd"])):
                o_, ob_ = tl[nm]
                dve(lambda e, o_=o_, srcv=srcv: e.tensor_tensor(o_[:], srcv[0][:], E3[:], ALU.mult),
                    (srcv[1], E3b), (ob_,))
            if final:
                lora_sig(tl["a_o"], 4, a2s, a0c, 0)
                ao, aob = tl["a_o"]
                dve(lambda e: e.tensor_tensor(tA[:], av[:], ao[:], ALU.add), (avb, aob), (tAb,))
                dve(lambda e: e.tensor_scalar(tA[:], tA[:], vec[:, 1:2], vec[:, 6:7], ALU.mult, ALU.add),
                    (tAb, par_b), (tAb,))
                dve(lambda e: e.tensor_tensor(tA[:], tA[:], ksh[:], ALU.mult), (tAb, kshb), (tAb,))
                dve(lambda e: e.tensor_tensor(tA[:], tA[:], rsh[:], ALU.mult), (tAb, rshb), (tAb,))
                dve(lambda e: e.tensor_scalar(tA[:], tA[:], vec[:, 2:3], None, ALU.mult), (tAb, par_b), (tAb,))
                bk, bkb = pb.next()
                mm(P, bk[:, 0:L], onesblk[:], tA[:], True, True, (cst_b, tAb), (bkb,))
                bon, bonb = tl["bon"]
                vsh, vshb = sh[2]
                dve(lambda e, bk=bk: e.tensor_tensor(bon[:], bk[:, 0:L], vsh[:], ALU.mult), (bkb, vshb), (bonb,))
                gsh, gshb = sh[5]
                act(lambda e: e.activation(tA[:], gsh[:], AF.Sigmoid), (gshb,), (tAb,))
                bk, bkb = pb.next()
                mm(P, bk[:, 0:L], g2s[:], tA[:], True, True, (par_b, tAb), (bkb,))
                gt, gtb = tl["gate"]
                act(lambda e, bk=bk: e.activation(gt[:], bk[:, 0:L], AF.Copy), (bkb,), (gtb,))

        NU = NCH * 2
        UT = []
        for ui in range(NU):
            t = {}
            for nm, shp in (("am1", [128, 256]), ("bm2", [128, 256]), ("QR0", [128, 256]), ("QR1", [128, 256]),
                            ("QT0", [128, 256]), ("QT1", [128, 256]),
                            ("u0", [128, 64]), ("Dsb", [128, 64]), ("dgt", [128, 64]), ("TT", [128, 64]),
                            ("RpT", [128, 128])):
                t[nm] = (sbt("%s_%d" % (nm, ui), shp), Buf("%s_%d" % (nm, ui)))
            UT.append(t)
        TOK = [(sbt("tok_%d" % c, [128, 512]), Buf("tok_%d" % c)) for c in range(NCH)]
        for ui in range(NU):
            for nm in ("QT0", "QT1"):
                tt_, ttb_ = UT[ui][nm]
                P.op("dve", lambda e, tt_=tt_: e.tensor_scalar(tt_[:, 128:256].bitcast(F32R), ident_f[:], 0.0, None, ALU.mult), (identf_b,), (ttb_,))

        def pre_segment(d):
            bt, btb = tl["bt"]; kt, ktb = tl["kt"]; bh, bhb = tl["bh"]; kh, khb = tl["kh"]
            vsh, vshb = sh[2]
            for c in range(NCH):
                cs = slice(c * 128, (c + 1) * 128)
                bk, bkb = pb.next()
                transp(P, bk[:, 0:128], KR[:, c, 0:128], ident_f[:], (KR_b, identf_b), (bkb,))
                transp(P, bk[:, 128:256], bh[:, cs], ident_f[:], (bhb, identf_b), (bkb,))
                transp(P, bk[:, 256:384], kh[:, cs], ident_f[:], (khb, identf_b), (bkb,))
                transp(P, bk[:, 384:512], vsh[:, cs], ident_f[:], (vshb, identf_b), (bkb,))
                tok, tokb = TOK[c]
                act(lambda e, tok=tok, bk=bk: e.activation(tok[:], bk[:], AF.Copy), (bkb,), (tokb,))
            st = []
            for c in range(NCH):
                cs = slice(c * 128, (c + 1) * 128)
                tok, tokb = TOK[c]
                for hh in range(2):
                    T_ = UT[c * 2 + hh]
                    hs = slice(hh * 64, (hh + 1) * 64)
                    hc = lambda base, hh=hh: slice(base + hh * 64, base + hh * 64 + 64)
                    b1, b1b = pb.next()
                    mm(P, b1[:, 0:256], bt[hs, cs], KR[hs, c, :], True, True, (btb, KR_b), (b1b,))
                    am1, am1b = T_["am1"]
                    dve(lambda e, am1=am1, b1=b1: e.tensor_tensor(am1[:].bitcast(F32R), b1[:, 0:256], MK[:, d, :], ALU.mult),
                        (b1b, cst_b), (am1b,))
                    b2, b2b = pb.next()
                    mm(P, b2[:, 0:256], kt[hs, cs], KR[hs, c, :], True, True, (ktb, KR_b), (b2b,))
                    bm2, bm2b = T_["bm2"]
                    dve(lambda e, bm2=bm2, b2=b2: e.tensor_tensor(bm2[:], b2[:, 0:256], BMK[:, d, :], ALU.mult),
                        (b2b, cst_b), (bm2b,))
                    b3, b3b = pb.next()
                    mm(P, b3[:, 0:128], KR[hs, c, 0:128], bt[hs, cs], True, True, (KR_b, btb), (b3b,))
                    qr, qrb = T_["QR0"]
                    dve(lambda e, qr=qr, b3=b3: e.tensor_tensor(qr[:, 0:128].bitcast(F32R), b3[:, 0:128], NK[:, d, :], ALU.mult),
                        (b3b, cst_b), (qrb,))
                    b4, b4b = pb.next()
                    mm(P, b4[:, 0:64], bm2[:, 0:128], tok[:, hc(384)], True, True, (bm2b, tokb), (b4b,))
                    act(lambda e, qr=qr, tok=tok, hc=hc: e.activation(qr[:, 128:192].bitcast(F32R), tok[:, hc(0)], AF.Copy), (tokb,), (qrb,))
                    act(lambda e, qr=qr, b4=b4: e.activation(qr[:, 192:256].bitcast(F32R), b4[:, 0:64], AF.Copy), (b4b,), (qrb,))
                    st.append(dict(c=c, hh=hh, hs=hs, hc=hc, T=T_, am1=(am1, am1b), bm2=(bm2, bm2b),
                                   QR=(qr, qrb), QT=(am1, am1b)))
            for j in range(7):
                for X in st:
                    T_ = X["T"]
                    qr, qrb = X["QR"]
                    qt, qtb = X["QT"]
                    nqr, nqrb = T_["QR%d" % ((j + 1) % 2)]
                    c1, c1b = pb.next()
                    mm(P, c1[:, 0:256], qt[:, 0:128], qr[:, 0:256], True, True, (qtb, qrb), (c1b,), f32r=USE_F32R)
                    if j < 6:
                        act(lambda e, nqr=nqr, c1=c1: e.activation(nqr[:, 0:128].bitcast(F32R), c1[:, 0:128], AF.Copy), (c1b,), (nqrb,))
                    dve(lambda e, nqr=nqr, qr=qr, c1=c1: e.tensor_tensor(nqr[:, 128:256].bitcast(F32R), qr[:, 128:256], c1[:, 128:256], ALU.add),
                        (qrb, c1b), (nqrb,))
                    if j < 6:
                        nqt, nqtb = T_["QT%d" % ((j + 1) % 2)]
                        c2, c2b = pb.next()
                        mm(P, c2[:, 0:256], qr[:, 0:128], qt[:, 0:256], True, True, (qrb, qtb), (c2b,), f32r=USE_F32R)
                        act(lambda e, nqt=nqt, c2=c2: e.activation(nqt[:, 0:128].bitcast(F32R), c2[:, 0:128], AF.Copy), (c2b,), (nqtb,))
                        X["QT"] = (nqt, nqtb)
                    X["QR"] = (nqr, nqrb)
            for X in st:
                qr, qrb = X["QR"]
                X["rh"] = (RhView(qr), qrb)
            for X in st:
                T_ = X["T"]
                c = X["c"]; hh = X["hh"]; hs = X["hs"]; hc = X["hc"]
                tok, tokb = TOK[c]
                rh, rhb = X["rh"]
                am1, am1b = X["am1"]
                u0, u0b = T_["u0"]
                act(lambda e, u0=u0, rh=rh: e.activation(u0[:], rh[:, 64:128], AF.Copy, scale=-1.0), (rhb,), (u0b,))
                b8, b8b = pb.next()
                mm(P, b8[hs, 0:64], rh[:, 0:64], tok[:, hc(128)], True, True, (rhb, tokb), (b8b,))
                dgt, dgtb = T_["dgt"]
                dve(lambda e, dgt=dgt, hs=hs, hh=hh, c=c: e.tensor_scalar(
                    dgt[hs, 0:64], ident_f[hs, hh * 64:(hh + 1) * 64], etot[hs, c:c + 1], None, ALU.mult),
                    (identf_b, tot_b), (dgtb,))
                TT, TTb = T_["TT"]
                dve(lambda e, TT=TT, b8=b8, dgt=dgt, hs=hs: e.scalar_tensor_tensor(
                    TT[hs, 0:64], b8[hs, 0:64], -1.0, dgt[hs, 0:64], ALU.mult, ALU.add), (b8b, dgtb), (TTb,))
                b9, b9b = pb.next()
                mm(P, b9[hs, 0:64], tok[:, hc(128)], u0[:], True, False, (tokb, u0b), (b9b,))
                mm(P, b9[hs, 0:64], tok[:, hc(256)], tok[:, hc(384)], False, True, (tokb,), (b9b,))
                Dsb, Dsbb = T_["Dsb"]
                act(lambda e, Dsb=Dsb, b9=b9, hs=hs: e.activation(Dsb[hs, :], b9[hs, 0:64], AF.Copy), (b9b,), (Dsbb,))
                b10, b10b = pb.next()
                mm(P, b10[hs, 0:128], rh[:, 0:64], am1[:, 128:256], True, True, (rhb, am1b), (b10b,))
                RpT, RpTb = T_["RpT"]
                dve(lambda e, RpT=RpT, b10=b10, hs=hs, c=c: e.tensor_tensor(
                    RpT[hs, :], KR[hs, c, 128:256], b10[hs, 0:128], ALU.subtract), (KR_b, b10b), (RpTb,))
            return st

        def chunk(seg, c, d, final, Scur, og, st):
            gc = seg * NCH + c
            cs = slice(c * 128, (c + 1) * 128)
            tok, tokb = TOK[c]
            ybk, ybkb = ybank
            newS = []
            for hh in range(2):
                X = st[c * 2 + hh]
                T_ = X["T"]
                hs, hc = X["hs"], X["hc"]
                am1, am1b = X["am1"]
                bm2, bm2b = X["bm2"]
                u0, u0b = T_["u0"]
                TT, TTb = T_["TT"]
                Dsb, Dsbb = T_["Dsb"]
                RpT, RpTb = T_["RpT"]
                S0, S0b = Scur[hh]
                b11, b11b = pb.next()
                mm(P, b11[hs, 0:64], TT[hs, 0:64], S0[hs, :], True, True, (TTb, S0b), (b11b,))
                S1, S1b = Srot[hh].next()
                dve(lambda e, S1=S1, b11=b11, Dsb=Dsb, hs=hs: e.tensor_tensor(
                    S1[hs, :], b11[hs, 0:64], Dsb[hs, :], ALU.add), (b11b, Dsbb), (S1b,))
                newS.append((S1, S1b))
                yo = ybk[:, hh * 64:(hh + 1) * 64]
                mm(P, yo, RpT[hs, :], S0[hs, :], True, False, (RpTb, S0b), (ybkb,))
                mm(P, yo, am1[:, 128:256], u0[:], False, False, (am1b, u0b), (ybkb,))
                mm(P, yo, bm2[:, 128:256], tok[:, hc(384)], False, True, (bm2b, tokb), (ybkb,))
            if not final:
                act(lambda e: e.activation(yacc[:, gc, :], ybk[:, 0:128], AF.Copy), (ybkb,), (yacc_b[gc],))
            else:
                yt, ytb = wk128.next()
                dve(lambda e, yt=yt: e.tensor_tensor(yt[:], yacc[:, gc, :], ybk[:, 0:128], ALU.add),
                    (yacc_b[gc], ybkb), (ytb,))
                g_, gb_ = gst.next()
                for hh in range(2):
                    hcol = slice(hh * 64, (hh + 1) * 64)
                    o6 = hh * 8
                    dve(lambda e, g_=g_, yt=yt, hcol=hcol, o6=o6: e.bn_stats(g_[:, o6:o6 + 6], yt[:, hcol]), (ytb,), (gb_,))
                    dve(lambda e, g_=g_, o6=o6: e.bn_aggr(g_[:, o6 + 6:o6 + 8], g_[:, o6:o6 + 6]), (gb_,), (gb_,))
                    dve(lambda e, g_=g_, o6=o6: e.tensor_scalar(g_[:, o6 + 7:o6 + 8], g_[:, o6 + 7:o6 + 8], GN_EPS, None, ALU.add),
                        (gb_,), (gb_,))
                    act(lambda e, g_=g_, o6=o6: e.activation(g_[:, o6 + 7:o6 + 8], g_[:, o6 + 7:o6 + 8], AF.Sqrt), (gb_,), (gb_,))
                    dve(lambda e, g_=g_, o6=o6: e.reciprocal(g_[:, o6 + 7:o6 + 8], g_[:, o6 + 7:o6 + 8]), (gb_,), (gb_,))
                    dve(lambda e, g_=g_, yt=yt, hcol=hcol, o6=o6: e.tensor_scalar(
                        yt[:, hcol], yt[:, hcol], g_[:, o6 + 6:o6 + 7], g_[:, o6 + 7:o6 + 8], ALU.subtract, ALU.mult),
                        (ytb, gb_), (ytb,))
                tb_, tbb = pb.next()
                transp(P, tb_[:, 0:128], yt[:], ident_f[:], (ytb, identf_b), (tbb,))
                o1, o1b = wk128.next()
                dve(lambda e, o1=o1, tb_=tb_: e.tensor_scalar(o1[:], tb_[:, 0:128], vec[:, 3:4], vec[:, 4:5], ALU.mult, ALU.add),
                    (tbb, par_b), (o1b,))
                bon, bonb = tl["bon"]
                gt, gtb = tl["gate"]
                import os
                DBG = int(os.environ.get("RW_DBG", "0"))
                if DBG == 1:
                    dve(lambda e: e.tensor_copy(og[0][:, cs], gt[:, cs]), (gtb,), (og[1],))
                elif DBG == 2:
                    dve(lambda e: e.tensor_copy(og[0][:, cs], bon[:, cs]), (bonb,), (og[1],))
                elif DBG in (6, 7, 8, 9):
                    srcd = {6: sh[1], 7: tl["a_"], 8: tl["a_o"], 9: sh[2]}[DBG]
                    dve(lambda e, srcd=srcd: e.tensor_copy(og[0][:, cs], srcd[0][:, cs]), (srcd[1],), (og[1],))
                elif DBG == 20:
                    srcd = [tl["G"], tl["E3"], tl["bt"], tl["logw"]][c]
                    dve(lambda e, srcd=srcd: e.tensor_copy(og[0][:, cs], srcd[0][:, cs]), (srcd[1],), (og[1],))
                elif DBG == 21:
                    srcd = [tl["kap"], tl["bb"], tl["kd"], tl["E1"]][c]
                    dve(lambda e, srcd=srcd: e.tensor_copy(og[0][:, cs], srcd[0][:, cs]), (srcd[1],), (og[1],))
                elif DBG == 3:
                    dve(lambda e, o1=o1: e.tensor_copy(og[0][:, cs], o1[:]), (o1b,), (og[1],))
                elif DBG == 4:
                    dve(lambda e: e.tensor_copy(og[0][:, cs], yacc[:, gc, :]), (yacc_b[gc],), (og[1],))
                elif DBG == 5:
                    dve(lambda e: e.tensor_copy(og[0][:, cs], ybk[:, 0:128]), (ybkb,), (og[1],))
                else:
                    dve(lambda e, o1=o1: e.tensor_tensor(o1[:], o1[:], bon[:, cs], ALU.add), (o1b, bonb), (o1b,))
                    dve(lambda e, o1=o1: e.tensor_tensor(og[0][:, cs], o1[:], gt[:, cs], ALU.mult), (o1b, gtb), (og[1],))
            return newS

        for d in range(2):
            final = d == 1
            Scur = []
            for hh in range(2):
                S0, S0b = Srot[hh].next()
                P.op("pool", lambda e, S0=S0: e.memset(S0[:], 0.0), (), (S0b,))
                Scur.append((S0, S0b))
            segs = range(NSEG) if d == 0 else range(NSEG - 1, -1, -1)
            for seg in segs:
                prep(seg, d, final)
                og = ostage.next() if final else None
                chs = range(NCH) if d == 0 else range(NCH - 1, -1, -1)
                st = pre_segment(d)
                for c in chs:
                    Scur = chunk(seg, c, d, final, Scur, og, st)
                if final:
                    P.dma("sp", oa_out_d[:, seg * L:(seg + 1) * L], og[0][:], (og[1],), (), is_output=True)
        P.barrier()


def build_k2r(S):
    nc = bass.Bass("TRN2", target_bir_lowering=False)
    R = rwkv_dram_inputs(nc, S)
    idf_d = nc.dram_tensor("ident_f", [128, 128], F32, kind="ExternalInput").ap()
    oa_d = nc.dram_tensor("oaT", [128, S], F32, kind="ExternalOutput").ap()
    with ExitStack() as es:
        P = Prog(nc, es)
        ident_f = P.sb("sb_ident_f", [128, 128], F32, es)
        identf_b = Buf("identf")
        P.dma("sp", ident_f[:], idf_d, (), (identf_b,))
        phase_rwkv(P, ident_f, identf_b, R, oa_d, S, "r")
        P.finish()
        P.emit()
    return nc


def consts_common():
    return {
        "ident_bf": np.eye(128, dtype=np.float32).astype(ml_dtypes.bfloat16),
        "ident_f": np.eye(128, dtype=np.float32),
    }


def build_k2(S):
    nc = bass.Bass("TRN2", target_bir_lowering=False)
    R = rwkv_dram_inputs(nc, S)
    A = attn_dram_inputs(nc, S)
    idf_d = nc.dram_tensor("ident_f", [128, 128], F32, kind="ExternalInput").ap()
    oa_d = nc.dram_tensor("oaT", [128, S], F32, kind="ExternalOutput").ap()
    ob_d = nc.dram_tensor("obT", [128, S], F32, kind="ExternalOutput").ap()
    with ExitStack() as es:
        P = Prog(nc, es)
        ident_f = P.sb("sb_ident_f", [128, 128], F32, es)
        identf_b = Buf("identf")
        P.dma("sp", ident_f[:], idf_d, (), (identf_b,))
        phase_rwkv(P, ident_f, identf_b, R, oa_d, S, "r")
        phase_attn(P, ident_f, identf_b, A, ob_d, S, "t")
        P.finish()
        P.emit()
    return nc


def kernel(**inputs):
    f = lambda a: np.ascontiguousarray(np.asarray(a, dtype=np.float32))
    inp = {k: np.asarray(v) for k, v in inputs.items()}
    x = f(inp["x"])
    NT = SEQ // 4
    cores = list(range(NCORES))
    cc = consts_common()
    rc = rwkv_consts()
    ac = [attn_consts(g, SEQ) for g in range(4)]
    ident_f = np.eye(128, dtype=np.float32)

    def k1_weights(l):
        return dict(g1=f(inp["norm_ffn1"][l]), w1=f(inp["ffn1_in"][l]), w2=f(inp["ffn1_out"][l]),
                    gm=f(inp["norm_mix"][l]), win=f(inp["w_in"][l]))

    xs = [f(x[c // 4, (c % 4) * NT:(c % 4 + 1) * NT]) for c in cores]
    nc1 = build_k1(NT)
    w = k1_weights(0)
    res = run_bass_kernel_spmd(nc1, [dict(x=xs[c], **w, **cc) for c in cores], core_ids=cores)
    x1 = [res.results[c]["x1"] for c in cores]
    pt = [res.results[c]["pt"] for c in cores]
    nc2 = build_k2(SEQ)
    nc3n = build_k3(NT, True)
    nc3l = build_k3(NT, False)
    for l in range(DEPTH):
        lam_init = 0.8 - 0.6 * math.exp(-0.3 * l)
        li = np.tile(np.array([[lam_init, 1.0 - lam_init]], np.float32), (128, 1))
        maps = []
        mu = inp["rwkv_mu"][l]
        for c in cores:
            b, g = c // 4, c % 4
            cols = slice(g * 128, (g + 1) * 128)
            PTb = np.concatenate([pt[b * 4 + j][0:C_MIX] for j in range(4)], axis=1)
            rkv = np.stack([PTb[0:512][cols], PTb[512:1024][cols], PTb[1024:1536][cols]])
            lor = PTb[1536:1920].reshape(3, 128, SEQ)
            mu6 = np.stack([mu[0:512][cols], mu[512:1024][cols], mu[1024:1536][cols],
                            mu[1536:1664], mu[1664:1792], mu[1792:1920]])
            pa = PTb[C_RWKV:]
            m = dict(
                rkv=f(rkv), lor=f(lor), mu6=f(mu6),
                w0=f(inp["decay_w0"][l][:, cols]), w2=f(inp["decay_w2"][l][:, :, cols].reshape(128, 128)),
                a0=f(inp["iclr_a0"][l][:, cols]), a2=f(inp["iclr_a2"][l][:, :, cols].reshape(128, 128)),
                g2=f(inp["gate_g2"][l][:, cols]),
                vec5=f(np.stack([inp["k_k"][l][cols], inp["k_a"][l][cols], inp["r_k"][l].reshape(512)[cols],
                                 inp["ln_x_g"][l][cols], inp["ln_x_b"][l][cols]])),
                qT=f(pa[0:512][cols].reshape(2, 64, SEQ)), kT=f(pa[512:1024][cols].reshape(2, 64, SEQ)),
                vT=f(pa[1024:1536][cols]),
                q_gain=f(inp["q_gain"][l]), k_gain=f(inp["k_gain"][l]), lamv=f(inp["diff_lambda"][l]),
                subg=f(inp["subln_g"][l]), laminit=li, ident_f=ident_f, **rc, **ac[g])
            maps.append(m)
        res = run_bass_kernel_spmd(nc2, maps, core_ids=cores)
        oaT = [res.results[c]["oaT"] for c in cores]
        obT = [res.results[c]["obT"] for c in cores]
        maps = []
        last = l == DEPTH - 1
        for c in cores:
            b, j = c // 4, c % 4
            ts = slice(j * NT, (j + 1) * NT)
            oa = np.concatenate([oaT[b * 4 + g][:, ts] for g in range(4)], axis=0)
            ob = np.concatenate([obT[b * 4 + g][:, ts] for g in range(4)], axis=0)
            m = dict(x=f(x1[c]), sg=f(pt[c][C_MIX:]), oa=f(oa), ob=f(ob),
                     wb=f(inp["w_branch"][l]), wout=f(inp["w_out"][l]), g2n=f(inp["norm_ffn2"][l]),
                     w1b=f(inp["ffn2_in"][l]), w2b=f(inp["ffn2_out"][l]), **cc)
            if not last:
                m.update(k1_weights(l + 1))
            maps.append(m)
        res = run_bass_kernel_spmd(nc3l if last else nc3n, maps, core_ids=cores)
        x1 = [res.results[c]["x1"] for c in cores]
        if not last:
            pt = [res.results[c]["pt"] for c in cores]
    out = np.zeros((BATCH, SEQ, D), np.float32)
    for c in cores:
        out[c // 4, (c % 4) * NT:(c % 4 + 1) * NT] = x1[c]
    return out
```

```python
import math
from contextlib import ExitStack
import numpy as np
import ml_dtypes
import concourse.bass as bass
import concourse.mybir as mybir
from concourse.bass_utils import run_bass_kernel_spmd

F32 = mybir.dt.float32
BF16 = mybir.dt.bfloat16
AF = mybir.ActivationFunctionType
ALU = mybir.AluOpType
AX = mybir.AxisListType

D = 1024
DFF = 2816
NJ = DFF // 128
DEPTH = 4
SEQ = 8192
BATCH = 2
C_RWKV = 1920
C_ATTN = 1536
C_MIX = C_RWKV + C_ATTN
C_IN = 5504
NCORES = 8
NORM_EPS = 1e-6
GN_EPS = 64e-5
DECAY_SCALE = math.exp(-0.5)


class Buf:
    __slots__ = ("w", "r", "name")

    def __init__(self, name=""):
        self.w = None
        self.r = []
        self.name = name


ENGS = ("pe", "act", "dve", "pool", "sp")
N_DMA_SEMS = 20


class Prog:
    def __init__(self, nc, es):
        self.nc = nc
        self.es = es
        self.streams = {e: [] for e in ENGS}
        self.count = {e: 0 for e in ENGS}
        self.known = {e: {} for e in ENGS}
        self.sems = {}
        for e in ENGS:
            self.sems[e] = es.enter_context(nc.semaphore("s_" + e))
        self.dma_sems = []
        self.dma_tot = []
        for i in range(N_DMA_SEMS):
            nm = "d%d" % i
            self.sems[nm] = es.enter_context(nc.semaphore("s_" + nm))
            self.dma_sems.append(nm)
            self.dma_tot.append(0)
        self.dma_rr = 0
        self.out_events = []
        self.banks = []
        for i in range(8):
            t = es.enter_context(nc.psum_tensor("bank%d" % i, [128, 512], F32))
            self.banks.append((t, Buf("bank%d" % i)))

    def sb(self, name, shape, dtype, es=None):
        t = (es or self.es).enter_context(self.nc.sbuf_tensor(name, list(shape), dtype))
        return t

    def ps(self, name, shape, dtype=F32, es=None):
        t = (es or self.es).enter_context(self.nc.psum_tensor(name, list(shape), dtype))
        return t

    def _deps(self, eng, reads, writes):
        deps = {}

        def add(ev):
            if ev is None:
                return
            s, v = ev
            if eng == "pe" and s == "pe":
                return
            if deps.get(s, 0) < v:
                deps[s] = v

        for b in reads:
            add(b.w)
        for b in writes:
            add(b.w)
            for ev in b.r:
                add(ev)
        waits = []
        kn = self.known[eng]
        for s, v in deps.items():
            if kn.get(s, 0) < v:
                kn[s] = v
                waits.append((s, v))
        return waits

    def _mark(self, ev, reads, writes):
        for b in writes:
            b.w = ev
            b.r = []
        for b in reads:
            if len(b.r) > 12:
                best = {}
                for s, v in b.r:
                    if best.get(s, 0) < v:
                        best[s] = v
                b.r = list(best.items())
            b.r.append(ev)

    def op(self, eng, fn, reads=(), writes=()):
        waits = self._deps(eng, reads, writes)
        self.count[eng] += 1
        ev = (eng, self.count[eng])
        self.streams[eng].append((waits, fn, (eng, 1)))
        self._mark(ev, reads, writes)
        return ev

    def dma(self, q, out_ap, in_ap, reads=(), writes=(), is_output=False, **kw):
        i = self.dma_rr
        self.dma_rr = (self.dma_rr + 1) % N_DMA_SEMS
        nm = self.dma_sems[i]
        waits = self._deps(q, reads, writes)
        prev = self.dma_tot[i]
        if prev > 0 and self.known[q].get(nm, 0) < prev:
            self.known[q][nm] = prev
            waits.append((nm, prev))
        self.dma_tot[i] = prev + 16
        ev = (nm, prev + 16)

        def fn(e, out_ap=out_ap, in_ap=in_ap, kw=kw):
            return e.dma_start(out=out_ap, in_=in_ap, **kw)

        self.streams[q].append((waits, fn, (nm, 16)))
        self._mark(ev, reads, writes)
        if is_output:
            self.out_events.append(ev)
        return ev

    def collective(self, kind, in_ap, out_ap, groups, reads=(), writes=()):
        i = self.dma_rr
        self.dma_rr = (self.dma_rr + 1) % N_DMA_SEMS
        nm = self.dma_sems[i]
        waits = self._deps("pool", reads, writes)
        prev = self.dma_tot[i]
        if prev > 0 and self.known["pool"].get(nm, 0) < prev:
            self.known["pool"][nm] = prev
            waits.append((nm, prev))
        self.dma_tot[i] = prev + 16
        ev = (nm, prev + 16)

        def fn(e):
            return e.collective_compute(kind, ALU.bypass, replica_groups=groups, ins=[in_ap], outs=[out_ap])

        self.streams["pool"].append((waits, fn, (nm, 16)))
        self._mark(ev, reads, writes)
        return ev

    def barrier(self):
        tot = {e: self.count[e] for e in ENGS}
        for i, nm in enumerate(self.dma_sems):
            tot[nm] = self.dma_tot[i]
        for e in ENGS:
            waits = []
            for sname, v in tot.items():
                if sname == e and e == "pe":
                    continue
                if v > 0 and self.known[e].get(sname, 0) < v:
                    self.known[e][sname] = v
                    waits.append((sname, v))
            if waits:
                self.streams[e].append((waits, None, None))

    def finish(self):
        best = {}
        for s, v in self.out_events:
            if best.get(s, 0) < v:
                best[s] = v
        waits = list(best.items())
        self.streams["sp"].append((waits, None, None))

    def emit(self):
        nc = self.nc
        sems = self.sems
        streams = self.streams

        def run(eng_handle, lst):
            for waits, fn, inc in lst:
                for s, v in waits:
                    eng_handle.wait_ge(sems[s], v)
                if fn is not None:
                    ins = fn(eng_handle)
                    ins.then_inc(sems[inc[0]], inc[1])

        with nc.Block() as block:
            @block.tensor
            def _(e):
                run(e, streams["pe"])

            @block.scalar
            def _(e):
                run(e, streams["act"])

            @block.vector
            def _(e):
                run(e, streams["dve"])

            @block.gpsimd
            def _(e):
                run(e, streams["pool"])

            @block.sync
            def _(e):
                run(e, streams["sp"])


F32R = mybir.dt.float32r


def mm(P, out, lhsT, rhs, start, stop, reads, writes, f32r=False):
    if f32r:
        lhsT = lhsT.bitcast(F32R)
        rhs = rhs.bitcast(F32R)
    return P.op("pe", lambda e: e.matmul(out, lhsT, rhs, start=start, stop=stop), reads, writes)


def transp(P, out, in_, ident, reads, writes):
    return P.op("pe", lambda e: e.transpose(out, in_, ident), reads, writes)


class BankView:
    def __init__(self, t):
        self.t = t

    def __getitem__(self, key):
        v = self.t[:].bitcast(BF16).rearrange("p (c t) -> p c t", t=128)
        return v[key]


class RhView:
    def __init__(self, t):
        self.t = t

    def __getitem__(self, key):
        return self.t[:, 128:256][key]


USE_F32R = True


class Rot:
    def __init__(self, items):
        self.items = items
        self.i = 0

    def next(self):
        it = self.items[self.i]
        self.i = (self.i + 1) % len(self.items)
        return it


def make_rot(P, kind, name, shape, dtype, n, es=None):
    items = []
    for i in range(n):
        if kind == "sb":
            t = P.sb("%s%d" % (name, i), shape, dtype, es)
        else:
            t = P.ps("%s%d" % (name, i), shape, dtype, es)
        items.append((t, Buf("%s%d" % (name, i))))
    return Rot(items)


class TokCtx:
    pass


def setup_tok(P, nc, NT, es):
    T = TokCtx()
    T.NT = NT
    T.NTT = NT // 128
    T.x = P.sb("x_res", [128, T.NTT, D], F32, es)
    T.xb = [Buf("x%d" % i) for i in range(T.NTT)]
    T.ident_bf = P.sb("sb_ident_bf", [128, 128], BF16, es)
    T.ident_b = Buf("ident_bf")
    T.ident_f = P.sb("sb_ident_f", [128, 128], F32, es)
    T.identf_b = Buf("ident_f")
    T.hT = P.sb("hT", [128, 8, 512], BF16, es)
    T.hT_b = Buf("hT")
    T.gcol = P.sb("gcol", [128, 8], F32, es)
    T.gcol_b = Buf("gcol")
    T.xn = make_rot(P, "sb", "xn", [128, D], BF16, 2, es)
    T.sq = P.sb("sq_junk", [128, D], BF16, es)
    T.sq_b = Buf("sq")
    T.stat = make_rot(P, "sb", "stat", [128, 4], F32, 4, es)
    T.w1 = make_rot(P, "sb", "w1s", [128, 8, 256], BF16, 3, es)
    T.ps_tp = Rot([(BankView(P.banks[0][0]), P.banks[0][1])])
    T.ps_a = Rot(P.banks[1:5])
    T.ps_b = Rot(P.banks[5:8])
    return T


def scope_ffn(P, T, es, tag):
    T.u = P.sb("u_hid" + tag, [128, NJ, 512], BF16, es)
    T.u_b = [Buf("u%d" % j) for j in range(NJ)]
    T.w2 = P.sb("w2_res" + tag, [128, NJ, D], BF16, es)
    T.w2_b = [Buf("w2_%d" % j) for j in range(NJ)]
    T.sg = make_rot(P, "sb", "sg" + tag, [128, 512], F32, 2, es)


def scope_proj(P, T, es, tag):
    T.stage = make_rot(P, "sb", "stage" + tag, [128, 512], F32, 3, es)


def load_consts(P, T, ident_bf_d, ident_f_d):
    P.dma("sp", T.ident_bf[:], ident_bf_d, (), (T.ident_b,))
    P.dma("sp", T.ident_f[:], ident_f_d, (), (T.identf_b,))


def load_gain(P, T, g_row):
    P.dma("sp", T.gcol[:], g_row.rearrange("(c p) -> p c", p=128), (), (T.gcol_b,),
          allow_slow_non_contiguous=True)


def prenorm_group(P, T, grp):
    for ti in range(4):
        tt = grp * 4 + ti
        xb = T.xb[tt]
        xt = T.x[:, tt, :]
        st, stb = T.stat.next()
        P.op("act", lambda e, xt=xt, st=st: e.activation(T.sq[:], xt, AF.Square, accum_out=st[:, 0:1]),
             (xb,), (T.sq_b, stb))
        P.op("dve", lambda e, st=st: e.tensor_scalar(st[:, 1:2], st[:, 0:1], 1.0 / D, NORM_EPS, ALU.mult, ALU.add),
             (stb,), (stb,))
        P.op("act", lambda e, st=st: e.activation(st[:, 2:3], st[:, 1:2], AF.Sqrt), (stb,), (stb,))
        P.op("dve", lambda e, st=st: e.reciprocal(st[:, 3:4], st[:, 2:3]), (stb,), (stb,))
        xn, xnb = T.xn.next()
        P.op("act", lambda e, xn=xn, xt=xt, st=st: e.activation(xn[:], xt, AF.Copy, scale=st[:, 3:4]),
             (xb, stb), (xnb,))
        tp, tpb = T.ps_tp.next()
        for c in range(8):
            transp(P, tp[:, c, :], xn[:, c * 128:(c + 1) * 128], T.ident_bf[:], (xnb, T.ident_b), (tpb,))
        gb = T.gcol[:].unsqueeze(2).to_broadcast([128, 8, 128])
        P.op("dve", lambda e, tp=tp, ti=ti, gb=gb: e.tensor_tensor(
            T.hT[:, :, ti * 128:(ti + 1) * 128], tp[:], gb, ALU.mult),
            (tpb, T.gcol_b), (T.hT_b,))


def load_w2(P, T, w2_d):
    for j in range(NJ):
        P.dma("pool", T.w2[:, j, :], w2_d[j * 128:(j + 1) * 128, :], (), (T.w2_b[j],))


def ffn_group(P, T, grp, w1_d):
    for j in range(NJ):
        w1, w1b = T.w1.next()
        P.dma("pool", w1[:, :, 0:128],
              w1_d[:, j * 128:(j + 1) * 128].rearrange("(c p) n -> p c n", p=128), (), (w1b,))
        P.dma("pool", w1[:, :, 128:256],
              w1_d[:, DFF + j * 128:DFF + (j + 1) * 128].rearrange("(c p) n -> p c n", p=128), (), (w1b,))
        pg, pgb = T.ps_a.next()
        pu, pub = T.ps_a.next()
        for c in range(8):
            mm(P, pg[:], w1[:, c, 0:128], T.hT[:, c, :], c == 0, c == 7, (w1b, T.hT_b), (pgb,))
        for c in range(8):
            mm(P, pu[:], w1[:, c, 128:256], T.hT[:, c, :], c == 0, c == 7, (w1b, T.hT_b), (pub,))
        sg, sgb = T.sg.next()
        P.op("act", lambda e, sg=sg, pg=pg: e.activation(sg[:], pg[:], AF.Silu), (pgb,), (sgb,))
        P.op("dve", lambda e, sg=sg, pu=pu, j=j: e.tensor_tensor(T.u[:, j, :], sg[:], pu[:], ALU.mult),
             (sgb, pub), (T.u_b[j],))
    for ti in range(4):
        tt = grp * 4 + ti
        for dh in range(2):
            po, pob = T.ps_b.next()
            for j in range(NJ):
                mm(P, po[:], T.u[:, j, ti * 128:(ti + 1) * 128], T.w2[:, j, dh * 512:(dh + 1) * 512],
                   j == 0, j == NJ - 1, (T.u_b[j], T.w2_b[j]), (pob,))
            xs = T.x[:, tt, dh * 512:(dh + 1) * 512]
            P.op("dve", lambda e, xs=xs, po=po: e.scalar_tensor_tensor(xs, po[:], 0.5, xs, ALU.mult, ALU.add),
                 (pob, T.xb[tt]), (T.xb[tt],))


def proj_group(P, T, grp, w_d, ncols, out_d, tok0, sigmoid_from=None):
    nch = ncols // 128
    for j in range(nch):
        w1, w1b = T.w1.next()
        P.dma("pool", w1[:, :, 0:128],
              w_d[:, j * 128:(j + 1) * 128].rearrange("(c p) n -> p c n", p=128), (), (w1b,))
        pg, pgb = T.ps_a.next()
        for c in range(8):
            mm(P, pg[:], w1[:, c, 0:128], T.hT[:, c, :], c == 0, c == 7, (w1b, T.hT_b), (pgb,))
        sg, sgb = T.stage.next()
        if sigmoid_from is not None and j * 128 >= sigmoid_from:
            P.op("act", lambda e, sg=sg, pg=pg: e.activation(sg[:], pg[:], AF.Sigmoid), (pgb,), (sgb,))
        else:
            P.op("act", lambda e, sg=sg, pg=pg: e.activation(sg[:], pg[:], AF.Copy), (pgb,), (sgb,))
        P.dma("sp", out_d[j * 128:(j + 1) * 128, tok0:tok0 + 512], sg[:], (sgb,), (), is_output=True)


def phase_ffn(P, T, g_d, w1_d, w2_d, tag):
    with ExitStack() as es2:
        scope_ffn(P, T, es2, tag)
        load_gain(P, T, g_d)
        load_w2(P, T, w2_d)
        for grp in range(T.NT // 512):
            prenorm_group(P, T, grp)
            ffn_group(P, T, grp, w1_d)
        P.barrier()


def phase_proj(P, T, gm_d, win_d, pt_d, tag):
    with ExitStack() as es2:
        scope_proj(P, T, es2, tag)
        load_gain(P, T, gm_d)
        for grp in range(T.NT // 512):
            prenorm_group(P, T, grp)
            proj_group(P, T, grp, win_d, C_IN, pt_d, grp * 512, sigmoid_from=C_MIX)
        P.barrier()


def phase_merge(P, T, sg_d, oa_d, ob_d, wb_d, wout_d, tag):
    with ExitStack() as es2:
        wb = P.sb("wb" + tag, [128, 2, 4, D], BF16, es2)
        wb_b = Buf("wb")
        wo = P.sb("wo" + tag, [128, 8, D], BF16, es2)
        wo_b = Buf("wo")
        oT = make_rot(P, "sb", "oT" + tag, [128, 2, 4, 512], BF16, 2, es2)
        sgl = make_rot(P, "sb", "sgl" + tag, [128, 2, 512], F32, 3, es2)
        t1 = make_rot(P, "sb", "mt1" + tag, [128, 512], F32, 2, es2)
        for g in range(2):
            for cc in range(4):
                P.dma("pool", wb[:, g, cc, :], wb_d[g, cc * 128:(cc + 1) * 128, :], (), (wb_b,))
        for dc in range(8):
            P.dma("pool", wo[:, dc, :], wout_d[dc * 128:(dc + 1) * 128, :], (), (wo_b,))
        for grp in range(T.NT // 512):
            tok0 = grp * 512
            o, ob_ = oT.next()
            for g, src in ((0, oa_d), (1, ob_d)):
                P.dma("pool", o[:, g, :, :],
                      src[:, tok0:tok0 + 512].rearrange("(c p) t -> p c t", p=128), (), (ob_,))
            for dc in range(8):
                sl, slb = sgl.next()
                for g in range(2):
                    P.dma("sp", sl[:, g, :], sg_d[g * D + dc * 128:g * D + (dc + 1) * 128, tok0:tok0 + 512],
                          (), (slb,))
                pa, pab = T.ps_a.next()
                pb, pbb = T.ps_a.next()
                for cc in range(4):
                    mm(P, pa[:], wb[:, 0, cc, dc * 128:(dc + 1) * 128], o[:, 0, cc, :], cc == 0, cc == 3,
                       (wb_b, ob_), (pab,))
                for cc in range(4):
                    mm(P, pb[:], wb[:, 1, cc, dc * 128:(dc + 1) * 128], o[:, 1, cc, :], cc == 0, cc == 3,
                       (wb_b, ob_), (pbb,))
                ta, tab = t1.next()
                P.op("dve", lambda e, ta=ta, pa=pa, sl=sl: e.tensor_tensor(ta[:], pa[:], sl[:, 0, :], ALU.mult),
                     (pab, slb), (tab,))
                tb, tbb = t1.next()
                P.op("dve", lambda e, tb=tb, pb=pb, sl=sl: e.tensor_tensor(tb[:], pb[:], sl[:, 1, :], ALU.mult),
                     (pbb, slb), (tbb,))
                P.op("dve", lambda e, ta=ta, tb=tb, dc=dc: e.tensor_tensor(T.hT[:, dc, :], ta[:], tb[:], ALU.add),
                     (tab, tbb), (T.hT_b,))
            for ti in range(4):
                tt = grp * 4 + ti
                for dh in range(2):
                    po, pob = T.ps_b.next()
                    for dc in range(8):
                        mm(P, po[:], T.hT[:, dc, ti * 128:(ti + 1) * 128], wo[:, dc, dh * 512:(dh + 1) * 512],
                           dc == 0, dc == 7, (T.hT_b, wo_b), (pob,))
                    xs = T.x[:, tt, dh * 512:(dh + 1) * 512]
                    P.op("dve", lambda e, xs=xs, po=po: e.tensor_tensor(xs, xs, po[:], ALU.add),
                         (pob, T.xb[tt]), (T.xb[tt],))
        P.barrier()


def tok_io(nc, NT, names):
    d = {}
    shapes = {
        "g1": [D], "w1": [D, 2 * DFF], "w2": [DFF, D], "gm": [D], "win": [D, C_IN],
        "g2n": [D], "w1b": [D, 2 * DFF], "w2b": [DFF, D],
        "wb": [2, 512, D], "wout": [D, D],
    }
    for n in names:
        d[n] = nc.dram_tensor(n, shapes[n], F32, kind="ExternalInput").ap()
    return d


def build_k1(NT):
    nc = bass.Bass("TRN2", target_bir_lowering=False)
    x_d = nc.dram_tensor("x", [NT, D], F32, kind="ExternalInput").ap()
    w = tok_io(nc, NT, ["g1", "w1", "w2", "gm", "win"])
    idb_d = nc.dram_tensor("ident_bf", [128, 128], BF16, kind="ExternalInput").ap()
    idf_d = nc.dram_tensor("ident_f", [128, 128], F32, kind="ExternalInput").ap()
    x1_d = nc.dram_tensor("x1", [NT, D], F32, kind="ExternalOutput").ap()
    pt_d = nc.dram_tensor("pt", [C_IN, NT], F32, kind="ExternalOutput").ap()
    with ExitStack() as es:
        P = Prog(nc, es)
        T = setup_tok(P, nc, NT, es)
        load_consts(P, T, idb_d, idf_d)
        for tt in range(T.NTT):
            P.dma("sp", T.x[:, tt, :], x_d[tt * 128:(tt + 1) * 128, :], (), (T.xb[tt],))
        phase_ffn(P, T, w["g1"], w["w1"], w["w2"], "a")
        for tt in range(T.NTT):
            P.dma("sp", x1_d[tt * 128:(tt + 1) * 128, :], T.x[:, tt, :], (T.xb[tt],), (), is_output=True)
        phase_proj(P, T, w["gm"], w["win"], pt_d, "a")
        P.finish()
        P.emit()
    return nc


def build_k3(NT, with_next):
    nc = bass.Bass("TRN2", target_bir_lowering=False)
    x_d = nc.dram_tensor("x", [NT, D], F32, kind="ExternalInput").ap()
    sg_d = nc.dram_tensor("sg", [2 * D, NT], F32, kind="ExternalInput").ap()
    oa_d = nc.dram_tensor("oa", [512, NT], F32, kind="ExternalInput").ap()
    ob_d = nc.dram_tensor("ob", [512, NT], F32, kind="ExternalInput").ap()
    names = ["wb", "wout", "g2n", "w1b", "w2b"]
    if with_next:
        names += ["g1", "w1", "w2", "gm", "win"]
    w = tok_io(nc, NT, names)
    idb_d = nc.dram_tensor("ident_bf", [128, 128], BF16, kind="ExternalInput").ap()
    idf_d = nc.dram_tensor("ident_f", [128, 128], F32, kind="ExternalInput").ap()
    x1_d = nc.dram_tensor("x1", [NT, D], F32, kind="ExternalOutput").ap()
    if with_next:
        pt_d = nc.dram_tensor("pt", [C_IN, NT], F32, kind="ExternalOutput").ap()
    with ExitStack() as es:
        P = Prog(nc, es)
        T = setup_tok(P, nc, NT, es)
        load_consts(P, T, idb_d, idf_d)
        for tt in range(T.NTT):
            P.dma("sp", T.x[:, tt, :], x_d[tt * 128:(tt + 1) * 128, :], (), (T.xb[tt],))
        phase_merge(P, T, sg_d, oa_d, ob_d, w["wb"], w["wout"], "m")
        phase_ffn(P, T, w["g2n"], w["w1b"], w["w2b"], "b")
        if with_next:
            phase_ffn(P, T, w["g1"], w["w1"], w["w2"], "a")
        for tt in range(T.NTT):
            P.dma("sp", x1_d[tt * 128:(tt + 1) * 128, :], T.x[:, tt, :], (T.xb[tt],), (), is_output=True)
        if with_next:
            phase_proj(P, T, w["gm"], w["win"], pt_d, "a")
        P.finish()
        P.emit()
    return nc


ATT_SHIFT = 4.0


def attn_consts(head, S):
    slope = 2.0 ** (-8.0 * (head + 1) / 4.0)
    ii = np.arange(S) % 512
    qaug = np.stack([(ii // 32) * 32, ii % 32]).astype(np.float32)
    ka = np.full((2, S), -slope, np.float32)
    kb = np.full((2, S), slope, np.float32)
    jj = np.arange(128, dtype=np.float32)[:, None]
    d = np.arange(64, dtype=np.float32)[None, :]
    biasA = -slope * 128.0 * d + slope * jj - ATT_SHIFT
    biasB = -slope * 128.0 * d - slope * jj - ATT_SHIFT
    iq = np.arange(512, dtype=np.float32)[None, None, :]
    off = np.arange(4, dtype=np.float32)[None, :, None]
    biasD = -slope * np.abs(iq - (128.0 * off + jj[:, :, None])) - ATT_SHIFT
    bf = ml_dtypes.bfloat16
    return {
        "qaug": qaug.astype(bf), "kaug_a": ka.astype(bf), "kaug_b": kb.astype(bf),
        "biasA": biasA.astype(np.float32), "biasB": biasB.astype(np.float32),
        "biasD": biasD.astype(np.float32),
        "ones64": np.full((64, 64), 1.0 / 64, np.float32),
    }


def attn_dram_inputs(nc, S):
    d = {}
    d["qT"] = nc.dram_tensor("qT", [2, 64, S], F32, kind="ExternalInput").ap()
    d["kT"] = nc.dram_tensor("kT", [2, 64, S], F32, kind="ExternalInput").ap()
    d["vT"] = nc.dram_tensor("vT", [128, S], F32, kind="ExternalInput").ap()
    d["qaug"] = nc.dram_tensor("qaug", [2, S], BF16, kind="ExternalInput").ap()
    d["kaug_a"] = nc.dram_tensor("kaug_a", [2, S], BF16, kind="ExternalInput").ap()
    d["kaug_b"] = nc.dram_tensor("kaug_b", [2, S], BF16, kind="ExternalInput").ap()
    d["biasA"] = nc.dram_tensor("biasA", [128, 64], F32, kind="ExternalInput").ap()
    d["biasB"] = nc.dram_tensor("biasB", [128, 64], F32, kind="ExternalInput").ap()
    d["biasD"] = nc.dram_tensor("biasD", [128, 4, 512], F32, kind="ExternalInput").ap()
    d["ones64"] = nc.dram_tensor("ones64", [64, 64], F32, kind="ExternalInput").ap()
    d["q_gain"] = nc.dram_tensor("q_gain", [64], F32, kind="ExternalInput").ap()
    d["k_gain"] = nc.dram_tensor("k_gain", [64], F32, kind="ExternalInput").ap()
    d["lamv"] = nc.dram_tensor("lamv", [4, 64], F32, kind="ExternalInput").ap()
    d["subg"] = nc.dram_tensor("subg", [128], F32, kind="ExternalInput").ap()
    d["laminit"] = nc.dram_tensor("laminit", [128, 2], F32, kind="ExternalInput").ap()
    return d


def phase_attn(P, ident_f, identf_b, A, ob_out_d, S, tag):
    NJT = S // 128
    NI = S // 512
    banks = P.banks
    sc_rot = Rot(banks[0:3])
    acc = banks[3:7]
    misc = banks[7]
    with ExitStack() as es2:
        Qa = P.sb("Qa" + tag, [66, S], BF16, es2)
        Ka = P.sb("Ka" + tag, [66, S], BF16, es2)
        Kb = P.sb("Kb" + tag, [66, S], BF16, es2)
        Qa_b, Ka_b, Kb_b = Buf("Qa"), Buf("Ka"), Buf("Kb")
        V = P.sb("V" + tag, [128, NJT, 129], BF16, es2)
        V_b = Buf("V")
        o0 = P.sb("o0" + tag, [128, NJT, 129], F32, es2)
        o0_b = Buf("o0")
        bA = P.sb("bA" + tag, [128, 64], F32, es2)
        bB = P.sb("bB" + tag, [128, 64], F32, es2)
        bD = P.sb("bD" + tag, [128, 4, 512], F32, es2)
        ones64 = P.sb("ones64" + tag, [64, 64], F32, es2)
        cst_b = Buf("cst")
        gq = P.sb("gq" + tag, [64, 2], F32, es2)
        gk = P.sb("gk" + tag, [64, 1], F32, es2)
        lamv = P.sb("lamv" + tag, [128, 4, 64], F32, es2)
        lamw = P.sb("lamw" + tag, [128, 8], F32, es2)
        subg = P.sb("subgc" + tag, [128, 2], F32, es2)
        li = P.sb("laminit_sb" + tag, [128, 2], F32, es2)
        par_b = Buf("par")
        ld = make_rot(P, "sb", "ald" + tag, [128, 512], F32, 3, es2)
        sq = make_rot(P, "sb", "asq" + tag, [64, 512], F32, 2, es2)
        rs = make_rot(P, "sb", "ars" + tag, [64, 512], F32, 2, es2)
        eT = make_rot(P, "sb", "eT" + tag, [128, 512], BF16, 4, es2)
        dtmp = make_rot(P, "sb", "dtmp" + tag, [128, 512], F32, 2, es2)
        osm = make_rot(P, "sb", "osm" + tag, [128, 136], F32, 3, es2)
        ost = make_rot(P, "sb", "ost" + tag, [128, 8], F32, 3, es2)
        ostage = make_rot(P, "sb", "ostage" + tag, [128, 512], F32, 2, es2)

        P.dma("sp", bA[:], A["biasA"], (), (cst_b,))
        P.dma("sp", bB[:], A["biasB"], (), (cst_b,))
        P.dma("sp", bD[:], A["biasD"], (), (cst_b,))
        P.dma("sp", ones64[:], A["ones64"], (), (cst_b,))
        P.dma("sp", gq[:, 0:1], A["q_gain"].rearrange("(p o) -> p o", o=1), (), (par_b,))
        P.dma("sp", gk[:, 0:1], A["k_gain"].rearrange("(p o) -> p o", o=1), (), (par_b,))
        P.dma("sp", subg[:, 0:1], A["subg"].rearrange("(p o) -> p o", o=1), (), (par_b,))
        P.dma("sp", li[:], A["laminit"], (), (par_b,))
        P.dma("sp", lamv[:].rearrange("p a b -> p (a b)"),
              A["lamv"].rearrange("a b -> (a b)").partition_broadcast(128), (), (par_b,))
        P.op("dve", lambda e: e.tensor_scalar(gq[:, 1:2], gq[:, 0:1], 0.125, None, ALU.mult), (par_b,), (par_b,))
        P.op("dve", lambda e: e.tensor_tensor(lamv[:, 0, :], lamv[:, 0, :], lamv[:, 1, :], ALU.mult), (par_b,), (par_b,))
        P.op("dve", lambda e: e.tensor_tensor(lamv[:, 2, :], lamv[:, 2, :], lamv[:, 3, :], ALU.mult), (par_b,), (par_b,))
        P.op("dve", lambda e: e.reduce_sum(lamw[:, 0:1], lamv[:, 0, :], axis=AX.X), (par_b,), (par_b,))
        P.op("dve", lambda e: e.reduce_sum(lamw[:, 1:2], lamv[:, 2, :], axis=AX.X), (par_b,), (par_b,))
        P.op("act", lambda e: e.activation(lamw[:, 2:4], lamw[:, 0:2], AF.Exp), (par_b,), (par_b,))
        P.op("dve", lambda e: e.tensor_tensor(lamw[:, 4:5], lamw[:, 3:4], lamw[:, 2:3], ALU.subtract), (par_b,), (par_b,))
        P.op("dve", lambda e: e.tensor_scalar(lamw[:, 4:5], lamw[:, 4:5], li[:, 0:1], None, ALU.subtract), (par_b,), (par_b,))
        P.op("dve", lambda e: e.tensor_scalar(subg[:, 1:2], subg[:, 0:1], li[:, 1:2], None, ALU.mult), (par_b,), (par_b,))

        P.op("pool", lambda e: e.memset(V[:, :, 128:129], 1.0), (), (V_b,))
        for ch in range(NI):
            l, lb = ld.next()
            P.dma("sp", l[:], A["vT"][:, ch * 512:(ch + 1) * 512], (), (lb,))
            mt, mb = misc
            for q4 in range(4):
                transp(P, mt[:, q4 * 128:(q4 + 1) * 128], l[:, q4 * 128:(q4 + 1) * 128], ident_f[:],
                       (lb, identf_b), (mb,))
            P.op("act", lambda e, ch=ch, mt=mt: e.activation(
                V[:, ch * 4:(ch + 1) * 4, 0:128], mt[:].rearrange("p (a b) -> p a b", b=128), AF.Copy),
                (mb,), (V_b,))

        for m in range(2):
            P.dma("sp", Qa[64:66, :], A["qaug"], (), (Qa_b,))
            P.dma("sp", Ka[64:66, :], A["kaug_a"], (), (Ka_b,))
            P.dma("sp", Kb[64:66, :], A["kaug_b"], (), (Kb_b,))
            for which in range(2):
                src = A["qT"] if which == 0 else A["kT"]
                for ch in range(NI):
                    cs = slice(ch * 512, (ch + 1) * 512)
                    l, lb = ld.next()
                    P.dma("sp", l[0:64, :], src[m, :, cs], (), (lb,))
                    s_, sb_ = sq.next()
                    P.op("act", lambda e, s_=s_, l=l: e.activation(s_[:], l[0:64, :], AF.Square), (lb,), (sb_,))
                    sc, scb = sc_rot.next()
                    mm(P, sc[0:64, :], ones64[:], s_[:], True, True, (sb_, cst_b), (scb,))
                    r_, rb_ = rs.next()
                    P.op("dve", lambda e, r_=r_, sc=sc: e.tensor_scalar(r_[:], sc[0:64, :], NORM_EPS, None, ALU.add),
                         (scb,), (rb_,))
                    P.op("act", lambda e, r_=r_: e.activation(r_[:], r_[:], AF.Sqrt), (rb_,), (rb_,))
                    P.op("dve", lambda e, r_=r_: e.reciprocal(r_[:], r_[:]), (rb_,), (rb_,))
                    if which == 0:
                        P.op("dve", lambda e, l=l, r_=r_, cs=cs: e.scalar_tensor_tensor(
                            Qa[0:64, cs], l[0:64, :], gq[:, 1:2], r_[:], ALU.mult, ALU.mult),
                            (lb, rb_, par_b), (Qa_b,))
                    else:
                        P.op("dve", lambda e, l=l, r_=r_, cs=cs: e.scalar_tensor_tensor(
                            Ka[0:64, cs], l[0:64, :], gk[:, 0:1], r_[:], ALU.mult, ALU.mult),
                            (lb, rb_, par_b), (Ka_b,))
                        P.op("act", lambda e, cs=cs: e.activation(Kb[0:64, cs], Ka[0:64, cs], AF.Copy),
                             (Ka_b,), (Kb_b,))
            def score(I, J):
                qs = slice(I * 512, (I + 1) * 512)
                ks = slice(J * 128, (J + 1) * 128)
                sc, scb = sc_rot.next()
                et, etb = eT.next()
                dlt = 4 * I - J
                if dlt >= 1:
                    mm(P, sc[:], Ka[:, ks], Qa[:, qs], True, True, (Ka_b, Qa_b), (scb,))
                    P.op("act", lambda e, et=et, sc=sc, dlt=dlt: e.activation(
                        et[:], sc[:], AF.Exp, bias=bA[:, dlt:dlt + 1]), (scb, cst_b), (etb,))
                elif dlt <= -4:
                    mm(P, sc[:], Kb[:, ks], Qa[:, qs], True, True, (Kb_b, Qa_b), (scb,))
                    P.op("act", lambda e, et=et, sc=sc, dlt=dlt: e.activation(
                        et[:], sc[:], AF.Exp, bias=bB[:, -dlt:-dlt + 1]), (scb, cst_b), (etb,))
                else:
                    off = -dlt
                    mm(P, sc[:], Ka[0:64, ks], Qa[0:64, qs], True, True, (Ka_b, Qa_b), (scb,))
                    dt_, dtb = dtmp.next()
                    P.op("dve", lambda e, dt_=dt_, sc=sc, off=off: e.tensor_tensor(
                        dt_[:], sc[:], bD[:, off, :], ALU.add), (scb, cst_b), (dtb,))
                    P.op("act", lambda e, et=et, dt_=dt_: e.activation(et[:], dt_[:], AF.Exp), (dtb,), (etb,))
                return et, etb

            seq = [(I, J) for I in range(NI) for J in range(NJT)]
            LOOK = 2
            pend = {}
            for idx in range(min(LOOK, len(seq))):
                pend[idx] = score(*seq[idx])
            for idx, (I, J) in enumerate(seq):
                qs = slice(I * 512, (I + 1) * 512)
                if idx + LOOK < len(seq):
                    pend[idx + LOOK] = score(*seq[idx + LOOK])
                et, etb = pend.pop(idx)
                for qi in range(4):
                    at, ab = acc[qi]
                    mm(P, at[:, 0:129], et[:, qi * 128:(qi + 1) * 128], V[:, J, :], J == 0, J == NJT - 1,
                       (etb, V_b), (ab,))
                if J != NJT - 1:
                    continue
                for qi in range(4):
                    at, ab = acc[qi]
                    qt = I * 4 + qi
                    if m == 0:
                        P.op("dve", lambda e, at=at, qt=qt: e.tensor_copy(o0[:, qt, :], at[:, 0:129]),
                             (ab,), (o0_b,))
                        continue
                    st, stb = ost.next()
                    om, omb = osm.next()
                    P.op("dve", lambda e, st=st, qt=qt: e.reciprocal(st[:, 0:1], o0[:, qt, 128:129]), (o0_b,), (stb,))
                    P.op("dve", lambda e, st=st, at=at: e.reciprocal(st[:, 1:2], at[:, 128:129]), (ab,), (stb,))
                    P.op("dve", lambda e, st=st: e.tensor_tensor(st[:, 1:2], st[:, 1:2], lamw[:, 4:5], ALU.mult),
                         (stb, par_b), (stb,))
                    P.op("dve", lambda e, om=om, st=st, qt=qt: e.tensor_scalar(
                        om[:, 0:128], o0[:, qt, 0:128], st[:, 0:1], None, ALU.mult), (o0_b, stb), (omb,))
                    P.op("dve", lambda e, om=om, st=st, at=at: e.scalar_tensor_tensor(
                        om[:, 0:128], at[:, 0:128], st[:, 1:2], om[:, 0:128], ALU.mult, ALU.add),
                        (ab, stb, omb), (omb,))
                    s_, sb_ = sq.next()
                    P.op("act", lambda e, om=om, st=st, s_=s_: e.activation(
                        s_[:, 0:128].bitcast(F32) if False else dtmp.items[0][0][:, 0:128], om[:, 0:128], AF.Square,
                        accum_out=st[:, 2:3]), (omb,), (stb, dtmp.items[0][1]))
                    P.op("dve", lambda e, st=st: e.tensor_scalar(st[:, 3:4], st[:, 2:3], 1.0 / 128, NORM_EPS,
                                                                ALU.mult, ALU.add), (stb,), (stb,))
                    P.op("act", lambda e, st=st: e.activation(st[:, 4:5], st[:, 3:4], AF.Sqrt), (stb,), (stb,))
                    P.op("dve", lambda e, st=st: e.reciprocal(st[:, 5:6], st[:, 4:5]), (stb,), (stb,))
                    P.op("dve", lambda e, om=om, st=st: e.tensor_scalar(
                        om[:, 0:128], om[:, 0:128], st[:, 5:6], None, ALU.mult), (omb, stb), (omb,))
                    mt, mb = misc
                    transp(P, mt[:, qi * 128:(qi + 1) * 128], om[:, 0:128], ident_f[:], (omb, identf_b), (mb,))
                if m == 1:
                    mt, mb = misc
                    og, ogb = ostage.next()
                    P.op("dve", lambda e, og=og, mt=mt: e.tensor_scalar(og[:], mt[:], subg[:, 1:2], None, ALU.mult),
                         (mb, par_b), (ogb,))
                    P.dma("sp", ob_out_d[:, qs], og[:], (ogb,), (), is_output=True)
        P.barrier()


def build_k2a(S):
    nc = bass.Bass("TRN2", target_bir_lowering=False)
    A = attn_dram_inputs(nc, S)
    idf_d = nc.dram_tensor("ident_f", [128, 128], F32, kind="ExternalInput").ap()
    ob_d = nc.dram_tensor("obT", [128, S], F32, kind="ExternalOutput").ap()
    with ExitStack() as es:
        P = Prog(nc, es)
        ident_f = P.sb("sb_ident_f", [128, 128], F32, es)
        identf_b = Buf("identf")
        P.dma("sp", ident_f[:], idf_d, (), (identf_b,))
        phase_attn(P, ident_f, identf_b, A, ob_d, S, "t")
        P.finish()
        P.emit()
    return nc


RW_L = 512


def rwkv_consts():
    p = np.arange(128)[:, None]
    f = np.arange(128)[None, :]
    su = (f > p).astype(np.float32)
    iu = (f >= p).astype(np.float32)
    sl = (f < p).astype(np.float32)
    il = (f <= p).astype(np.float32)
    MK = np.stack([np.concatenate([-su, iu], 1), np.concatenate([-sl, il], 1)])
    BMK = np.stack([np.concatenate([su, iu], 1), np.concatenate([sl, il], 1)])
    NK = np.stack([-sl, -su])
    blk = np.zeros((128, 128), np.float32)
    blk[:64, :64] = 1.0
    blk[64:, 64:] = 1.0
    rm = np.ones((128, RW_L), np.float32)
    rm[:, ::128] = 0.0
    return {"MK": MK, "BMK": BMK, "NK": NK, "onesblk": blk, "resetm": rm}


def rwkv_dram_inputs(nc, S):
    d = {}
    def inp(n, shape):
        d[n] = nc.dram_tensor(n, shape, F32, kind="ExternalInput").ap()
    inp("rkv", [3, 128, S])
    inp("lor", [3, 128, S])
    inp("mu6", [6, 128])
    inp("w0", [2, 128]); inp("w2", [128, 128]); inp("a0", [2, 128]); inp("a2", [128, 128])
    inp("g2", [128, 128])
    inp("vec5", [5, 128])
    inp("MK", [2, 128, 256]); inp("BMK", [2, 128, 256]); inp("NK", [2, 128, 128])
    inp("onesblk", [128, 128]); inp("resetm", [128, RW_L])
    return d


def phase_rwkv(P, ident_f, identf_b, R, oa_out_d, S, tag):
    L = RW_L
    NCH = L // 128
    NSEG = S // L
    pb = Rot(P.banks[0:7])
    ybank = P.banks[7]
    with ExitStack() as es2:
        def sbt(name, shape):
            return P.sb(name + tag, shape, F32, es2)
        MK = sbt("MK", [128, 2, 256]); BMK = sbt("BMK", [128, 2, 256]); NK = sbt("NK", [128, 2, 128])
        onesblk = sbt("onesblk", [128, 128]); resetm = sbt("resetm", [128, L])
        cst_b = Buf("rcst")
        mu = sbt("mu", [128, 6]); hmu = sbt("hmu", [128, 6]); omm = sbt("omm", [128, 6])
        w0c = sbt("w0c", [128, 2]); a0c = sbt("a0c", [128, 2])
        w2s = sbt("w2s", [128, 128]); a2s = sbt("a2s", [128, 128]); g2s = sbt("g2s", [128, 128])
        vec = sbt("vec", [128, 8])
        par_b = Buf("rpar")
        raw = [(sbt("raw%d" % i, [128, L + 2]), Buf("raw%d" % i)) for i in range(6)]
        sh = [(sbt("sh%d" % i, [128, L]), Buf("sh%d" % i)) for i in range(6)]
        names = ["tmpA", "logw", "a_", "a_o", "kap", "kd", "bb", "G", "E1", "E3", "bt", "kt", "bh", "kh", "bon", "gate"]
        tl = {n: (sbt(n, [128, L]), Buf(n)) for n in names}
        KR = sbt("KR", [128, NCH, 256]); KR_b = Buf("KR")
        tot = sbt("tot", [128, NCH]); etot = sbt("etot", [128, NCH]); tot_b = Buf("tot")
        yacc = sbt("yacc", [128, S // 128, 128]); yacc_b = [Buf("yacc%d" % i) for i in range(S // 128)]
        wk128 = make_rot(P, "sb", "wk128" + tag, [128, 128], F32, 4, es2)
        Srot = [make_rot(P, "sb", "S%d" % hh + tag, [128, 64], F32, 3, es2) for hh in range(2)]
        gst = make_rot(P, "sb", "gst" + tag, [128, 16], F32, 3, es2)
        ostage = make_rot(P, "sb", "rostage" + tag, [128, 512], F32, 2, es2)

        def dve(fn, reads, writes):
            return P.op("dve", fn, reads, writes)

        def act(fn, reads, writes):
            return P.op("act", fn, reads, writes)

        P.dma("sp", MK[:], R["MK"].rearrange("d p f -> p d f"), (), (cst_b,))
        P.dma("sp", BMK[:], R["BMK"].rearrange("d p f -> p d f"), (), (cst_b,))
        P.dma("sp", NK[:], R["NK"].rearrange("d p f -> p d f"), (), (cst_b,))
        P.dma("sp", onesblk[:], R["onesblk"], (), (cst_b,))
        P.dma("sp", resetm[:], R["resetm"], (), (cst_b,))
        P.dma("sp", mu[:], R["mu6"].rearrange("i p -> p i"), (), (par_b,), allow_slow_non_contiguous=True)
        P.dma("sp", w0c[:], R["w0"].rearrange("i p -> p i"), (), (par_b,), allow_slow_non_contiguous=True)
        P.dma("sp", a0c[:], R["a0"].rearrange("i p -> p i"), (), (par_b,), allow_slow_non_contiguous=True)
        P.dma("sp", vec[:, 0:5], R["vec5"].rearrange("i p -> p i"), (), (par_b,), allow_slow_non_contiguous=True)
        P.dma("sp", w2s[:], R["w2"], (), (par_b,))
        P.dma("sp", a2s[:], R["a2"], (), (par_b,))
        P.dma("sp", g2s[:], R["g2"], (), (par_b,))
        dve(lambda e: e.tensor_scalar(hmu[:], mu[:], 0.5, None, ALU.mult), (par_b,), (par_b,))
        dve(lambda e: e.tensor_scalar(omm[:], mu[:], -1.0, 1.0, ALU.mult, ALU.add), (par_b,), (par_b,))
        dve(lambda e: e.tensor_scalar(vec[:, 5:6], vec[:, 1:2], -1.0, 1.0, ALU.mult, ALU.add), (par_b,), (par_b,))
        dve(lambda e: e.tensor_scalar(vec[:, 6:7], vec[:, 1:2], -2.0, 2.0, ALU.mult, ALU.add), (par_b,), (par_b,))

        def lora_sig(dst, src_i, wsb, biascol, d):
            ds_ = slice(d * 64, (d + 1) * 64)
            st, stb = sh[src_i]
            rhs_t, rhs_b = st, stb
            if src_i == 3:
                tt_, ttb = tl["tmpA"]
                act(lambda e: e.activation(tt_[ds_, :], st[ds_, :], AF.Tanh), (stb,), (ttb,))
                rhs_t, rhs_b = tt_, ttb
            bk, bkb = pb.next()
            mm(P, bk[:, 0:L], wsb[ds_, :], rhs_t[ds_, :], True, True, (par_b, rhs_b), (bkb,))
            act(lambda e: e.activation(dst[0][:], bk[:, 0:L], AF.Sigmoid, bias=biascol[:, d:d + 1]),
                (bkb, par_b), (dst[1],))

        def prep(seg, d, final):
            t0 = seg * L
            use = [0, 1, 2, 3, 4] + ([5] if final else [])
            for i in use:
                src = R["rkv"][i] if i < 3 else R["lor"][i - 3]
                rt, rb = raw[i]
                lo = max(t0 - 1, 0)
                hi = min(t0 + L + 1, S)
                P.dma("sp", rt[:, lo - (t0 - 1):hi - (t0 - 1)], src[:, lo:hi], (), (rb,))
                if t0 == 0:
                    P.op("pool", lambda e, rt=rt: e.memset(rt[:, 0:1], 0.0), (), (rb,))
                if t0 + L == S:
                    P.op("pool", lambda e, rt=rt: e.memset(rt[:, L + 1:L + 2], 0.0), (), (rb,))
                pA, pAb = tl["E1"]
                pB, pBb = tl["E3"]
                st, stb = sh[i]
                P.op("pool", lambda e, rt=rt: e.tensor_tensor(pA[:], rt[:, 0:L], rt[:, 2:L + 2], ALU.add), (rb,), (pAb,))
                P.op("pool", lambda e, i=i: e.tensor_scalar(pA[:], pA[:], hmu[:, i:i + 1], 0.0, ALU.mult, ALU.add),
                     (pAb, par_b), (pAb,))
                P.op("pool", lambda e, rt=rt, i=i: e.tensor_scalar(pB[:], rt[:, 1:L + 1], omm[:, i:i + 1], 0.0, ALU.mult, ALU.add),
                     (rb, par_b), (pBb,))
                P.op("pool", lambda e, st=st: e.tensor_tensor(st[:], pA[:], pB[:], ALU.add), (pAb, pBb), (stb,))
            kp, kpb = tl["kap"]
            tA, tAb = tl["tmpA"]
            ksh, kshb = sh[1]
            dve(lambda e: e.tensor_scalar(kp[:], ksh[:], vec[:, 0:1], None, ALU.mult), (kshb, par_b), (kpb,))
            act(lambda e: e.activation(tA[:], kp[:], AF.Square), (kpb,), (tAb,))
            bk, bkb = pb.next()
            mm(P, bk[:, 0:L], onesblk[:], tA[:], True, True, (cst_b, tAb), (bkb,))
            act(lambda e, bk=bk: e.activation(tA[:], bk[:, 0:L], AF.Sqrt), (bkb,), (tAb,))
            dve(lambda e: e.tensor_scalar(tA[:], tA[:], 1e-12, None, ALU.max), (tAb,), (tAb,))
            dve(lambda e: e.reciprocal(tA[:], tA[:]), (tAb,), (tAb,))
            dve(lambda e: e.tensor_tensor(kp[:], kp[:], tA[:], ALU.mult), (kpb, tAb), (kpb,))
            lw, lwb = tl["logw"]
            lora_sig(tl["logw"], 3, w2s, w0c, d)
            dve(lambda e: e.tensor_scalar(lw[:], lw[:], -DECAY_SCALE, None, ALU.mult), (lwb,), (lwb,))
            lora_sig(tl["a_"], 4, a2s, a0c, d)
            av, avb = tl["a_"]
            kdv, kdb = tl["kd"]
            dve(lambda e: e.tensor_scalar(tA[:], av[:], vec[:, 1:2], vec[:, 5:6], ALU.mult, ALU.add),
                (avb, par_b), (tAb,))
            dve(lambda e: e.tensor_tensor(kdv[:], ksh[:], tA[:], ALU.mult), (kshb, tAb), (kdb,))
            bbv, bbb = tl["bb"]
            dve(lambda e: e.tensor_tensor(bbv[:], kp[:], av[:], ALU.mult), (kpb, avb), (bbb,))
            G, Gb = tl["G"]
            dve(lambda e: e.tensor_tensor_scan(G[:], resetm[:], lw[:], 0.0, ALU.mult, ALU.add), (cst_b, lwb), (Gb,))
            G3 = G[:].rearrange("p (c t) -> p c t", t=128)
            dve(lambda e: e.tensor_copy(tot[:], G3[:, :, 127]), (Gb,), (tot_b,))
            act(lambda e: e.activation(etot[:], tot[:], AF.Exp), (tot_b,), (tot_b,))
            if d == 1:
                tb3 = tot[:].unsqueeze(2).to_broadcast([128, NCH, 128])
                dve(lambda e: e.tensor_tensor(G3, G3, tb3, ALU.subtract), (Gb, tot_b), (Gb,))
                dve(lambda e: e.scalar_tensor_tensor(G[:], G[:], -1.0, lw[:], ALU.mult, ALU.add), (Gb, lwb), (Gb,))
            E1, E1b = tl["E1"]
            E3, E3b = tl["E3"]
            rsh, rshb = sh[0]
            KR3k = KR[:, :, 0:128]
            KR3r = KR[:, :, 128:256]
            act(lambda e: e.activation(E1[:], G[:], AF.Exp), (Gb,), (E1b,))
            dve(lambda e: e.tensor_tensor(KR3r.bitcast(F32R), rsh[:].rearrange("p (c t) -> p c t", t=128),
                                          E1[:].rearrange("p (c t) -> p c t", t=128), ALU.mult),
                (rshb, E1b), (KR_b,))
            dve(lambda e: e.tensor_tensor(tA[:], G[:], lw[:], ALU.subtract), (Gb, lwb), (tAb,))
            act(lambda e: e.activation(E1[:], tA[:], AF.Exp), (tAb,), (E1b,))
            dve(lambda e: e.tensor_tensor(KR3k.bitcast(F32R), kp[:].rearrange("p (c t) -> p c t", t=128),
                                          E1[:].rearrange("p (c t) -> p c t", t=128), ALU.mult),
                (kpb, E1b), (KR_b,))
            act(lambda e: e.activation(E3[:], G[:], AF.Exp, scale=-1.0), (Gb,), (E3b,))
            for nm, srcv in (("bt", tl["bb"]), ("kt", tl["kd"])):
                o_, ob_ = tl[nm]
                dve(lambda e, o_=o_, srcv=srcv: e.tensor_tensor(o_[:].bitcast(F32R), srcv[0][:], E3[:], ALU.mult),
                    (srcv[1], E3b), (ob_,))
            eb3 = etot[:].unsqueeze(2).to_broadcast([128, NCH, 128])
            E33 = E3[:].rearrange("p (c t) -> p c t", t=128)
            dve(lambda e: e.tensor_tensor(E33, E33, eb3, ALU.mult), (E3b, tot_b), (E3b,))
            for nm, srcv in (("bh", tl["bb"]), ("kh", tl["kd"])):
                o_, ob_ = tl[nm]
                dve(lambda e, o_=o_, srcv=srcv: e.tensor_tensor(o_[:], srcv[0][:], E3[:], ALU.mult),
                    (srcv[1], E3b), (ob_,))
            if final:
                lora_sig(tl["a_o"], 4, a2s, a0c, 0)
                ao, aob = tl["a_o"]
                dve(lambda e: e.tensor_tensor(tA[:], av[:], ao[:], ALU.add), (avb, aob), (tAb,))
                dve(lambda e: e.tensor_scalar(tA[:], tA[:], vec[:, 1:2], vec[:, 6:7], ALU.mult, ALU.add),
                    (tAb, par_b), (tAb,))
                dve(lambda e: e.tensor_tensor(tA[:], tA[:], ksh[:], ALU.mult), (tAb, kshb), (tAb,))
                dve(lambda e: e.tensor_tensor(tA[:], tA[:], rsh[:], ALU.mult), (tAb, rshb), (tAb,))
                dve(lambda e: e.tensor_scalar(tA[:], tA[:], vec[:, 2:3], None, ALU.mult), (tAb, par_b), (tAb,))
                bk, bkb = pb.next()
                mm(P, bk[:, 0:L], onesblk[:], tA[:], True, True, (cst_b, tAb), (bkb,))
                bon, bonb = tl["bon"]
                vsh, vshb = sh[2]
                dve(lambda e, bk=bk: e.tensor_tensor(bon[:], bk[:, 0:L], vsh[:], ALU.mult), (bkb, vshb), (bonb,))
                gsh, gshb = sh[5]
                act(lambda e: e.activation(tA[:], gsh[:], AF.Sigmoid), (gshb,), (tAb,))
                bk, bkb = pb.next()
                mm(P, bk[:, 0:L], g2s[:], tA[:], True, True, (par_b, tAb), (bkb,))
                gt, gtb = tl["gate"]
                act(lambda e, bk=bk: e.activation(gt[:], bk[:, 0:L], AF.Copy), (bkb,), (gtb,))

        NU = NCH * 2
        UT = []
        for ui in range(NU):
            t = {}
            for nm, shp in (("am1", [128, 256]), ("bm2", [128, 256]), ("QR0", [128, 256]), ("QR1", [128, 256]),
                            ("QT0", [128, 256]), ("QT1", [128, 256]),
                            ("u0", [128, 64]), ("Dsb", [128, 64]), ("dgt", [128, 64]), ("TT", [128, 64]),
                            ("RpT", [128, 128])):
                t[nm] = (sbt("%s_%d" % (nm, ui), shp), Buf("%s_%d" % (nm, ui)))
            UT.append(t)
        TOK = [(sbt("tok_%d" % c, [128, 512]), Buf("tok_%d" % c)) for c in range(NCH)]
        for ui in range(NU):
            for nm in ("QT0", "QT1"):
                tt_, ttb_ = UT[ui][nm]
                P.op("dve", lambda e, tt_=tt_: e.tensor_scalar(tt_[:, 128:256].bitcast(F32R), ident_f[:], 0.0, None, ALU.mult), (identf_b,), (ttb_,))

        def pre_segment(d):
            bt, btb = tl["bt"]; kt, ktb = tl["kt"]; bh, bhb = tl["bh"]; kh, khb = tl["kh"]
            vsh, vshb = sh[2]
            for c in range(NCH):
                cs = slice(c * 128, (c + 1) * 128)
                bk, bkb = pb.next()
                transp(P, bk[:, 0:128], KR[:, c, 0:128], ident_f[:], (KR_b, identf_b), (bkb,))
                transp(P, bk[:, 128:256], bh[:, cs], ident_f[:], (bhb, identf_b), (bkb,))
                transp(P, bk[:, 256:384], kh[:, cs], ident_f[:], (khb, identf_b), (bkb,))
                transp(P, bk[:, 384:512], vsh[:, cs], ident_f[:], (vshb, identf_b), (bkb,))
                tok, tokb = TOK[c]
                act(lambda e, tok=tok, bk=bk: e.activation(tok[:], bk[:], AF.Copy), (bkb,), (tokb,))
            st = []
            for c in range(NCH):
                cs = slice(c * 128, (c + 1) * 128)
                tok, tokb = TOK[c]
                for hh in range(2):
                    T_ = UT[c * 2 + hh]
                    hs = slice(hh * 64, (hh + 1) * 64)
                    hc = lambda base, hh=hh: slice(base + hh * 64, base + hh * 64 + 64)
                    b1, b1b = pb.next()
                    mm(P, b1[:, 0:256], bt[hs, cs], KR[hs, c, :], True, True, (btb, KR_b), (b1b,), f32r=USE_F32R)
                    am1, am1b = T_["am1"]
                    dve(lambda e, am1=am1, b1=b1: e.tensor_tensor(am1[:].bitcast(F32R), b1[:, 0:256], MK[:, d, :], ALU.mult),
                        (b1b, cst_b), (am1b,))
                    b2, b2b = pb.next()
                    mm(P, b2[:, 0:256], kt[hs, cs], KR[hs, c, :], True, True, (ktb, KR_b), (b2b,), f32r=USE_F32R)
                    bm2, bm2b = T_["bm2"]
                    dve(lambda e, bm2=bm2, b2=b2: e.tensor_tensor(bm2[:], b2[:, 0:256], BMK[:, d, :], ALU.mult),
                        (b2b, cst_b), (bm2b,))
                    b3, b3b = pb.next()
                    mm(P, b3[:, 0:128], KR[hs, c, 0:128], bt[hs, cs], True, True, (KR_b, btb), (b3b,))
                    qr, qrb = T_["QR0"]
                    dve(lambda e, qr=qr, b3=b3: e.tensor_tensor(qr[:, 0:128].bitcast(F32R), b3[:, 0:128], NK[:, d, :], ALU.mult),
                        (b3b, cst_b), (qrb,))
                    b4, b4b = pb.next()
                    mm(P, b4[:, 0:64], bm2[:, 0:128], tok[:, hc(384)], True, True, (bm2b, tokb), (b4b,))
                    act(lambda e, qr=qr, tok=tok, hc=hc: e.activation(qr[:, 128:192].bitcast(F32R), tok[:, hc(0)], AF.Copy), (tokb,), (qrb,))
                    act(lambda e, qr=qr, b4=b4: e.activation(qr[:, 192:256].bitcast(F32R), b4[:, 0:64], AF.Copy), (b4b,), (qrb,))
                    st.append(dict(c=c, hh=hh, hs=hs, hc=hc, T=T_, am1=(am1, am1b), bm2=(bm2, bm2b),
                                   QR=(qr, qrb), QT=(am1, am1b)))
            for j in range(7):
                for X in st:
                    T_ = X["T"]
                    qr, qrb = X["QR"]
                    qt, qtb = X["QT"]
                    nqr, nqrb = T_["QR%d" % ((j + 1) % 2)]
                    c1, c1b = pb.next()
                    mm(P, c1[:, 0:256], qt[:, 0:128], qr[:, 0:256], True, True, (qtb, qrb), (c1b,), f32r=USE_F32R)
                    if j < 6:
                        act(lambda e, nqr=nqr, c1=c1: e.activation(nqr[:, 0:128].bitcast(F32R), c1[:, 0:128], AF.Copy), (c1b,), (nqrb,))
                    dve(lambda e, nqr=nqr, qr=qr, c1=c1: e.tensor_tensor(nqr[:, 128:256].bitcast(F32R), qr[:, 128:256], c1[:, 128:256], ALU.add),
                        (qrb, c1b), (nqrb,))
                    if j < 6:
                        nqt, nqtb = T_["QT%d" % ((j + 1) % 2)]
                        c2, c2b = pb.next()
                        mm(P, c2[:, 0:256], qr[:, 0:128], qt[:, 0:256], True, True, (qrb, qtb), (c2b,), f32r=USE_F32R)
                        act(lambda e, nqt=nqt, c2=c2: e.activation(nqt[:, 0:128].bitcast(F32R), c2[:, 0:128], AF.Copy), (c2b,), (nqtb,))
                        X["QT"] = (nqt, nqtb)
                    X["QR"] = (nqr, nqrb)
            for X in st:
                qr, qrb = X["QR"]
                X["rh"] = (RhView(qr), qrb)
            for X in st:
                T_ = X["T"]
                c = X["c"]; hh = X["hh"]; hs = X["hs"]; hc = X["hc"]
                tok, tokb = TOK[c]
                rh, rhb = X["rh"]
                am1, am1b = X["am1"]
                u0, u0b = T_["u0"]
                act(lambda e, u0=u0, rh=rh: e.activation(u0[:], rh[:, 64:128], AF.Copy, scale=-1.0), (rhb,), (u0b,))
                b8, b8b = pb.next()
                mm(P, b8[hs, 0:64], rh[:, 0:64], tok[:, hc(128)], True, True, (rhb, tokb), (b8b,))
                dgt, dgtb = T_["dgt"]
                dve(lambda e, dgt=dgt, hs=hs, hh=hh, c=c: e.tensor_scalar(
                    dgt[hs, 0:64], ident_f[hs, hh * 64:(hh + 1) * 64], etot[hs, c:c + 1], None, ALU.mult),
                    (identf_b, tot_b), (dgtb,))
                TT, TTb = T_["TT"]
                dve(lambda e, TT=TT, b8=b8, dgt=dgt, hs=hs: e.scalar_tensor_tensor(
                    TT[hs, 0:64], b8[hs, 0:64], -1.0, dgt[hs, 0:64], ALU.mult, ALU.add), (b8b, dgtb), (TTb,))
                b9, b9b = pb.next()
                mm(P, b9[hs, 0:64], tok[:, hc(128)], u0[:], True, False, (tokb, u0b), (b9b,))
                mm(P, b9[hs, 0:64], tok[:, hc(256)], tok[:, hc(384)], False, True, (tokb,), (b9b,))
                Dsb, Dsbb = T_["Dsb"]
                act(lambda e, Dsb=Dsb, b9=b9, hs=hs: e.activation(Dsb[hs, :], b9[hs, 0:64], AF.Copy), (b9b,), (Dsbb,))
                b10, b10b = pb.next()
                mm(P, b10[hs, 0:128], rh[:, 0:64], am1[:, 128:256], True, True, (rhb, am1b), (b10b,))
                RpT, RpTb = T_["RpT"]
                dve(lambda e, RpT=RpT, b10=b10, hs=hs, c=c: e.tensor_tensor(
                    RpT[hs, :], KR[hs, c, 128:256], b10[hs, 0:128], ALU.subtract), (KR_b, b10b), (RpTb,))
            return st

        def chunk(seg, c, d, final, Scur, og, st):
            gc = seg * NCH + c
            cs = slice(c * 128, (c + 1) * 128)
            tok, tokb = TOK[c]
            ybk, ybkb = ybank
            newS = []
            for hh in range(2):
                X = st[c * 2 + hh]
                T_ = X["T"]
                hs, hc = X["hs"], X["hc"]
                am1, am1b = X["am1"]
                bm2, bm2b = X["bm2"]
                u0, u0b = T_["u0"]
                TT, TTb = T_["TT"]
                Dsb, Dsbb = T_["Dsb"]
                RpT, RpTb = T_["RpT"]
                S0, S0b = Scur[hh]
                b11, b11b = pb.next()
                mm(P, b11[hs, 0:64], TT[hs, 0:64], S0[hs, :], True, True, (TTb, S0b), (b11b,))
                S1, S1b = Srot[hh].next()
                dve(lambda e, S1=S1, b11=b11, Dsb=Dsb, hs=hs: e.tensor_tensor(
                    S1[hs, :], b11[hs, 0:64], Dsb[hs, :], ALU.add), (b11b, Dsbb), (S1b,))
                newS.append((S1, S1b))
                yo = ybk[:, hh * 64:(hh + 1) * 64]
                mm(P, yo, RpT[hs, :], S0[hs, :], True, False, (RpTb, S0b), (ybkb,))
                mm(P, yo, am1[:, 128:256], u0[:], False, False, (am1b, u0b), (ybkb,))
                mm(P, yo, bm2[:, 128:256], tok[:, hc(384)], False, True, (bm2b, tokb), (ybkb,))
            if not final:
                act(lambda e: e.activation(yacc[:, gc, :], ybk[:, 0:128], AF.Copy), (ybkb,), (yacc_b[gc],))
            else:
                yt, ytb = wk128.next()
                dve(lambda e, yt=yt: e.tensor_tensor(yt[:], yacc[:, gc, :], ybk[:, 0:128], ALU.add),
                    (yacc_b[gc], ybkb), (ytb,))
                g_, gb_ = gst.next()
                for hh in range(2):
                    hcol = slice(hh * 64, (hh + 1) * 64)
                    o6 = hh * 8
                    dve(lambda e, g_=g_, yt=yt, hcol=hcol, o6=o6: e.bn_stats(g_[:, o6:o6 + 6], yt[:, hcol]), (ytb,), (gb_,))
                    dve(lambda e, g_=g_, o6=o6: e.bn_aggr(g_[:, o6 + 6:o6 + 8], g_[:, o6:o6 + 6]), (gb_,), (gb_,))
                    dve(lambda e, g_=g_, o6=o6: e.tensor_scalar(g_[:, o6 + 7:o6 + 8], g_[:, o6 + 7:o6 + 8], GN_EPS, None, ALU.add),
                        (gb_,), (gb_,))
                    act(lambda e, g_=g_, o6=o6: e.activation(g_[:, o6 + 7:o6 + 8], g_[:, o6 + 7:o6 + 8], AF.Sqrt), (gb_,), (gb_,))
                    dve(lambda e, g_=g_, o6=o6: e.reciprocal(g_[:, o6 + 7:o6 + 8], g_[:, o6 + 7:o6 + 8]), (gb_,), (gb_,))
                    dve(lambda e, g_=g_, yt=yt, hcol=hcol, o6=o6: e.tensor_scalar(
                        yt[:, hcol], yt[:, hcol], g_[:, o6 + 6:o6 + 7], g_[:, o6 + 7:o6 + 8], ALU.subtract, ALU.mult),
                        (ytb, gb_), (ytb,))
                tb_, tbb = pb.next()
                transp(P, tb_[:, 0:128], yt[:], ident_f[:], (ytb, identf_b), (tbb,))
                o1, o1b = wk128.next()
                dve(lambda e, o1=o1, tb_=tb_: e.tensor_scalar(o1[:], tb_[:, 0:128], vec[:, 3:4], vec[:, 4:5], ALU.mult, ALU.add),
                    (tbb, par_b), (o1b,))
                bon, bonb = tl["bon"]
                gt, gtb = tl["gate"]
                import os
                DBG = int(os.environ.get("RW_DBG", "0"))
                if DBG == 1:
                    dve(lambda e: e.tensor_copy(og[0][:, cs], gt[:, cs]), (gtb,), (og[1],))
                elif DBG == 2:
                    dve(lambda e: e.tensor_copy(og[0][:, cs], bon[:, cs]), (bonb,), (og[1],))
                elif DBG in (6, 7, 8, 9):
                    srcd = {6: sh[1], 7: tl["a_"], 8: tl["a_o"], 9: sh[2]}[DBG]
                    dve(lambda e, srcd=srcd: e.tensor_copy(og[0][:, cs], srcd[0][:, cs]), (srcd[1],), (og[1],))
                elif DBG == 20:
                    srcd = [tl["G"], tl["E3"], tl["bt"], tl["logw"]][c]
                    dve(lambda e, srcd=srcd: e.tensor_copy(og[0][:, cs], srcd[0][:, cs]), (srcd[1],), (og[1],))
                elif DBG == 21:
                    srcd = [tl["kap"], tl["bb"], tl["kd"], tl["E1"]][c]
                    dve(lambda e, srcd=srcd: e.tensor_copy(og[0][:, cs], srcd[0][:, cs]), (srcd[1],), (og[1],))
                elif DBG == 3:
                    dve(lambda e, o1=o1: e.tensor_copy(og[0][:, cs], o1[:]), (o1b,), (og[1],))
                elif DBG == 4:
                    dve(lambda e: e.tensor_copy(og[0][:, cs], yacc[:, gc, :]), (yacc_b[gc],), (og[1],))
                elif DBG == 5:
                    dve(lambda e: e.tensor_copy(og[0][:, cs], ybk[:, 0:128]), (ybkb,), (og[1],))
                else:
                    dve(lambda e, o1=o1: e.tensor_tensor(o1[:], o1[:], bon[:, cs], ALU.add), (o1b, bonb), (o1b,))
                    dve(lambda e, o1=o1: e.tensor_tensor(og[0][:, cs], o1[:], gt[:, cs], ALU.mult), (o1b, gtb), (og[1],))
            return newS

        for d in range(2):
            final = d == 1
            Scur = []
            for hh in range(2):
                S0, S0b = Srot[hh].next()
                P.op("pool", lambda e, S0=S0: e.memset(S0[:], 0.0), (), (S0b,))
                Scur.append((S0, S0b))
            segs = range(NSEG) if d == 0 else range(NSEG - 1, -1, -1)
            for seg in segs:
                prep(seg, d, final)
                og = ostage.next() if final else None
                chs = range(NCH) if d == 0 else range(NCH - 1, -1, -1)
                st = pre_segment(d)
                for c in chs:
                    Scur = chunk(seg, c, d, final, Scur, og, st)
                if final:
                    P.dma("sp", oa_out_d[:, seg * L:(seg + 1) * L], og[0][:], (og[1],), (), is_output=True)
        P.barrier()


def build_k2r(S):
    nc = bass.Bass("TRN2", target_bir_lowering=False)
    R = rwkv_dram_inputs(nc, S)
    idf_d = nc.dram_tensor("ident_f", [128, 128], F32, kind="ExternalInput").ap()
    oa_d = nc.dram_tensor("oaT", [128, S], F32, kind="ExternalOutput").ap()
    with ExitStack() as es:
        P = Prog(nc, es)
        ident_f = P.sb("sb_ident_f", [128, 128], F32, es)
        identf_b = Buf("identf")
        P.dma("sp", ident_f[:], idf_d, (), (identf_b,))
        phase_rwkv(P, ident_f, identf_b, R, oa_d, S, "r")
        P.finish()
        P.emit()
    return nc


def consts_common():
    return {
        "ident_bf": np.eye(128, dtype=np.float32).astype(ml_dtypes.bfloat16),
        "ident_f": np.eye(128, dtype=np.float32),
    }


def build_k2(S):
    nc = bass.Bass("TRN2", target_bir_lowering=False)
    R = rwkv_dram_inputs(nc, S)
    A = attn_dram_inputs(nc, S)
    idf_d = nc.dram_tensor("ident_f", [128, 128], F32, kind="ExternalInput").ap()
    oa_d = nc.dram_tensor("oaT", [128, S], F32, kind="ExternalOutput").ap()
    ob_d = nc.dram_tensor("obT", [128, S], F32, kind="ExternalOutput").ap()
    with ExitStack() as es:
        P = Prog(nc, es)
        ident_f = P.sb("sb_ident_f", [128, 128], F32, es)
        identf_b = Buf("identf")
        P.dma("sp", ident_f[:], idf_d, (), (identf_b,))
        phase_rwkv(P, ident_f, identf_b, R, oa_d, S, "r")
        phase_attn(P, ident_f, identf_b, A, ob_d, S, "t")
        P.finish()
        P.emit()
    return nc


def kernel(**inputs):
    f = lambda a: np.ascontiguousarray(np.asarray(a, dtype=np.float32))
    inp = {k: np.asarray(v) for k, v in inputs.items()}
    x = f(inp["x"])
    NT = SEQ // 4
    cores = list(range(NCORES))
    cc = consts_common()
    rc = rwkv_consts()
    ac = [attn_consts(g, SEQ) for g in range(4)]
    ident_f = np.eye(128, dtype=np.float32)

    def k1_weights(l):
        return dict(g1=f(inp["norm_ffn1"][l]), w1=f(inp["ffn1_in"][l]), w2=f(inp["ffn1_out"][l]),
                    gm=f(inp["norm_mix"][l]), win=f(inp["w_in"][l]))

    xs = [f(x[c // 4, (c % 4) * NT:(c % 4 + 1) * NT]) for c in cores]
    nc1 = build_k1(NT)
    w = k1_weights(0)
    res = run_bass_kernel_spmd(nc1, [dict(x=xs[c], **w, **cc) for c in cores], core_ids=cores)
    x1 = [res.results[c]["x1"] for c in cores]
    pt = [res.results[c]["pt"] for c in cores]
    nc2 = build_k2(SEQ)
    nc3n = build_k3(NT, True)
    nc3l = build_k3(NT, False)
    for l in range(DEPTH):
        lam_init = 0.8 - 0.6 * math.exp(-0.3 * l)
        li = np.tile(np.array([[lam_init, 1.0 - lam_init]], np.float32), (128, 1))
        maps = []
        mu = inp["rwkv_mu"][l]
        for c in cores:
            b, g = c // 4, c % 4
            cols = slice(g * 128, (g + 1) * 128)
            PTb = np.concatenate([pt[b * 4 + j][0:C_MIX] for j in range(4)], axis=1)
            rkv = np.stack([PTb[0:512][cols], PTb[512:1024][cols], PTb[1024:1536][cols]])
            lor = PTb[1536:1920].reshape(3, 128, SEQ)
            mu6 = np.stack([mu[0:512][cols], mu[512:1024][cols], mu[1024:1536][cols],
                            mu[1536:1664], mu[1664:1792], mu[1792:1920]])
            pa = PTb[C_RWKV:]
            m = dict(
                rkv=f(rkv), lor=f(lor), mu6=f(mu6),
                w0=f(inp["decay_w0"][l][:, cols]), w2=f(inp["decay_w2"][l][:, :, cols].reshape(128, 128)),
                a0=f(inp["iclr_a0"][l][:, cols]), a2=f(inp["iclr_a2"][l][:, :, cols].reshape(128, 128)),
                g2=f(inp["gate_g2"][l][:, cols]),
                vec5=f(np.stack([inp["k_k"][l][cols], inp["k_a"][l][cols], inp["r_k"][l].reshape(512)[cols],
                                 inp["ln_x_g"][l][cols], inp["ln_x_b"][l][cols]])),
                qT=f(pa[0:512][cols].reshape(2, 64, SEQ)), kT=f(pa[512:1024][cols].reshape(2, 64, SEQ)),
                vT=f(pa[1024:1536][cols]),
                q_gain=f(inp["q_gain"][l]), k_gain=f(inp["k_gain"][l]), lamv=f(inp["diff_lambda"][l]),
                subg=f(inp["subln_g"][l]), laminit=li, ident_f=ident_f, **rc, **ac[g])
            maps.append(m)
        res = run_bass_kernel_spmd(nc2, maps, core_ids=cores)
        oaT = [res.results[c]["oaT"] for c in cores]
        obT = [res.results[c]["obT"] for c in cores]
        maps = []
        last = l == DEPTH - 1
        for c in cores:
            b, j = c // 4, c % 4
            ts = slice(j * NT, (j + 1) * NT)
            oa = np.concatenate([oaT[b * 4 + g][:, ts] for g in range(4)], axis=0)
            ob = np.concatenate([obT[b * 4 + g][:, ts] for g in range(4)], axis=0)
            m = dict(x=f(x1[c]), sg=f(pt[c][C_MIX:]), oa=f(oa), ob=f(ob),
                     wb=f(inp["w_branch"][l]), wout=f(inp["w_out"][l]), g2n=f(inp["norm_ffn2"][l]),
                     w1b=f(inp["ffn2_in"][l]), w2b=f(inp["ffn2_out"][l]), **cc)
            if not last:
                m.update(k1_weights(l + 1))
            maps.append(m)
        res = run_bass_kernel_spmd(nc3l if last else nc3n, maps, core_ids=cores)
        x1 = [res.results[c]["x1"] for c in cores]
        if not last:
            pt = [res.results[c]["pt"] for c in cores]
    out = np.zeros((BATCH, SEQ, D), np.float32)
    for c in cores:
        out[c // 4, (c % 4) * NT:(c % 4 + 1) * NT] = x1[c]
    return out
```

```python
import math
from contextlib import ExitStack
import numpy as np
import ml_dtypes
import concourse.bass as bass
import concourse.mybir as mybir
from concourse.bass_utils import run_bass_kernel_spmd

F32 = mybir.dt.float32
BF16 = mybir.dt.bfloat16
AF = mybir.ActivationFunctionType
ALU = mybir.AluOpType
AX = mybir.AxisListType

D = 1024
DFF = 2816
NJ = DFF // 128
DEPTH = 4
SEQ = 8192
BATCH = 2
C_RWKV = 1920
C_ATTN = 1536
C_MIX = C_RWKV + C_ATTN
C_IN = 5504
NCORES = 8
NORM_EPS = 1e-6
GN_EPS = 64e-5
DECAY_SCALE = math.exp(-0.5)


class Buf:
    __slots__ = ("w", "r", "name")

    def __init__(self, name=""):
        self.w = None
        self.r = []
        self.name = name


ENGS = ("pe", "act", "dve", "pool", "sp")
N_DMA_SEMS = 20


class Prog:
    def __init__(self, nc, es):
        self.nc = nc
        self.es = es
        self.streams = {e: [] for e in ENGS}
        self.count = {e: 0 for e in ENGS}
        self.known = {e: {} for e in ENGS}
        self.sems = {}
        for e in ENGS:
            self.sems[e] = es.enter_context(nc.semaphore("s_" + e))
        self.dma_sems = []
        self.dma_tot = []
        for i in range(N_DMA_SEMS):
            nm = "d%d" % i
            self.sems[nm] = es.enter_context(nc.semaphore("s_" + nm))
            self.dma_sems.append(nm)
            self.dma_tot.append(0)
        self.dma_rr = 0
        self.out_events = []
        self.banks = []
        for i in range(8):
            t = es.enter_context(nc.psum_tensor("bank%d" % i, [128, 512], F32))
            self.banks.append((t, Buf("bank%d" % i)))

    def sb(self, name, shape, dtype, es=None):
        t = (es or self.es).enter_context(self.nc.sbuf_tensor(name, list(shape), dtype))
        return t

    def ps(self, name, shape, dtype=F32, es=None):
        t = (es or self.es).enter_context(self.nc.psum_tensor(name, list(shape), dtype))
        return t

    def _deps(self, eng, reads, writes):
        deps = {}

        def add(ev):
            if ev is None:
                return
            s, v = ev
            if eng == "pe" and s == "pe":
                return
            if deps.get(s, 0) < v:
                deps[s] = v

        for b in reads:
            add(b.w)
        for b in writes:
            add(b.w)
            for ev in b.r:
                add(ev)
        waits = []
        kn = self.known[eng]
        for s, v in deps.items():
            if kn.get(s, 0) < v:
                kn[s] = v
                waits.append((s, v))
        return waits

    def _mark(self, ev, reads, writes):
        for b in writes:
            b.w = ev
            b.r = []
        for b in reads:
            if len(b.r) > 12:
                best = {}
                for s, v in b.r:
                    if best.get(s, 0) < v:
                        best[s] = v
                b.r = list(best.items())
            b.r.append(ev)

    def op(self, eng, fn, reads=(), writes=()):
        waits = self._deps(eng, reads, writes)
        self.count[eng] += 1
        ev = (eng, self.count[eng])
        self.streams[eng].append((waits, fn, (eng, 1)))
        self._mark(ev, reads, writes)
        return ev

    def dma(self, q, out_ap, in_ap, reads=(), writes=(), is_output=False, **kw):
        i = self.dma_rr
        self.dma_rr = (self.dma_rr + 1) % N_DMA_SEMS
        nm = self.dma_sems[i]
        waits = self._deps(q, reads, writes)
        prev = self.dma_tot[i]
        if prev > 0 and self.known[q].get(nm, 0) < prev:
            self.known[q][nm] = prev
            waits.append((nm, prev))
        self.dma_tot[i] = prev + 16
        ev = (nm, prev + 16)

        def fn(e, out_ap=out_ap, in_ap=in_ap, kw=kw):
            return e.dma_start(out=out_ap, in_=in_ap, **kw)

        self.streams[q].append((waits, fn, (nm, 16)))
        self._mark(ev, reads, writes)
        if is_output:
            self.out_events.append(ev)
        return ev

    def collective(self, kind, in_ap, out_ap, groups, reads=(), writes=()):
        i = self.dma_rr
        self.dma_rr = (self.dma_rr + 1) % N_DMA_SEMS
        nm = self.dma_sems[i]
        waits = self._deps("pool", reads, writes)
        prev = self.dma_tot[i]
        if prev > 0 and self.known["pool"].get(nm, 0) < prev:
            self.known["pool"][nm] = prev
            waits.append((nm, prev))
        self.dma_tot[i] = prev + 16
        ev = (nm, prev + 16)

        def fn(e):
            return e.collective_compute(kind, ALU.bypass, replica_groups=groups, ins=[in_ap], outs=[out_ap])

        self.streams["pool"].append((waits, fn, (nm, 16)))
        self._mark(ev, reads, writes)
        return ev

    def barrier(self):
        tot = {e: self.count[e] for e in ENGS}
        for i, nm in enumerate(self.dma_sems):
            tot[nm] = self.dma_tot[i]
        for e in ENGS:
            waits = []
            for sname, v in tot.items():
                if sname == e and e == "pe":
                    continue
                if v > 0 and self.known[e].get(sname, 0) < v:
                    self.known[e][sname] = v
                    waits.append((sname, v))
            if waits:
                self.streams[e].append((waits, None, None))

    def finish(self):
        best = {}
        for s, v in self.out_events:
            if best.get(s, 0) < v:
                best[s] = v
        waits = list(best.items())
        self.streams["sp"].append((waits, None, None))

    def emit(self):
        nc = self.nc
        sems = self.sems
        streams = self.streams

        def run(eng_handle, lst):
            for waits, fn, inc in lst:
                for s, v in waits:
                    eng_handle.wait_ge(sems[s], v)
                if fn is not None:
                    ins = fn(eng_handle)
                    ins.then_inc(sems[inc[0]], inc[1])

        with nc.Block() as block:
            @block.tensor
            def _(e):
                run(e, streams["pe"])

            @block.scalar
            def _(e):
                run(e, streams["act"])

            @block.vector
            def _(e):
                run(e, streams["dve"])

            @block.gpsimd
            def _(e):
                run(e, streams["pool"])

            @block.sync
            def _(e):
                run(e, streams["sp"])


F32R = mybir.dt.float32r


def mm(P, out, lhsT, rhs, start, stop, reads, writes, f32r=False):
    if f32r:
        lhsT = lhsT.bitcast(F32R)
        rhs = rhs.bitcast(F32R)
    return P.op("pe", lambda e: e.matmul(out, lhsT, rhs, start=start, stop=stop), reads, writes)


def transp(P, out, in_, ident, reads, writes):
    return P.op("pe", lambda e: e.transpose(out, in_, ident), reads, writes)


class BankView:
    def __init__(self, t):
        self.t = t

    def __getitem__(self, key):
        v = self.t[:].bitcast(BF16).rearrange("p (c t) -> p c t", t=128)
        return v[key]


class RhView:
    def __init__(self, t):
        self.t = t

    def __getitem__(self, key):
        return self.t[:, 128:256][key]


USE_F32R = True


class Rot:
    def __init__(self, items):
        self.items = items
        self.i = 0

    def next(self):
        it = self.items[self.i]
        self.i = (self.i + 1) % len(self.items)
        return it


def make_rot(P, kind, name, shape, dtype, n, es=None):
    items = []
    for i in range(n):
        if kind == "sb":
            t = P.sb("%s%d" % (name, i), shape, dtype, es)
        else:
            t = P.ps("%s%d" % (name, i), shape, dtype, es)
        items.append((t, Buf("%s%d" % (name, i))))
    return Rot(items)


class TokCtx:
    pass


def setup_tok(P, nc, NT, es):
    T = TokCtx()
    T.NT = NT
    T.NTT = NT // 128
    T.x = P.sb("x_res", [128, T.NTT, D], F32, es)
    T.xb = [Buf("x%d" % i) for i in range(T.NTT)]
    T.ident_bf = P.sb("sb_ident_bf", [128, 128], BF16, es)
    T.ident_b = Buf("ident_bf")
    T.ident_f = P.sb("sb_ident_f", [128, 128], F32, es)
    T.identf_b = Buf("ident_f")
    T.hT = P.sb("hT", [128, 8, 512], BF16, es)
    T.hT_b = Buf("hT")
    T.gcol = P.sb("gcol", [128, 8], F32, es)
    T.gcol_b = Buf("gcol")
    T.xn = make_rot(P, "sb", "xn", [128, D], BF16, 2, es)
    T.sq = P.sb("sq_junk", [128, D], BF16, es)
    T.sq_b = Buf("sq")
    T.stat = make_rot(P, "sb", "stat", [128, 4], F32, 4, es)
    T.w1 = make_rot(P, "sb", "w1s", [128, 8, 256], BF16, 6, es)
    T.ps_tp = Rot([(BankView(P.banks[0][0]), P.banks[0][1])])
    T.ps_a = Rot(P.banks[1:5])
    T.ps_b = Rot(P.banks[5:8])
    return T


def scope_ffn(P, T, es, tag):
    T.u = P.sb("u_hid" + tag, [128, NJ, 512], BF16, es)
    T.u_b = [Buf("u%d" % j) for j in range(NJ)]
    T.w2 = P.sb("w2_res" + tag, [128, NJ, D], BF16, es)
    T.w2_b = [Buf("w2_%d" % j) for j in range(NJ)]
    T.sg = make_rot(P, "sb", "sg" + tag, [128, 512], F32, 2, es)


def scope_proj(P, T, es, tag):
    T.stage = make_rot(P, "sb", "stage" + tag, [128, 512], F32, 3, es)


def load_consts(P, T, ident_bf_d, ident_f_d):
    P.dma("sp", T.ident_bf[:], ident_bf_d, (), (T.ident_b,))
    P.dma("sp", T.ident_f[:], ident_f_d, (), (T.identf_b,))


def load_gain(P, T, g_row):
    P.dma("sp", T.gcol[:], g_row.rearrange("(c p) -> p c", p=128), (), (T.gcol_b,),
          allow_slow_non_contiguous=True)


def prenorm_group(P, T, grp):
    for ti in range(4):
        tt = grp * 4 + ti
        xb = T.xb[tt]
        xt = T.x[:, tt, :]
        st, stb = T.stat.next()
        P.op("act", lambda e, xt=xt, st=st: e.activation(T.sq[:], xt, AF.Square, accum_out=st[:, 0:1]),
             (xb,), (T.sq_b, stb))
        P.op("dve", lambda e, st=st: e.tensor_scalar(st[:, 1:2], st[:, 0:1], 1.0 / D, NORM_EPS, ALU.mult, ALU.add),
             (stb,), (stb,))
        P.op("act", lambda e, st=st: e.activation(st[:, 2:3], st[:, 1:2], AF.Sqrt), (stb,), (stb,))
        P.op("dve", lambda e, st=st: e.reciprocal(st[:, 3:4], st[:, 2:3]), (stb,), (stb,))
        xn, xnb = T.xn.next()
        P.op("act", lambda e, xn=xn, xt=xt, st=st: e.activation(xn[:], xt, AF.Copy, scale=st[:, 3:4]),
             (xb, stb), (xnb,))
        tp, tpb = T.ps_tp.next()
        for c in range(8):
            transp(P, tp[:, c, :], xn[:, c * 128:(c + 1) * 128], T.ident_bf[:], (xnb, T.ident_b), (tpb,))
        gb = T.gcol[:].unsqueeze(2).to_broadcast([128, 8, 128])
        P.op("dve", lambda e, tp=tp, ti=ti, gb=gb: e.tensor_tensor(
            T.hT[:, :, ti * 128:(ti + 1) * 128], tp[:], gb, ALU.mult),
            (tpb, T.gcol_b), (T.hT_b,))


def load_w2(P, T, w2_d):
    for j in range(NJ):
        P.dma("pool", T.w2[:, j, :], w2_d[j * 128:(j + 1) * 128, :], (), (T.w2_b[j],))


def ffn_group(P, T, grp, w1_d):
    for j in range(NJ):
        w1, w1b = T.w1.next()
        P.dma("pool", w1[:, :, 0:128],
              w1_d[:, j * 128:(j + 1) * 128].rearrange("(c p) n -> p c n", p=128), (), (w1b,))
        P.dma("pool", w1[:, :, 128:256],
              w1_d[:, DFF + j * 128:DFF + (j + 1) * 128].rearrange("(c p) n -> p c n", p=128), (), (w1b,))
        pg, pgb = T.ps_a.next()
        pu, pub = T.ps_a.next()
        for c in range(8):
            mm(P, pg[:], w1[:, c, 0:128], T.hT[:, c, :], c == 0, c == 7, (w1b, T.hT_b), (pgb,))
        for c in range(8):
            mm(P, pu[:], w1[:, c, 128:256], T.hT[:, c, :], c == 0, c == 7, (w1b, T.hT_b), (pub,))
        sg, sgb = T.sg.next()
        P.op("act", lambda e, sg=sg, pg=pg: e.activation(sg[:], pg[:], AF.Silu), (pgb,), (sgb,))
        P.op("dve", lambda e, sg=sg, pu=pu, j=j: e.tensor_tensor(T.u[:, j, :], sg[:], pu[:], ALU.mult),
             (sgb, pub), (T.u_b[j],))
    for ti in range(4):
        tt = grp * 4 + ti
        for dh in range(2):
            po, pob = T.ps_b.next()
            for j in range(NJ):
                mm(P, po[:], T.u[:, j, ti * 128:(ti + 1) * 128], T.w2[:, j, dh * 512:(dh + 1) * 512],
                   j == 0, j == NJ - 1, (T.u_b[j], T.w2_b[j]), (pob,))
            xs = T.x[:, tt, dh * 512:(dh + 1) * 512]
            P.op("dve", lambda e, xs=xs, po=po: e.scalar_tensor_tensor(xs, po[:], 0.5, xs, ALU.mult, ALU.add),
                 (pob, T.xb[tt]), (T.xb[tt],))


def proj_group(P, T, grp, w_d, ncols, out_d, tok0, sigmoid_from=None):
    nch = ncols // 128
    for j in range(nch):
        w1, w1b = T.w1.next()
        P.dma("pool", w1[:, :, 0:128],
              w_d[:, j * 128:(j + 1) * 128].rearrange("(c p) n -> p c n", p=128), (), (w1b,))
        pg, pgb = T.ps_a.next()
        for c in range(8):
            mm(P, pg[:], w1[:, c, 0:128], T.hT[:, c, :], c == 0, c == 7, (w1b, T.hT_b), (pgb,))
        sg, sgb = T.stage.next()
        if sigmoid_from is not None and j * 128 >= sigmoid_from:
            P.op("act", lambda e, sg=sg, pg=pg: e.activation(sg[:], pg[:], AF.Sigmoid), (pgb,), (sgb,))
        else:
            P.op("act", lambda e, sg=sg, pg=pg: e.activation(sg[:], pg[:], AF.Copy), (pgb,), (sgb,))
        P.dma("sp", out_d[j * 128:(j + 1) * 128, tok0:tok0 + 512], sg[:], (sgb,), (), is_output=True)


def phase_ffn(P, T, g_d, w1_d, w2_d, tag):
    with ExitStack() as es2:
        scope_ffn(P, T, es2, tag)
        load_gain(P, T, g_d)
        load_w2(P, T, w2_d)
        for grp in range(T.NT // 512):
            prenorm_group(P, T, grp)
            ffn_group(P, T, grp, w1_d)
        P.barrier()


def phase_proj(P, T, gm_d, win_d, pt_d, tag):
    with ExitStack() as es2:
        scope_proj(P, T, es2, tag)
        load_gain(P, T, gm_d)
        for grp in range(T.NT // 512):
            prenorm_group(P, T, grp)
            proj_group(P, T, grp, win_d, C_IN, pt_d, grp * 512, sigmoid_from=C_MIX)
        P.barrier()


def phase_merge(P, T, sg_d, oa_d, ob_d, wb_d, wout_d, tag):
    with ExitStack() as es2:
        wb = P.sb("wb" + tag, [128, 2, 4, D], BF16, es2)
        wb_b = Buf("wb")
        wo = P.sb("wo" + tag, [128, 8, D], BF16, es2)
        wo_b = Buf("wo")
        oT = make_rot(P, "sb", "oT" + tag, [128, 2, 4, 512], BF16, 2, es2)
        sgl = make_rot(P, "sb", "sgl" + tag, [128, 2, 512], F32, 3, es2)
        t1 = make_rot(P, "sb", "mt1" + tag, [128, 512], F32, 2, es2)
        for g in range(2):
            for cc in range(4):
                P.dma("pool", wb[:, g, cc, :], wb_d[g, cc * 128:(cc + 1) * 128, :], (), (wb_b,))
        for dc in range(8):
            P.dma("pool", wo[:, dc, :], wout_d[dc * 128:(dc + 1) * 128, :], (), (wo_b,))
        for grp in range(T.NT // 512):
            tok0 = grp * 512
            o, ob_ = oT.next()
            for g, src in ((0, oa_d), (1, ob_d)):
                P.dma("pool", o[:, g, :, :],
                      src[:, tok0:tok0 + 512].rearrange("(c p) t -> p c t", p=128), (), (ob_,))
            for dc in range(8):
                sl, slb = sgl.next()
                for g in range(2):
                    P.dma("sp", sl[:, g, :], sg_d[g * D + dc * 128:g * D + (dc + 1) * 128, tok0:tok0 + 512],
                          (), (slb,))
                pa, pab = T.ps_a.next()
                pb, pbb = T.ps_a.next()
                for cc in range(4):
                    mm(P, pa[:], wb[:, 0, cc, dc * 128:(dc + 1) * 128], o[:, 0, cc, :], cc == 0, cc == 3,
                       (wb_b, ob_), (pab,))
                for cc in range(4):
                    mm(P, pb[:], wb[:, 1, cc, dc * 128:(dc + 1) * 128], o[:, 1, cc, :], cc == 0, cc == 3,
                       (wb_b, ob_), (pbb,))
                ta, tab = t1.next()
                P.op("dve", lambda e, ta=ta, pa=pa, sl=sl: e.tensor_tensor(ta[:], pa[:], sl[:, 0, :], ALU.mult),
                     (pab, slb), (tab,))
                tb, tbb = t1.next()
                P.op("dve", lambda e, tb=tb, pb=pb, sl=sl: e.tensor_tensor(tb[:], pb[:], sl[:, 1, :], ALU.mult),
                     (pbb, slb), (tbb,))
                P.op("dve", lambda e, ta=ta, tb=tb, dc=dc: e.tensor_tensor(T.hT[:, dc, :], ta[:], tb[:], ALU.add),
                     (tab, tbb), (T.hT_b,))
            for ti in range(4):
                tt = grp * 4 + ti
                for dh in range(2):
                    po, pob = T.ps_b.next()
                    for dc in range(8):
                        mm(P, po[:], T.hT[:, dc, ti * 128:(ti + 1) * 128], wo[:, dc, dh * 512:(dh + 1) * 512],
                           dc == 0, dc == 7, (T.hT_b, wo_b), (pob,))
                    xs = T.x[:, tt, dh * 512:(dh + 1) * 512]
                    P.op("dve", lambda e, xs=xs, po=po: e.tensor_tensor(xs, xs, po[:], ALU.add),
                         (pob, T.xb[tt]), (T.xb[tt],))
        P.barrier()


def tok_io(nc, NT, names):
    d = {}
    shapes = {
        "g1": [D], "w1": [D, 2 * DFF], "w2": [DFF, D], "gm": [D], "win": [D, C_IN],
        "g2n": [D], "w1b": [D, 2 * DFF], "w2b": [DFF, D],
        "wb": [2, 512, D], "wout": [D, D],
    }
    for n in names:
        d[n] = nc.dram_tensor(n, shapes[n], F32, kind="ExternalInput").ap()
    return d


def build_k1(NT):
    nc = bass.Bass("TRN2", target_bir_lowering=False)
    x_d = nc.dram_tensor("x", [NT, D], F32, kind="ExternalInput").ap()
    w = tok_io(nc, NT, ["g1", "w1", "w2", "gm", "win"])
    idb_d = nc.dram_tensor("ident_bf", [128, 128], BF16, kind="ExternalInput").ap()
    idf_d = nc.dram_tensor("ident_f", [128, 128], F32, kind="ExternalInput").ap()
    x1_d = nc.dram_tensor("x1", [NT, D], F32, kind="ExternalOutput").ap()
    pt_d = nc.dram_tensor("pt", [C_IN, NT], F32, kind="ExternalOutput").ap()
    with ExitStack() as es:
        P = Prog(nc, es)
        T = setup_tok(P, nc, NT, es)
        load_consts(P, T, idb_d, idf_d)
        for tt in range(T.NTT):
            P.dma("sp", T.x[:, tt, :], x_d[tt * 128:(tt + 1) * 128, :], (), (T.xb[tt],))
        phase_ffn(P, T, w["g1"], w["w1"], w["w2"], "a")
        for tt in range(T.NTT):
            P.dma("sp", x1_d[tt * 128:(tt + 1) * 128, :], T.x[:, tt, :], (T.xb[tt],), (), is_output=True)
        phase_proj(P, T, w["gm"], w["win"], pt_d, "a")
        P.finish()
        P.emit()
    return nc


def build_k3(NT, with_next):
    nc = bass.Bass("TRN2", target_bir_lowering=False)
    x_d = nc.dram_tensor("x", [NT, D], F32, kind="ExternalInput").ap()
    sg_d = nc.dram_tensor("sg", [2 * D, NT], F32, kind="ExternalInput").ap()
    oa_d = nc.dram_tensor("oa", [512, NT], F32, kind="ExternalInput").ap()
    ob_d = nc.dram_tensor("ob", [512, NT], F32, kind="ExternalInput").ap()
    names = ["wb", "wout", "g2n", "w1b", "w2b"]
    if with_next:
        names += ["g1", "w1", "w2", "gm", "win"]
    w = tok_io(nc, NT, names)
    idb_d = nc.dram_tensor("ident_bf", [128, 128], BF16, kind="ExternalInput").ap()
    idf_d = nc.dram_tensor("ident_f", [128, 128], F32, kind="ExternalInput").ap()
    x1_d = nc.dram_tensor("x1", [NT, D], F32, kind="ExternalOutput").ap()
    if with_next:
        pt_d = nc.dram_tensor("pt", [C_IN, NT], F32, kind="ExternalOutput").ap()
    with ExitStack() as es:
        P = Prog(nc, es)
        T = setup_tok(P, nc, NT, es)
        load_consts(P, T, idb_d, idf_d)
        for tt in range(T.NTT):
            P.dma("sp", T.x[:, tt, :], x_d[tt * 128:(tt + 1) * 128, :], (), (T.xb[tt],))
        phase_merge(P, T, sg_d, oa_d, ob_d, w["wb"], w["wout"], "m")
        phase_ffn(P, T, w["g2n"], w["w1b"], w["w2b"], "b")
        if with_next:
            phase_ffn(P, T, w["g1"], w["w1"], w["w2"], "a")
        for tt in range(T.NTT):
            P.dma("sp", x1_d[tt * 128:(tt + 1) * 128, :], T.x[:, tt, :], (T.xb[tt],), (), is_output=True)
        if with_next:
            phase_proj(P, T, w["gm"], w["win"], pt_d, "a")
        P.finish()
        P.emit()
    return nc


ATT_SHIFT = 4.0


def attn_consts(head, S):
    slope = 2.0 ** (-8.0 * (head + 1) / 4.0)
    ii = np.arange(S) % 512
    qaug = np.stack([(ii // 32) * 32, ii % 32]).astype(np.float32)
    ka = np.full((2, S), -slope, np.float32)
    kb = np.full((2, S), slope, np.float32)
    jj = np.arange(128, dtype=np.float32)[:, None]
    d = np.arange(64, dtype=np.float32)[None, :]
    biasA = -slope * 128.0 * d + slope * jj - ATT_SHIFT
    biasB = -slope * 128.0 * d - slope * jj - ATT_SHIFT
    iq = np.arange(512, dtype=np.float32)[None, None, :]
    off = np.arange(4, dtype=np.float32)[None, :, None]
    biasD = -slope * np.abs(iq - (128.0 * off + jj[:, :, None])) - ATT_SHIFT
    bf = ml_dtypes.bfloat16
    return {
        "qaug": qaug.astype(bf), "kaug_a": ka.astype(bf), "kaug_b": kb.astype(bf),
        "biasA": biasA.astype(np.float32), "biasB": biasB.astype(np.float32),
        "biasD": biasD.astype(np.float32),
        "ones64": np.full((64, 64), 1.0 / 64, np.float32),
    }


def attn_dram_inputs(nc, S):
    d = {}
    d["qT"] = nc.dram_tensor("qT", [2, 64, S], F32, kind="ExternalInput").ap()
    d["kT"] = nc.dram_tensor("kT", [2, 64, S], F32, kind="ExternalInput").ap()
    d["vT"] = nc.dram_tensor("vT", [128, S], F32, kind="ExternalInput").ap()
    d["qaug"] = nc.dram_tensor("qaug", [2, S], BF16, kind="ExternalInput").ap()
    d["kaug_a"] = nc.dram_tensor("kaug_a", [2, S], BF16, kind="ExternalInput").ap()
    d["kaug_b"] = nc.dram_tensor("kaug_b", [2, S], BF16, kind="ExternalInput").ap()
    d["biasA"] = nc.dram_tensor("biasA", [128, 64], F32, kind="ExternalInput").ap()
    d["biasB"] = nc.dram_tensor("biasB", [128, 64], F32, kind="ExternalInput").ap()
    d["biasD"] = nc.dram_tensor("biasD", [128, 4, 512], F32, kind="ExternalInput").ap()
    d["ones64"] = nc.dram_tensor("ones64", [64, 64], F32, kind="ExternalInput").ap()
    d["q_gain"] = nc.dram_tensor("q_gain", [64], F32, kind="ExternalInput").ap()
    d["k_gain"] = nc.dram_tensor("k_gain", [64], F32, kind="ExternalInput").ap()
    d["lamv"] = nc.dram_tensor("lamv", [4, 64], F32, kind="ExternalInput").ap()
    d["subg"] = nc.dram_tensor("subg", [128], F32, kind="ExternalInput").ap()
    d["laminit"] = nc.dram_tensor("laminit", [128, 2], F32, kind="ExternalInput").ap()
    return d


def phase_attn(P, ident_f, identf_b, A, ob_out_d, S, tag):
    NJT = S // 128
    NI = S // 512
    banks = P.banks
    sc_rot = Rot(banks[0:3])
    acc = banks[3:7]
    misc = banks[7]
    with ExitStack() as es2:
        Qa = P.sb("Qa" + tag, [66, S], BF16, es2)
        Ka = P.sb("Ka" + tag, [66, S], BF16, es2)
        Kb = P.sb("Kb" + tag, [66, S], BF16, es2)
        Qa_b, Ka_b, Kb_b = Buf("Qa"), Buf("Ka"), Buf("Kb")
        V = P.sb("V" + tag, [128, NJT, 129], BF16, es2)
        V_b = Buf("V")
        o0 = P.sb("o0" + tag, [128, NJT, 129], F32, es2)
        o0_b = Buf("o0")
        bA = P.sb("bA" + tag, [128, 64], F32, es2)
        bB = P.sb("bB" + tag, [128, 64], F32, es2)
        bD = P.sb("bD" + tag, [128, 4, 512], F32, es2)
        ones64 = P.sb("ones64" + tag, [64, 64], F32, es2)
        cst_b = Buf("cst")
        gq = P.sb("gq" + tag, [64, 2], F32, es2)
        gk = P.sb("gk" + tag, [64, 1], F32, es2)
        lamv = P.sb("lamv" + tag, [128, 4, 64], F32, es2)
        lamw = P.sb("lamw" + tag, [128, 8], F32, es2)
        subg = P.sb("subgc" + tag, [128, 2], F32, es2)
        li = P.sb("laminit_sb" + tag, [128, 2], F32, es2)
        par_b = Buf("par")
        ld = make_rot(P, "sb", "ald" + tag, [128, 512], F32, 3, es2)
        sq = make_rot(P, "sb", "asq" + tag, [64, 512], F32, 2, es2)
        rs = make_rot(P, "sb", "ars" + tag, [64, 512], F32, 2, es2)
        eT = make_rot(P, "sb", "eT" + tag, [128, 512], BF16, 4, es2)
        dtmp = make_rot(P, "sb", "dtmp" + tag, [128, 512], F32, 2, es2)
        osm = make_rot(P, "sb", "osm" + tag, [128, 136], F32, 3, es2)
        ost = make_rot(P, "sb", "ost" + tag, [128, 8], F32, 3, es2)
        ostage = make_rot(P, "sb", "ostage" + tag, [128, 512], F32, 2, es2)

        P.dma("sp", bA[:], A["biasA"], (), (cst_b,))
        P.dma("sp", bB[:], A["biasB"], (), (cst_b,))
        P.dma("sp", bD[:], A["biasD"], (), (cst_b,))
        P.dma("sp", ones64[:], A["ones64"], (), (cst_b,))
        P.dma("sp", gq[:, 0:1], A["q_gain"].rearrange("(p o) -> p o", o=1), (), (par_b,))
        P.dma("sp", gk[:, 0:1], A["k_gain"].rearrange("(p o) -> p o", o=1), (), (par_b,))
        P.dma("sp", subg[:, 0:1], A["subg"].rearrange("(p o) -> p o", o=1), (), (par_b,))
        P.dma("sp", li[:], A["laminit"], (), (par_b,))
        P.dma("sp", lamv[:].rearrange("p a b -> p (a b)"),
              A["lamv"].rearrange("a b -> (a b)").partition_broadcast(128), (), (par_b,))
        P.op("dve", lambda e: e.tensor_scalar(gq[:, 1:2], gq[:, 0:1], 0.125, None, ALU.mult), (par_b,), (par_b,))
        P.op("dve", lambda e: e.tensor_tensor(lamv[:, 0, :], lamv[:, 0, :], lamv[:, 1, :], ALU.mult), (par_b,), (par_b,))
        P.op("dve", lambda e: e.tensor_tensor(lamv[:, 2, :], lamv[:, 2, :], lamv[:, 3, :], ALU.mult), (par_b,), (par_b,))
        P.op("dve", lambda e: e.reduce_sum(lamw[:, 0:1], lamv[:, 0, :], axis=AX.X), (par_b,), (par_b,))
        P.op("dve", lambda e: e.reduce_sum(lamw[:, 1:2], lamv[:, 2, :], axis=AX.X), (par_b,), (par_b,))
        P.op("act", lambda e: e.activation(lamw[:, 2:4], lamw[:, 0:2], AF.Exp), (par_b,), (par_b,))
        P.op("dve", lambda e: e.tensor_tensor(lamw[:, 4:5], lamw[:, 3:4], lamw[:, 2:3], ALU.subtract), (par_b,), (par_b,))
        P.op("dve", lambda e: e.tensor_scalar(lamw[:, 4:5], lamw[:, 4:5], li[:, 0:1], None, ALU.subtract), (par_b,), (par_b,))
        P.op("dve", lambda e: e.tensor_scalar(subg[:, 1:2], subg[:, 0:1], li[:, 1:2], None, ALU.mult), (par_b,), (par_b,))

        P.op("pool", lambda e: e.memset(V[:, :, 128:129], 1.0), (), (V_b,))
        for ch in range(NI):
            l, lb = ld.next()
            P.dma("sp", l[:], A["vT"][:, ch * 512:(ch + 1) * 512], (), (lb,))
            mt, mb = misc
            for q4 in range(4):
                transp(P, mt[:, q4 * 128:(q4 + 1) * 128], l[:, q4 * 128:(q4 + 1) * 128], ident_f[:],
                       (lb, identf_b), (mb,))
            P.op("act", lambda e, ch=ch, mt=mt: e.activation(
                V[:, ch * 4:(ch + 1) * 4, 0:128], mt[:].rearrange("p (a b) -> p a b", b=128), AF.Copy),
                (mb,), (V_b,))

        for m in range(2):
            P.dma("sp", Qa[64:66, :], A["qaug"], (), (Qa_b,))
            P.dma("sp", Ka[64:66, :], A["kaug_a"], (), (Ka_b,))
            P.dma("sp", Kb[64:66, :], A["kaug_b"], (), (Kb_b,))
            for which in range(2):
                src = A["qT"] if which == 0 else A["kT"]
                for ch in range(NI):
                    cs = slice(ch * 512, (ch + 1) * 512)
                    l, lb = ld.next()
                    P.dma("sp", l[0:64, :], src[m, :, cs], (), (lb,))
                    s_, sb_ = sq.next()
                    P.op("act", lambda e, s_=s_, l=l: e.activation(s_[:], l[0:64, :], AF.Square), (lb,), (sb_,))
                    sc, scb = sc_rot.next()
                    mm(P, sc[0:64, :], ones64[:], s_[:], True, True, (sb_, cst_b), (scb,))
                    r_, rb_ = rs.next()
                    P.op("dve", lambda e, r_=r_, sc=sc: e.tensor_scalar(r_[:], sc[0:64, :], NORM_EPS, None, ALU.add),
                         (scb,), (rb_,))
                    P.op("act", lambda e, r_=r_: e.activation(r_[:], r_[:], AF.Sqrt), (rb_,), (rb_,))
                    P.op("dve", lambda e, r_=r_: e.reciprocal(r_[:], r_[:]), (rb_,), (rb_,))
                    if which == 0:
                        P.op("dve", lambda e, l=l, r_=r_, cs=cs: e.scalar_tensor_tensor(
                            Qa[0:64, cs], l[0:64, :], gq[:, 1:2], r_[:], ALU.mult, ALU.mult),
                            (lb, rb_, par_b), (Qa_b,))
                    else:
                        P.op("dve", lambda e, l=l, r_=r_, cs=cs: e.scalar_tensor_tensor(
                            Ka[0:64, cs], l[0:64, :], gk[:, 0:1], r_[:], ALU.mult, ALU.mult),
                            (lb, rb_, par_b), (Ka_b,))
                        P.op("act", lambda e, cs=cs: e.activation(Kb[0:64, cs], Ka[0:64, cs], AF.Copy),
                             (Ka_b,), (Kb_b,))
            def score(I, J):
                qs = slice(I * 512, (I + 1) * 512)
                ks = slice(J * 128, (J + 1) * 128)
                sc, scb = sc_rot.next()
                et, etb = eT.next()
                dlt = 4 * I - J
                if dlt >= 1:
                    mm(P, sc[:], Ka[:, ks], Qa[:, qs], True, True, (Ka_b, Qa_b), (scb,))
                    P.op("act", lambda e, et=et, sc=sc, dlt=dlt: e.activation(
                        et[:], sc[:], AF.Exp, bias=bA[:, dlt:dlt + 1]), (scb, cst_b), (etb,))
                elif dlt <= -4:
                    mm(P, sc[:], Kb[:, ks], Qa[:, qs], True, True, (Kb_b, Qa_b), (scb,))
                    P.op("act", lambda e, et=et, sc=sc, dlt=dlt: e.activation(
                        et[:], sc[:], AF.Exp, bias=bB[:, -dlt:-dlt + 1]), (scb, cst_b), (etb,))
                else:
                    off = -dlt
                    mm(P, sc[:], Ka[0:64, ks], Qa[0:64, qs], True, True, (Ka_b, Qa_b), (scb,))
                    dt_, dtb = dtmp.next()
                    P.op("dve", lambda e, dt_=dt_, sc=sc, off=off: e.tensor_tensor(
                        dt_[:], sc[:], bD[:, off, :], ALU.add), (scb, cst_b), (dtb,))
                    P.op("act", lambda e, et=et, dt_=dt_: e.activation(et[:], dt_[:], AF.Exp), (dtb,), (etb,))
                return et, etb

            seq = [(I, J) for I in range(NI) for J in range(NJT)]
            LOOK = 2
            pend = {}
            for idx in range(min(LOOK, len(seq))):
                pend[idx] = score(*seq[idx])
            for idx, (I, J) in enumerate(seq):
                qs = slice(I * 512, (I + 1) * 512)
                if idx + LOOK < len(seq):
                    pend[idx + LOOK] = score(*seq[idx + LOOK])
                et, etb = pend.pop(idx)
                for qi in range(4):
                    at, ab = acc[qi]
                    mm(P, at[:, 0:129], et[:, qi * 128:(qi + 1) * 128], V[:, J, :], J == 0, J == NJT - 1,
                       (etb, V_b), (ab,))
                if J != NJT - 1:
                    continue
                for qi in range(4):
                    at, ab = acc[qi]
                    qt = I * 4 + qi
                    if m == 0:
                        P.op("dve", lambda e, at=at, qt=qt: e.tensor_copy(o0[:, qt, :], at[:, 0:129]),
                             (ab,), (o0_b,))
                        continue
                    st, stb = ost.next()
                    om, omb = osm.next()
                    P.op("dve", lambda e, st=st, qt=qt: e.reciprocal(st[:, 0:1], o0[:, qt, 128:129]), (o0_b,), (stb,))
                    P.op("dve", lambda e, st=st, at=at: e.reciprocal(st[:, 1:2], at[:, 128:129]), (ab,), (stb,))
                    P.op("dve", lambda e, st=st: e.tensor_tensor(st[:, 1:2], st[:, 1:2], lamw[:, 4:5], ALU.mult),
                         (stb, par_b), (stb,))
                    P.op("dve", lambda e, om=om, st=st, qt=qt: e.tensor_scalar(
                        om[:, 0:128], o0[:, qt, 0:128], st[:, 0:1], None, ALU.mult), (o0_b, stb), (omb,))
                    P.op("dve", lambda e, om=om, st=st, at=at: e.scalar_tensor_tensor(
                        om[:, 0:128], at[:, 0:128], st[:, 1:2], om[:, 0:128], ALU.mult, ALU.add),
                        (ab, stb, omb), (omb,))
                    s_, sb_ = sq.next()
                    P.op("act", lambda e, om=om, st=st, s_=s_: e.activation(
                        s_[:, 0:128].bitcast(F32) if False else dtmp.items[0][0][:, 0:128], om[:, 0:128], AF.Square,
                        accum_out=st[:, 2:3]), (omb,), (stb, dtmp.items[0][1]))
                    P.op("dve", lambda e, st=st: e.tensor_scalar(st[:, 3:4], st[:, 2:3], 1.0 / 128, NORM_EPS,
                                                                ALU.mult, ALU.add), (stb,), (stb,))
                    P.op("act", lambda e, st=st: e.activation(st[:, 4:5], st[:, 3:4], AF.Sqrt), (stb,), (stb,))
                    P.op("dve", lambda e, st=st: e.reciprocal(st[:, 5:6], st[:, 4:5]), (stb,), (stb,))
                    P.op("dve", lambda e, om=om, st=st: e.tensor_scalar(
                        om[:, 0:128], om[:, 0:128], st[:, 5:6], None, ALU.mult), (omb, stb), (omb,))
                    mt, mb = misc
                    transp(P, mt[:, qi * 128:(qi + 1) * 128], om[:, 0:128], ident_f[:], (omb, identf_b), (mb,))
                if m == 1:
                    mt, mb = misc
                    og, ogb = ostage.next()
                    P.op("dve", lambda e, og=og, mt=mt: e.tensor_scalar(og[:], mt[:], subg[:, 1:2], None, ALU.mult),
                         (mb, par_b), (ogb,))
                    P.dma("sp", ob_out_d[:, qs], og[:], (ogb,), (), is_output=True)
        P.barrier()


def build_k2a(S):
    nc = bass.Bass("TRN2", target_bir_lowering=False)
    A = attn_dram_inputs(nc, S)
    idf_d = nc.dram_tensor("ident_f", [128, 128], F32, kind="ExternalInput").ap()
    ob_d = nc.dram_tensor("obT", [128, S], F32, kind="ExternalOutput").ap()
    with ExitStack() as es:
        P = Prog(nc, es)
        ident_f = P.sb("sb_ident_f", [128, 128], F32, es)
        identf_b = Buf("identf")
        P.dma("sp", ident_f[:], idf_d, (), (identf_b,))
        phase_attn(P, ident_f, identf_b, A, ob_d, S, "t")
        P.finish()
        P.emit()
    return nc


RW_L = 512


def rwkv_consts():
    p = np.arange(128)[:, None]
    f = np.arange(128)[None, :]
    su = (f > p).astype(np.float32)
    iu = (f >= p).astype(np.float32)
    sl = (f < p).astype(np.float32)
    il = (f <= p).astype(np.float32)
    MK = np.stack([np.concatenate([-su, iu], 1), np.concatenate([-sl, il], 1)])
    BMK = np.stack([np.concatenate([su, iu], 1), np.concatenate([sl, il], 1)])
    NK = np.stack([-sl, -su])
    blk = np.zeros((128, 128), np.float32)
    blk[:64, :64] = 1.0
    blk[64:, 64:] = 1.0
    rm = np.ones((128, RW_L), np.float32)
    rm[:, ::128] = 0.0
    return {"MK": MK, "BMK": BMK, "NK": NK, "onesblk": blk, "resetm": rm}


def rwkv_dram_inputs(nc, S):
    d = {}
    def inp(n, shape):
        d[n] = nc.dram_tensor(n, shape, F32, kind="ExternalInput").ap()
    inp("rkv", [3, 128, S])
    inp("lor", [3, 128, S])
    inp("mu6", [6, 128])
    inp("w0", [2, 128]); inp("w2", [128, 128]); inp("a0", [2, 128]); inp("a2", [128, 128])
    inp("g2", [128, 128])
    inp("vec5", [5, 128])
    inp("MK", [2, 128, 256]); inp("BMK", [2, 128, 256]); inp("NK", [2, 128, 128])
    inp("onesblk", [128, 128]); inp("resetm", [128, RW_L])
    return d


def phase_rwkv(P, ident_f, identf_b, R, oa_out_d, S, tag):
    L = RW_L
    NCH = L // 128
    NSEG = S // L
    pb = Rot(P.banks[0:7])
    ybank = P.banks[7]
    with ExitStack() as es2:
        def sbt(name, shape):
            return P.sb(name + tag, shape, F32, es2)
        MK = sbt("MK", [128, 2, 256]); BMK = sbt("BMK", [128, 2, 256]); NK = sbt("NK", [128, 2, 128])
        onesblk = sbt("onesblk", [128, 128]); resetm = sbt("resetm", [128, L])
        cst_b = Buf("rcst")
        mu = sbt("mu", [128, 6]); hmu = sbt("hmu", [128, 6]); omm = sbt("omm", [128, 6])
        w0c = sbt("w0c", [128, 2]); a0c = sbt("a0c", [128, 2])
        w2s = sbt("w2s", [128, 128]); a2s = sbt("a2s", [128, 128]); g2s = sbt("g2s", [128, 128])
        vec = sbt("vec", [128, 8])
        par_b = Buf("rpar")
        raw = [(sbt("raw%d" % i, [128, L + 2]), Buf("raw%d" % i)) for i in range(6)]
        sh = [(sbt("sh%d" % i, [128, L]), Buf("sh%d" % i)) for i in range(6)]
        names = ["tmpA", "logw", "a_", "a_o", "kap", "kd", "bb", "G", "E1", "E3", "bt", "kt", "bh", "kh", "bon", "gate"]
        tl = {n: (sbt(n, [128, L]), Buf(n)) for n in names}
        KR = sbt("KR", [128, NCH, 256]); KR_b = Buf("KR")
        tot = sbt("tot", [128, NCH]); etot = sbt("etot", [128, NCH]); tot_b = Buf("tot")
        yacc = sbt("yacc", [128, S // 128, 128]); yacc_b = [Buf("yacc%d" % i) for i in range(S // 128)]
        wk128 = make_rot(P, "sb", "wk128" + tag, [128, 128], F32, 4, es2)
        Srot = [make_rot(P, "sb", "S%d" % hh + tag, [128, 64], F32, 3, es2) for hh in range(2)]
        gst = make_rot(P, "sb", "gst" + tag, [128, 16], F32, 3, es2)
        ostage = make_rot(P, "sb", "rostage" + tag, [128, 512], F32, 2, es2)

        def dve(fn, reads, writes):
            return P.op("dve", fn, reads, writes)

        def act(fn, reads, writes):
            return P.op("act", fn, reads, writes)

        P.dma("sp", MK[:], R["MK"].rearrange("d p f -> p d f"), (), (cst_b,))
        P.dma("sp", BMK[:], R["BMK"].rearrange("d p f -> p d f"), (), (cst_b,))
        P.dma("sp", NK[:], R["NK"].rearrange("d p f -> p d f"), (), (cst_b,))
        P.dma("sp", onesblk[:], R["onesblk"], (), (cst_b,))
        P.dma("sp", resetm[:], R["resetm"], (), (cst_b,))
        P.dma("sp", mu[:], R["mu6"].rearrange("i p -> p i"), (), (par_b,), allow_slow_non_contiguous=True)
        P.dma("sp", w0c[:], R["w0"].rearrange("i p -> p i"), (), (par_b,), allow_slow_non_contiguous=True)
        P.dma("sp", a0c[:], R["a0"].rearrange("i p -> p i"), (), (par_b,), allow_slow_non_contiguous=True)
        P.dma("sp", vec[:, 0:5], R["vec5"].rearrange("i p -> p i"), (), (par_b,), allow_slow_non_contiguous=True)
        P.dma("sp", w2s[:], R["w2"], (), (par_b,))
        P.dma("sp", a2s[:], R["a2"], (), (par_b,))
        P.dma("sp", g2s[:], R["g2"], (), (par_b,))
        dve(lambda e: e.tensor_scalar(hmu[:], mu[:], 0.5, None, ALU.mult), (par_b,), (par_b,))
        dve(lambda e: e.tensor_scalar(omm[:], mu[:], -1.0, 1.0, ALU.mult, ALU.add), (par_b,), (par_b,))
        dve(lambda e: e.tensor_scalar(vec[:, 5:6], vec[:, 1:2], -1.0, 1.0, ALU.mult, ALU.add), (par_b,), (par_b,))
        dve(lambda e: e.tensor_scalar(vec[:, 6:7], vec[:, 1:2], -2.0, 2.0, ALU.mult, ALU.add), (par_b,), (par_b,))

        def lora_sig(dst, src_i, wsb, biascol, d):
            ds_ = slice(d * 64, (d + 1) * 64)
            st, stb = sh[src_i]
            rhs_t, rhs_b = st, stb
            if src_i == 3:
                tt_, ttb = tl["tmpA"]
                act(lambda e: e.activation(tt_[ds_, :], st[ds_, :], AF.Tanh), (stb,), (ttb,))
                rhs_t, rhs_b = tt_, ttb
            bk, bkb = pb.next()
            mm(P, bk[:, 0:L], wsb[ds_, :], rhs_t[ds_, :], True, True, (par_b, rhs_b), (bkb,))
            act(lambda e: e.activation(dst[0][:], bk[:, 0:L], AF.Sigmoid, bias=biascol[:, d:d + 1]),
                (bkb, par_b), (dst[1],))

        def prep(seg, d, final):
            t0 = seg * L
            use = [0, 1, 2, 3, 4] + ([5] if final else [])
            for i in use:
                src = R["rkv"][i] if i < 3 else R["lor"][i - 3]
                rt, rb = raw[i]
                lo = max(t0 - 1, 0)
                hi = min(t0 + L + 1, S)
                P.dma("sp", rt[:, lo - (t0 - 1):hi - (t0 - 1)], src[:, lo:hi], (), (rb,))
                if t0 == 0:
                    P.op("pool", lambda e, rt=rt: e.memset(rt[:, 0:1], 0.0), (), (rb,))
                if t0 + L == S:
                    P.op("pool", lambda e, rt=rt: e.memset(rt[:, L + 1:L + 2], 0.0), (), (rb,))
                pA, pAb = tl["E1"]
                pB, pBb = tl["E3"]
                st, stb = sh[i]
                P.op("pool", lambda e, rt=rt: e.tensor_tensor(pA[:], rt[:, 0:L], rt[:, 2:L + 2], ALU.add), (rb,), (pAb,))
                P.op("pool", lambda e, i=i: e.tensor_scalar(pA[:], pA[:], hmu[:, i:i + 1], 0.0, ALU.mult, ALU.add),
                     (pAb, par_b), (pAb,))
                P.op("pool", lambda e, rt=rt, i=i: e.tensor_scalar(pB[:], rt[:, 1:L + 1], omm[:, i:i + 1], 0.0, ALU.mult, ALU.add),
                     (rb, par_b), (pBb,))
                P.op("pool", lambda e, st=st: e.tensor_tensor(st[:], pA[:], pB[:], ALU.add), (pAb, pBb), (stb,))
            kp, kpb = tl["kap"]
            tA, tAb = tl["tmpA"]
            ksh, kshb = sh[1]
            dve(lambda e: e.tensor_scalar(kp[:], ksh[:], vec[:, 0:1], None, ALU.mult), (kshb, par_b), (kpb,))
            act(lambda e: e.activation(tA[:], kp[:], AF.Square), (kpb,), (tAb,))
            bk, bkb = pb.next()
            mm(P, bk[:, 0:L], onesblk[:], tA[:], True, True, (cst_b, tAb), (bkb,))
            act(lambda e, bk=bk: e.activation(tA[:], bk[:, 0:L], AF.Sqrt), (bkb,), (tAb,))
            dve(lambda e: e.tensor_scalar(tA[:], tA[:], 1e-12, None, ALU.max), (tAb,), (tAb,))
            dve(lambda e: e.reciprocal(tA[:], tA[:]), (tAb,), (tAb,))
            dve(lambda e: e.tensor_tensor(kp[:], kp[:], tA[:], ALU.mult), (kpb, tAb), (kpb,))
            lw, lwb = tl["logw"]
            lora_sig(tl["logw"], 3, w2s, w0c, d)
            dve(lambda e: e.tensor_scalar(lw[:], lw[:], -DECAY_SCALE, None, ALU.mult), (lwb,), (lwb,))
            lora_sig(tl["a_"], 4, a2s, a0c, d)
            av, avb = tl["a_"]
            kdv, kdb = tl["kd"]
            dve(lambda e: e.tensor_scalar(tA[:], av[:], vec[:, 1:2], vec[:, 5:6], ALU.mult, ALU.add),
                (avb, par_b), (tAb,))
            dve(lambda e: e.tensor_tensor(kdv[:], ksh[:], tA[:], ALU.mult), (kshb, tAb), (kdb,))
            bbv, bbb = tl["bb"]
            dve(lambda e: e.tensor_tensor(bbv[:], kp[:], av[:], ALU.mult), (kpb, avb), (bbb,))
            G, Gb = tl["G"]
            dve(lambda e: e.tensor_tensor_scan(G[:], resetm[:], lw[:], 0.0, ALU.mult, ALU.add), (cst_b, lwb), (Gb,))
            G3 = G[:].rearrange("p (c t) -> p c t", t=128)
            dve(lambda e: e.tensor_copy(tot[:], G3[:, :, 127]), (Gb,), (tot_b,))
            act(lambda e: e.activation(etot[:], tot[:], AF.Exp), (tot_b,), (tot_b,))
            if d == 1:
                tb3 = tot[:].unsqueeze(2).to_broadcast([128, NCH, 128])
                dve(lambda e: e.tensor_tensor(G3, G3, tb3, ALU.subtract), (Gb, tot_b), (Gb,))
                dve(lambda e: e.scalar_tensor_tensor(G[:], G[:], -1.0, lw[:], ALU.mult, ALU.add), (Gb, lwb), (Gb,))
            E1, E1b = tl["E1"]
            E3, E3b = tl["E3"]
            rsh, rshb = sh[0]
            KR3k = KR[:, :, 0:128]
            KR3r = KR[:, :, 128:256]
            act(lambda e: e.activation(E1[:], G[:], AF.Exp), (Gb,), (E1b,))
            dve(lambda e: e.tensor_tensor(KR3r.bitcast(F32R), rsh[:].rearrange("p (c t) -> p c t", t=128),
                                          E1[:].rearrange("p (c t) -> p c t", t=128), ALU.mult),
                (rshb, E1b), (KR_b,))
            dve(lambda e: e.tensor_tensor(tA[:], G[:], lw[:], ALU.subtract), (Gb, lwb), (tAb,))
            act(lambda e: e.activation(E1[:], tA[:], AF.Exp), (tAb,), (E1b,))
            dve(lambda e: e.tensor_tensor(KR3k.bitcast(F32R), kp[:].rearrange("p (c t) -> p c t", t=128),
                                          E1[:].rearrange("p (c t) -> p c t", t=128), ALU.mult),
                (kpb, E1b), (KR_b,))
            act(lambda e: e.activation(E3[:], G[:], AF.Exp, scale=-1.0), (Gb,), (E3b,))
            for nm, srcv in (("bt", tl["bb"]), ("kt", tl["kd"])):
                o_, ob_ = tl[nm]
                dve(lambda e, o_=o_, srcv=srcv: e.tensor_tensor(o_[:].bitcast(F32R), srcv[0][:], E3[:], ALU.mult),
                    (srcv[1], E3b), (ob_,))
            eb3 = etot[:].unsqueeze(2).to_broadcast([128, NCH, 128])
            E33 = E3[:].rearrange("p (c t) -> p c t", t=128)
            dve(lambda e: e.tensor_tensor(E33, E33, eb3, ALU.mult), (E3b, tot_b), (E3b,))
            for nm, srcv in (("bh", tl["bb"]), ("kh", tl["kd"])):
                o_, ob_ = tl[nm]
                dve(lambda e, o_=o_, srcv=srcv: e.tensor_tensor(o_[:], srcv[0][:], E3[:], ALU.mult),
                    (srcv[1], E3b), (ob_,))
            if final:
                lora_sig(tl["a_o"], 4, a2s, a0c, 0)
                ao, aob = tl["a_o"]
                dve(lambda e: e.tensor_tensor(tA[:], av[:], ao[:], ALU.add), (avb, aob), (tAb,))
                dve(lambda e: e.tensor_scalar(tA[:], tA[:], vec[:, 1:2], vec[:, 6:7], ALU.mult, ALU.add),
                    (tAb, par_b), (tAb,))
                dve(lambda e: e.tensor_tensor(tA[:], tA[:], ksh[:], ALU.mult), (tAb, kshb), (tAb,))
                dve(lambda e: e.tensor_tensor(tA[:], tA[:], rsh[:], ALU.mult), (tAb, rshb), (tAb,))
                dve(lambda e: e.tensor_scalar(tA[:], tA[:], vec[:, 2:3], None, ALU.mult), (tAb, par_b), (tAb,))
                bk, bkb = pb.next()
                mm(P, bk[:, 0:L], onesblk[:], tA[:], True, True, (cst_b, tAb), (bkb,))
                bon, bonb = tl["bon"]
                vsh, vshb = sh[2]
                dve(lambda e, bk=bk: e.tensor_tensor(bon[:], bk[:, 0:L], vsh[:], ALU.mult), (bkb, vshb), (bonb,))
                gsh, gshb = sh[5]
                act(lambda e: e.activation(tA[:], gsh[:], AF.Sigmoid), (gshb,), (tAb,))
                bk, bkb = pb.next()
                mm(P, bk[:, 0:L], g2s[:], tA[:], True, True, (par_b, tAb), (bkb,))
                gt, gtb = tl["gate"]
                act(lambda e, bk=bk: e.activation(gt[:], bk[:, 0:L], AF.Copy), (bkb,), (gtb,))

        NU = NCH * 2
        UT = []
        for ui in range(NU):
            t = {}
            for nm, shp in (("am1", [128, 256]), ("bm2", [128, 256]), ("QR0", [128, 256]), ("QR1", [128, 256]),
                            ("QT0", [128, 256]), ("QT1", [128, 256]),
                            ("u0", [128, 64]), ("Dsb", [128, 64]), ("dgt", [128, 64]), ("TT", [128, 64]),
                            ("RpT", [128, 128])):
                t[nm] = (sbt("%s_%d" % (nm, ui), shp), Buf("%s_%d" % (nm, ui)))
            UT.append(t)
        TOK = [(sbt("tok_%d" % c, [128, 512]), Buf("tok_%d" % c)) for c in range(NCH)]
        for ui in range(NU):
            for nm in ("QT0", "QT1"):
                tt_, ttb_ = UT[ui][nm]
                P.op("dve", lambda e, tt_=tt_: e.tensor_scalar(tt_[:, 128:256].bitcast(F32R), ident_f[:], 0.0, None, ALU.mult), (identf_b,), (ttb_,))

        def pre_segment(d):
            bt, btb = tl["bt"]; kt, ktb = tl["kt"]; bh, bhb = tl["bh"]; kh, khb = tl["kh"]
            vsh, vshb = sh[2]
            for c in range(NCH):
                cs = slice(c * 128, (c + 1) * 128)
                bk, bkb = pb.next()
                transp(P, bk[:, 0:128], KR[:, c, 0:128], ident_f[:], (KR_b, identf_b), (bkb,))
                transp(P, bk[:, 128:256], bh[:, cs], ident_f[:], (bhb, identf_b), (bkb,))
                transp(P, bk[:, 256:384], kh[:, cs], ident_f[:], (khb, identf_b), (bkb,))
                transp(P, bk[:, 384:512], vsh[:, cs], ident_f[:], (vshb, identf_b), (bkb,))
                tok, tokb = TOK[c]
                act(lambda e, tok=tok, bk=bk: e.activation(tok[:], bk[:], AF.Copy), (bkb,), (tokb,))
            st = []
            for c in range(NCH):
                cs = slice(c * 128, (c + 1) * 128)
                tok, tokb = TOK[c]
                for hh in range(2):
                    T_ = UT[c * 2 + hh]
                    hs = slice(hh * 64, (hh + 1) * 64)
                    hc = lambda base, hh=hh: slice(base + hh * 64, base + hh * 64 + 64)
                    b1, b1b = pb.next()
                    mm(P, b1[:, 0:256], bt[hs, cs], KR[hs, c, :], True, True, (btb, KR_b), (b1b,), f32r=USE_F32R)
                    am1, am1b = T_["am1"]
                    dve(lambda e, am1=am1, b1=b1: e.tensor_tensor(am1[:].bitcast(F32R), b1[:, 0:256], MK[:, d, :], ALU.mult),
                        (b1b, cst_b), (am1b,))
                    b2, b2b = pb.next()
                    mm(P, b2[:, 0:256], kt[hs, cs], KR[hs, c, :], True, True, (ktb, KR_b), (b2b,), f32r=USE_F32R)
                    bm2, bm2b = T_["bm2"]
                    dve(lambda e, bm2=bm2, b2=b2: e.tensor_tensor(bm2[:], b2[:, 0:256], BMK[:, d, :], ALU.mult),
                        (b2b, cst_b), (bm2b,))
                    b3, b3b = pb.next()
                    mm(P, b3[:, 0:128], KR[hs, c, 0:128], bt[hs, cs], True, True, (KR_b, btb), (b3b,))
                    qr, qrb = T_["QR0"]
                    dve(lambda e, qr=qr, b3=b3: e.tensor_tensor(qr[:, 0:128].bitcast(F32R), b3[:, 0:128], NK[:, d, :], ALU.mult),
                        (b3b, cst_b), (qrb,))
                    b4, b4b = pb.next()
                    mm(P, b4[:, 0:64], bm2[:, 0:128], tok[:, hc(384)], True, True, (bm2b, tokb), (b4b,))
                    act(lambda e, qr=qr, tok=tok, hc=hc: e.activation(qr[:, 128:192].bitcast(F32R), tok[:, hc(0)], AF.Copy), (tokb,), (qrb,))
                    act(lambda e, qr=qr, b4=b4: e.activation(qr[:, 192:256].bitcast(F32R), b4[:, 0:64], AF.Copy), (b4b,), (qrb,))
                    st.append(dict(c=c, hh=hh, hs=hs, hc=hc, T=T_, am1=(am1, am1b), bm2=(bm2, bm2b),
                                   QR=(qr, qrb), QT=(am1, am1b)))
            for j in range(7):
                for X in st:
                    T_ = X["T"]
                    qr, qrb = X["QR"]
                    qt, qtb = X["QT"]
                    nqr, nqrb = T_["QR%d" % ((j + 1) % 2)]
                    c1, c1b = pb.next()
                    mm(P, c1[:, 0:256], qt[:, 0:128], qr[:, 0:256], True, True, (qtb, qrb), (c1b,), f32r=USE_F32R)
                    if j < 6:
                        act(lambda e, nqr=nqr, c1=c1: e.activation(nqr[:, 0:128].bitcast(F32R), c1[:, 0:128], AF.Copy), (c1b,), (nqrb,))
                    dve(lambda e, nqr=nqr, qr=qr, c1=c1: e.tensor_tensor(nqr[:, 128:256].bitcast(F32R), qr[:, 128:256], c1[:, 128:256], ALU.add),
                        (qrb, c1b), (nqrb,))
                    if j < 6:
                        nqt, nqtb = T_["QT%d" % ((j + 1) % 2)]
                        c2, c2b = pb.next()
                        mm(P, c2[:, 0:256], qr[:, 0:128], qt[:, 0:256], True, True, (qrb, qtb), (c2b,), f32r=USE_F32R)
                        act(lambda e, nqt=nqt, c2=c2: e.activation(nqt[:, 0:128].bitcast(F32R), c2[:, 0:128], AF.Copy), (c2b,), (nqtb,))
                        X["QT"] = (nqt, nqtb)
                    X["QR"] = (nqr, nqrb)
            for X in st:
                qr, qrb = X["QR"]
                X["rh"] = (RhView(qr), qrb)
            for X in st:
                T_ = X["T"]
                c = X["c"]; hh = X["hh"]; hs = X["hs"]; hc = X["hc"]
                tok, tokb = TOK[c]
                rh, rhb = X["rh"]
                am1, am1b = X["am1"]
                u0, u0b = T_["u0"]
                act(lambda e, u0=u0, rh=rh: e.activation(u0[:], rh[:, 64:128], AF.Copy, scale=-1.0), (rhb,), (u0b,))
                b8, b8b = pb.next()
                mm(P, b8[hs, 0:64], rh[:, 0:64], tok[:, hc(128)], True, True, (rhb, tokb), (b8b,))
                dgt, dgtb = T_["dgt"]
                dve(lambda e, dgt=dgt, hs=hs, hh=hh, c=c: e.tensor_scalar(
                    dgt[hs, 0:64], ident_f[hs, hh * 64:(hh + 1) * 64], etot[hs, c:c + 1], None, ALU.mult),
                    (identf_b, tot_b), (dgtb,))
                TT, TTb = T_["TT"]
                dve(lambda e, TT=TT, b8=b8, dgt=dgt, hs=hs: e.scalar_tensor_tensor(
                    TT[hs, 0:64], b8[hs, 0:64], -1.0, dgt[hs, 0:64], ALU.mult, ALU.add), (b8b, dgtb), (TTb,))
                b9, b9b = pb.next()
                mm(P, b9[hs, 0:64], tok[:, hc(128)], u0[:], True, False, (tokb, u0b), (b9b,))
                mm(P, b9[hs, 0:64], tok[:, hc(256)], tok[:, hc(384)], False, True, (tokb,), (b9b,))
                Dsb, Dsbb = T_["Dsb"]
                act(lambda e, Dsb=Dsb, b9=b9, hs=hs: e.activation(Dsb[hs, :], b9[hs, 0:64], AF.Copy), (b9b,), (Dsbb,))
                b10, b10b = pb.next()
                mm(P, b10[hs, 0:128], rh[:, 0:64], am1[:, 128:256], True, True, (rhb, am1b), (b10b,))
                RpT, RpTb = T_["RpT"]
                dve(lambda e, RpT=RpT, b10=b10, hs=hs, c=c: e.tensor_tensor(
                    RpT[hs, :], KR[hs, c, 128:256], b10[hs, 0:128], ALU.subtract), (KR_b, b10b), (RpTb,))
            return st

        def chunk(seg, c, d, final, Scur, og, st):
            gc = seg * NCH + c
            cs = slice(c * 128, (c + 1) * 128)
            tok, tokb = TOK[c]
            ybk, ybkb = ybank
            newS = []
            for hh in range(2):
                X = st[c * 2 + hh]
                T_ = X["T"]
                hs, hc = X["hs"], X["hc"]
                am1, am1b = X["am1"]
                bm2, bm2b = X["bm2"]
                u0, u0b = T_["u0"]
                TT, TTb = T_["TT"]
                Dsb, Dsbb = T_["Dsb"]
                RpT, RpTb = T_["RpT"]
                S0, S0b = Scur[hh]
                b11, b11b = pb.next()
                mm(P, b11[hs, 0:64], TT[hs, 0:64], S0[hs, :], True, True, (TTb, S0b), (b11b,))
                S1, S1b = Srot[hh].next()
                dve(lambda e, S1=S1, b11=b11, Dsb=Dsb, hs=hs: e.tensor_tensor(
                    S1[hs, :], b11[hs, 0:64], Dsb[hs, :], ALU.add), (b11b, Dsbb), (S1b,))
                newS.append((S1, S1b))
                yo = ybk[:, hh * 64:(hh + 1) * 64]
                mm(P, yo, RpT[hs, :], S0[hs, :], True, False, (RpTb, S0b), (ybkb,))
                mm(P, yo, am1[:, 128:256], u0[:], False, False, (am1b, u0b), (ybkb,))
                mm(P, yo, bm2[:, 128:256], tok[:, hc(384)], False, True, (bm2b, tokb), (ybkb,))
            if not final:
                act(lambda e: e.activation(yacc[:, gc, :], ybk[:, 0:128], AF.Copy), (ybkb,), (yacc_b[gc],))
            else:
                yt, ytb = wk128.next()
                dve(lambda e, yt=yt: e.tensor_tensor(yt[:], yacc[:, gc, :], ybk[:, 0:128], ALU.add),
                    (yacc_b[gc], ybkb), (ytb,))
                g_, gb_ = gst.next()
                for hh in range(2):
                    hcol = slice(hh * 64, (hh + 1) * 64)
                    o6 = hh * 8
                    dve(lambda e, g_=g_, yt=yt, hcol=hcol, o6=o6: e.bn_stats(g_[:, o6:o6 + 6], yt[:, hcol]), (ytb,), (gb_,))
                    dve(lambda e, g_=g_, o6=o6: e.bn_aggr(g_[:, o6 + 6:o6 + 8], g_[:, o6:o6 + 6]), (gb_,), (gb_,))
                    dve(lambda e, g_=g_, o6=o6: e.tensor_scalar(g_[:, o6 + 7:o6 + 8], g_[:, o6 + 7:o6 + 8], GN_EPS, None, ALU.add),
                        (gb_,), (gb_,))
                    act(lambda e, g_=g_, o6=o6: e.activation(g_[:, o6 + 7:o6 + 8], g_[:, o6 + 7:o6 + 8], AF.Sqrt), (gb_,), (gb_,))
                    dve(lambda e, g_=g_, o6=o6: e.reciprocal(g_[:, o6 + 7:o6 + 8], g_[:, o6 + 7:o6 + 8]), (gb_,), (gb_,))
                    dve(lambda e, g_=g_, yt=yt, hcol=hcol, o6=o6: e.tensor_scalar(
                        yt[:, hcol], yt[:, hcol], g_[:, o6 + 6:o6 + 7], g_[:, o6 + 7:o6 + 8], ALU.subtract, ALU.mult),
                        (ytb, gb_), (ytb,))
                tb_, tbb = pb.next()
                transp(P, tb_[:, 0:128], yt[:], ident_f[:], (ytb, identf_b), (tbb,))
                o1, o1b = wk128.next()
                dve(lambda e, o1=o1, tb_=tb_: e.tensor_scalar(o1[:], tb_[:, 0:128], vec[:, 3:4], vec[:, 4:5], ALU.mult, ALU.add),
                    (tbb, par_b), (o1b,))
                bon, bonb = tl["bon"]
                gt, gtb = tl["gate"]
                import os
                DBG = int(os.environ.get("RW_DBG", "0"))
                if DBG == 1:
                    dve(lambda e: e.tensor_copy(og[0][:, cs], gt[:, cs]), (gtb,), (og[1],))
                elif DBG == 2:
                    dve(lambda e: e.tensor_copy(og[0][:, cs], bon[:, cs]), (bonb,), (og[1],))
                elif DBG in (6, 7, 8, 9):
                    srcd = {6: sh[1], 7: tl["a_"], 8: tl["a_o"], 9: sh[2]}[DBG]
                    dve(lambda e, srcd=srcd: e.tensor_copy(og[0][:, cs], srcd[0][:, cs]), (srcd[1],), (og[1],))
                elif DBG == 20:
                    srcd = [tl["G"], tl["E3"], tl["bt"], tl["logw"]][c]
                    dve(lambda e, srcd=srcd: e.tensor_copy(og[0][:, cs], srcd[0][:, cs]), (srcd[1],), (og[1],))
                elif DBG == 21:
                    srcd = [tl["kap"], tl["bb"], tl["kd"], tl["E1"]][c]
                    dve(lambda e, srcd=srcd: e.tensor_copy(og[0][:, cs], srcd[0][:, cs]), (srcd[1],), (og[1],))
                elif DBG == 3:
                    dve(lambda e, o1=o1: e.tensor_copy(og[0][:, cs], o1[:]), (o1b,), (og[1],))
                elif DBG == 4:
                    dve(lambda e: e.tensor_copy(og[0][:, cs], yacc[:, gc, :]), (yacc_b[gc],), (og[1],))
                elif DBG == 5:
                    dve(lambda e: e.tensor_copy(og[0][:, cs], ybk[:, 0:128]), (ybkb,), (og[1],))
                else:
                    dve(lambda e, o1=o1: e.tensor_tensor(o1[:], o1[:], bon[:, cs], ALU.add), (o1b, bonb), (o1b,))
                    dve(lambda e, o1=o1: e.tensor_tensor(og[0][:, cs], o1[:], gt[:, cs], ALU.mult), (o1b, gtb), (og[1],))
            return newS

        for d in range(2):
            final = d == 1
            Scur = []
            for hh in range(2):
                S0, S0b = Srot[hh].next()
                P.op("pool", lambda e, S0=S0: e.memset(S0[:], 0.0), (), (S0b,))
                Scur.append((S0, S0b))
            segs = range(NSEG) if d == 0 else range(NSEG - 1, -1, -1)
            for seg in segs:
                prep(seg, d, final)
                og = ostage.next() if final else None
                chs = range(NCH) if d == 0 else range(NCH - 1, -1, -1)
                st = pre_segment(d)
                for c in chs:
                    Scur = chunk(seg, c, d, final, Scur, og, st)
                if final:
                    P.dma("sp", oa_out_d[:, seg * L:(seg + 1) * L], og[0][:], (og[1],), (), is_output=True)
        P.barrier()


def build_k2r(S):
    nc = bass.Bass("TRN2", target_bir_lowering=False)
    R = rwkv_dram_inputs(nc, S)
    idf_d = nc.dram_tensor("ident_f", [128, 128], F32, kind="ExternalInput").ap()
    oa_d = nc.dram_tensor("oaT", [128, S], F32, kind="ExternalOutput").ap()
    with ExitStack() as es:
        P = Prog(nc, es)
        ident_f = P.sb("sb_ident_f", [128, 128], F32, es)
        identf_b = Buf("identf")
        P.dma("sp", ident_f[:], idf_d, (), (identf_b,))
        phase_rwkv(P, ident_f, identf_b, R, oa_d, S, "r")
        P.finish()
        P.emit()
    return nc


def consts_common():
    return {
        "ident_bf": np.eye(128, dtype=np.float32).astype(ml_dtypes.bfloat16),
        "ident_f": np.eye(128, dtype=np.float32),
    }


def build_k2(S):
    nc = bass.Bass("TRN2", target_bir_lowering=False)
    R = rwkv_dram_inputs(nc, S)
    A = attn_dram_inputs(nc, S)
    idf_d = nc.dram_tensor("ident_f", [128, 128], F32, kind="ExternalInput").ap()
    oa_d = nc.dram_tensor("oaT", [128, S], F32, kind="ExternalOutput").ap()
    ob_d = nc.dram_tensor("obT", [128, S], F32, kind="ExternalOutput").ap()
    with ExitStack() as es:
        P = Prog(nc, es)
        ident_f = P.sb("sb_ident_f", [128, 128], F32, es)
        identf_b = Buf("identf")
        P.dma("sp", ident_f[:], idf_d, (), (identf_b,))
        phase_rwkv(P, ident_f, identf_b, R, oa_d, S, "r")
        phase_attn(P, ident_f, identf_b, A, ob_d, S, "t")
        P.finish()
        P.emit()
    return nc


def kernel(**inputs):
    f = lambda a: np.ascontiguousarray(np.asarray(a, dtype=np.float32))
    inp = {k: np.asarray(v) for k, v in inputs.items()}
    x = f(inp["x"])
    NT = SEQ // 4
    cores = list(range(NCORES))
    cc = consts_common()
    rc = rwkv_consts()
    ac = [attn_consts(g, SEQ) for g in range(4)]
    ident_f = np.eye(128, dtype=np.float32)

    def k1_weights(l):
        return dict(g1=f(inp["norm_ffn1"][l]), w1=f(inp["ffn1_in"][l]), w2=f(inp["ffn1_out"][l]),
                    gm=f(inp["norm_mix"][l]), win=f(inp["w_in"][l]))

    xs = [f(x[c // 4, (c % 4) * NT:(c % 4 + 1) * NT]) for c in cores]
    nc1 = build_k1(NT)
    w = k1_weights(0)
    res = run_bass_kernel_spmd(nc1, [dict(x=xs[c], **w, **cc) for c in cores], core_ids=cores)
    x1 = [res.results[c]["x1"] for c in cores]
    pt = [res.results[c]["pt"] for c in cores]
    nc2 = build_k2(SEQ)
    nc3n = build_k3(NT, True)
    nc3l = build_k3(NT, False)
    for l in range(DEPTH):
        lam_init = 0.8 - 0.6 * math.exp(-0.3 * l)
        li = np.tile(np.array([[lam_init, 1.0 - lam_init]], np.float32), (128, 1))
        maps = []
        mu = inp["rwkv_mu"][l]
        for c in cores:
            b, g = c // 4, c % 4
            cols = slice(g * 128, (g + 1) * 128)
            PTb = np.concatenate([pt[b * 4 + j][0:C_MIX] for j in range(4)], axis=1)
            rkv = np.stack([PTb[0:512][cols], PTb[512:1024][cols], PTb[1024:1536][cols]])
            lor = PTb[1536:1920].reshape(3, 128, SEQ)
            mu6 = np.stack([mu[0:512][cols], mu[512:1024][cols], mu[1024:1536][cols],
                            mu[1536:1664], mu[1664:1792], mu[1792:1920]])
            pa = PTb[C_RWKV:]
            m = dict(
                rkv=f(rkv), lor=f(lor), mu6=f(mu6),
                w0=f(inp["decay_w0"][l][:, cols]), w2=f(inp["decay_w2"][l][:, :, cols].reshape(128, 128)),
                a0=f(inp["iclr_a0"][l][:, cols]), a2=f(inp["iclr_a2"][l][:, :, cols].reshape(128, 128)),
                g2=f(inp["gate_g2"][l][:, cols]),
                vec5=f(np.stack([inp["k_k"][l][cols], inp["k_a"][l][cols], inp["r_k"][l].reshape(512)[cols],
                                 inp["ln_x_g"][l][cols], inp["ln_x_b"][l][cols]])),
                qT=f(pa[0:512][cols].reshape(2, 64, SEQ)), kT=f(pa[512:1024][cols].reshape(2, 64, SEQ)),
                vT=f(pa[1024:1536][cols]),
                q_gain=f(inp["q_gain"][l]), k_gain=f(inp["k_gain"][l]), lamv=f(inp["diff_lambda"][l]),
                subg=f(inp["subln_g"][l]), laminit=li, ident_f=ident_f, **rc, **ac[g])
            maps.append(m)
        res = run_bass_kernel_spmd(nc2, maps, core_ids=cores)
        oaT = [res.results[c]["oaT"] for c in cores]
        obT = [res.results[c]["obT"] for c in cores]
        maps = []
        last = l == DEPTH - 1
        for c in cores:
            b, j = c // 4, c % 4
            ts = slice(j * NT, (j + 1) * NT)
            oa = np.concatenate([oaT[b * 4 + g][:, ts] for g in range(4)], axis=0)
            ob = np.concatenate([obT[b * 4 + g][:, ts] for g in range(4)], axis=0)
            m = dict(x=f(x1[c]), sg=f(pt[c][C_MIX:]), oa=f(oa), ob=f(ob),
                     wb=f(inp["w_branch"][l]), wout=f(inp["w_out"][l]), g2n=f(inp["norm_ffn2"][l]),
                     w1b=f(inp["ffn2_in"][l]), w2b=f(inp["ffn2_out"][l]), **cc)
            if not last:
                m.update(k1_weights(l + 1))
            maps.append(m)
        res = run_bass_kernel_spmd(nc3l if last else nc3n, maps, core_ids=cores)
        x1 = [res.results[c]["x1"] for c in cores]
        if not last:
            pt = [res.results[c]["pt"] for c in cores]
    out = np.zeros((BATCH, SEQ, D), np.float32)
    for c in cores:
        out[c // 4, (c % 4) * NT:(c % 4 + 1) * NT] = x1[c]
    return out
```
